# Optimizing a Trainium2 kernel written in Bass

```python
import jax, jax.numpy as jnp
from jax import lax
import numpy as np

D_MODEL = 1024
BATCH = 32
SEQ = 2048
DEPTH = 1

EPS = 1e-6
LN_EPS = 1e-5
N_Q_HEADS = 8
N_KV_HEADS = 2
HEAD_DIM = 64
Q_PER_KV = N_Q_HEADS // N_KV_HEADS
ATTN_WIDTH = N_Q_HEADS * HEAD_DIM
KV_WIDTH = N_KV_HEADS * HEAD_DIM
WINDOW = 128
BLOCK = 128
GMLP_GROUPS = 4
GMLP_GROUP_WIDTH = 128
GMLP_WIDTH = GMLP_GROUPS * GMLP_GROUP_WIDTH
CHUNK = 128
D_FF = -(-8 * D_MODEL // (3 * 256)) * 256
IN_WIDTH = ATTN_WIDTH + 2 * KV_WIDTH + 2 * GMLP_WIDTH + 2 * D_MODEL

kernel_name = "hybrid_swa_sink_gmlp_gated_block"


def rms_norm(x, g):
    xf = x.astype(jnp.float32)
    y = xf * lax.rsqrt(jnp.mean(xf * xf, axis=-1, keepdims=True) + EPS)
    return (y * g.astype(jnp.float32)).astype(x.dtype)


def layer_norm(x, g, b):
    xf = x.astype(jnp.float32)
    mu = jnp.mean(xf, axis=-1, keepdims=True)
    var = jnp.mean(jnp.square(xf - mu), axis=-1, keepdims=True)
    y = (xf - mu) * lax.rsqrt(var + LN_EPS)
    return (y * g.astype(jnp.float32) + b.astype(jnp.float32)).astype(x.dtype)


def alibi_slopes(n_heads):
    return 2.0 ** (-8.0 * jnp.arange(1, n_heads + 1, dtype=jnp.float32) / n_heads)


def banded_sink_attention(q, k, v, sinks):
    B, S = q.shape[0], q.shape[1]
    nb = S // BLOCK
    qb = q.reshape(B, nb, BLOCK, N_KV_HEADS, Q_PER_KV, HEAD_DIM)
    pad = ((0, 0), (BLOCK, 0), (0, 0), (0, 0))
    kp = jnp.pad(k, pad).reshape(B, nb + 1, BLOCK, N_KV_HEADS, HEAD_DIM)
    vp = jnp.pad(v, pad).reshape(B, nb + 1, BLOCK, N_KV_HEADS, HEAD_DIM)
    kb = jnp.concatenate([kp[:, :-1], kp[:, 1:]], axis=2)
    vb = jnp.concatenate([vp[:, :-1], vp[:, 1:]], axis=2)
    scale = HEAD_DIM ** -0.5
    s = jnp.einsum('bnqhgd,bnkhd->bhgnqk', qb, kb).astype(jnp.float32) * scale
    a = jnp.arange(BLOCK)[:, None]
    j = jnp.arange(2 * BLOCK)[None, :]
    rel = BLOCK + a - j
    blk = jnp.arange(nb)[:, None, None]
    s_abs = (blk - 1) * BLOCK + j[None]
    valid = (rel[None] >= 0) & (rel[None] < WINDOW) & (s_abs >= 0)
    slopes = alibi_slopes(N_Q_HEADS).reshape(N_KV_HEADS, Q_PER_KV)
    alibi = -slopes[:, :, None, None, None] * rel.astype(jnp.float32)[None, None, None]
    logits = jnp.where(valid, s + alibi, -1e30)
    sink = sinks.astype(jnp.float32).reshape(N_KV_HEADS, Q_PER_KV)[:, :, None, None]
    m = jnp.maximum(jnp.max(logits, axis=-1), sink)
    p = jnp.exp(logits - m[..., None])
    denom = jnp.sum(p, axis=-1) + jnp.exp(sink - m)
    probs = (p / denom[..., None]).astype(v.dtype)
    o = jnp.einsum('bhgnqk,bnkhd->bnqhgd', probs, vb)
    return o.reshape(B, S, ATTN_WIDTH)


def chunked_spatial_gating(z, ln_g, ln_b, w_s, b_s):
    B, S = z.shape[0], z.shape[1]
    u, v = jnp.split(z, 2, axis=-1)
    v = layer_norm(v, ln_g, ln_b)
    nc = S // CHUNK
    vc = v.reshape(B, nc, CHUNK, GMLP_GROUPS, GMLP_GROUP_WIDTH)
    causal = jnp.tril(jnp.ones((CHUNK, CHUNK), dtype=w_s.dtype))
    w = w_s * causal[None]
    f = jnp.einsum('gts,bnsgc->bntgc', w, vc) + b_s.T[:, :, None]
    return u * f.reshape(B, S, GMLP_WIDTH)


def mixer_block(xn, w_in, attn_sinks, gmlp_ln_g, gmlp_ln_b, gmlp_w_s, gmlp_b_s,
                w_attn_branch, w_gmlp_branch, w_out):
    B, S = xn.shape[0], xn.shape[1]
    proj = jnp.einsum('bsd,de->bse', xn, w_in)
    splits = np.cumsum([ATTN_WIDTH, KV_WIDTH, KV_WIDTH, 2 * GMLP_WIDTH, D_MODEL])
    q, k, v, zg, g_a, g_b = jnp.split(proj, splits, axis=-1)
    q = q.reshape(B, S, N_Q_HEADS, HEAD_DIM)
    k = k.reshape(B, S, N_KV_HEADS, HEAD_DIM)
    v = v.reshape(B, S, N_KV_HEADS, HEAD_DIM)
    attn = banded_sink_attention(q, k, v, attn_sinks)
    gm = chunked_spatial_gating(jax.nn.gelu(zg, approximate=False),
                                gmlp_ln_g, gmlp_ln_b, gmlp_w_s, gmlp_b_s)
    br_a = jnp.einsum('bse,ed->bsd', attn, w_attn_branch)
    br_b = jnp.einsum('bse,ed->bsd', gm, w_gmlp_branch)
    merged = jax.nn.sigmoid(g_a) * br_a + jax.nn.sigmoid(g_b) * br_b
    return jnp.einsum('bsd,de->bse', merged, w_out)


def swiglu(x, w_gate, w_up, w_down):
    h = jax.nn.silu(jnp.einsum('bsd,df->bsf', x, w_gate)) * jnp.einsum('bsd,df->bsf', x, w_up)
    return jnp.einsum('bsf,fd->bsd', h, w_down)


def setup_inputs(seed: int = 0) -> dict:
    key = jax.random.key(seed)
    ks = jax.random.split(key, 20)
    f32 = jnp.float32

    def nrm(k, shape, scale):
        return jax.random.normal(k, shape, f32) * scale

    def gain(k, n):
        return 1.0 + 0.05 * jax.random.normal(k, (DEPTH, n), f32)

    return {
        "x": jax.random.normal(ks[0], (BATCH, SEQ, D_MODEL), f32),
        "norm_mix_pre": gain(ks[1], D_MODEL),
        "w_in": nrm(ks[2], (DEPTH, D_MODEL, IN_WIDTH), D_MODEL ** -0.5),
        "attn_sinks": nrm(ks[3], (DEPTH, N_Q_HEADS), 0.5),
        "gmlp_ln_g": gain(ks[4], GMLP_WIDTH),
        "gmlp_ln_b": nrm(ks[5], (DEPTH, GMLP_WIDTH), 0.02),
        "gmlp_w_s": nrm(ks[6], (DEPTH, GMLP_GROUPS, CHUNK, CHUNK), CHUNK ** -0.5),
        "gmlp_b_s": 1.0 + 0.1 * jax.random.normal(ks[7], (DEPTH, GMLP_GROUPS, CHUNK), f32),
        "w_attn_branch": nrm(ks[8], (DEPTH, ATTN_WIDTH, D_MODEL), ATTN_WIDTH ** -0.5),
        "w_gmlp_branch": nrm(ks[9], (DEPTH, GMLP_WIDTH, D_MODEL), GMLP_WIDTH ** -0.5),
        "w_out": nrm(ks[10], (DEPTH, D_MODEL, D_MODEL), D_MODEL ** -0.5),
        "norm_mix_post": gain(ks[11], D_MODEL),
        "norm_ffn_pre": gain(ks[12], D_MODEL),
        "w_ffn_gate": nrm(ks[13], (DEPTH, D_MODEL, D_FF), D_MODEL ** -0.5),
        "w_ffn_up": nrm(ks[14], (DEPTH, D_MODEL, D_FF), D_MODEL ** -0.5),
        "w_ffn_down": nrm(ks[15], (DEPTH, D_FF, D_MODEL), D_FF ** -0.5),
        "norm_ffn_post": gain(ks[16], D_MODEL),
    }


def reference(x, norm_mix_pre, w_in, attn_sinks, gmlp_ln_g, gmlp_ln_b, gmlp_w_s, gmlp_b_s,
              w_attn_branch, w_gmlp_branch, w_out, norm_mix_post, norm_ffn_pre,
              w_ffn_gate, w_ffn_up, w_ffn_down, norm_ffn_post):
    h = x
    for l in range(DEPTH):
        xn = rms_norm(h, norm_mix_pre[l])
        mix = mixer_block(xn, w_in[l], attn_sinks[l], gmlp_ln_g[l], gmlp_ln_b[l],
                          gmlp_w_s[l], gmlp_b_s[l], w_attn_branch[l], w_gmlp_branch[l], w_out[l])
        h = h + rms_norm(mix, norm_mix_post[l])
        hn = rms_norm(h, norm_ffn_pre[l])
        ff = swiglu(hn, w_ffn_gate[l], w_ffn_up[l], w_ffn_down[l])
        h = h + rms_norm(ff, norm_ffn_post[l])
    return h
```

```python
import numpy as np
from contextlib import ExitStack
import concourse.bass as bass
import concourse.mybir as mybir
from concourse.bass_utils import run_bass_kernel_spmd

F32 = mybir.dt.float32
BF16 = mybir.dt.bfloat16
AF = mybir.ActivationFunctionType
ALU = mybir.AluOpType

D = 1024
DFF = 2816
NFC = 22
TT = 512
NB = 4
EPS = 1e-6
LN_EPS = 1e-5
NCH = 31
CHE = 4096
NSLOT = 4
NTMP = 10
N_CORES = 8
CAST_DMA = True
NXB = 3

ENGS = ("pe", "act", "dve", "pool", "sp")


class Op:
    __slots__ = ("eng", "fn", "deps", "marked", "mark", "dma_key", "dma_val", "waits")

    def __init__(self, eng, fn):
        self.eng = eng
        self.fn = fn
        self.deps = []
        self.marked = False
        self.mark = 0
        self.dma_key = None
        self.dma_val = 0
        self.waits = []


class Sched:
    def __init__(self):
        self.ops = {e: [] for e in ENGS}
        self.state = {}
        self.dma_count = {}
        self.bulk = {}

    def add(self, eng, fn, reads=(), writes=(), dma=None, bulk=False):
        op = Op(eng, fn)
        deps = {}
        tmpkey = False

        def dep(o, kind):
            nonlocal tmpkey
            if o is op:
                return
            if o.dma_key is None and o.eng == eng and kind != "raw" and not (kind == "waw" and tmpkey):
                return
            k = id(o)
            if k not in deps:
                deps[k] = o

        for k in reads:
            st = self.state.get(k)
            if st is None:
                continue
            if st[0] is not None:
                dep(st[0], "raw")
            if isinstance(k, tuple) and k[0] == "ps":
                for r in st[1].values():
                    if r.eng != eng:
                        dep(r, "rar")
        for k in writes:
            st = self.state.get(k)
            if st is None:
                continue
            tmpkey = isinstance(k, tuple) and k[0] == "tmp" and eng != "pe"
            if st[0] is not None:
                dep(st[0], "waw")
            tmpkey = False
            for r in st[1].values():
                dep(r, "war")
        op.deps = list(deps.values())
        for o in op.deps:
            o.marked = True
        if dma is not None:
            op.dma_key = dma
            c = self.dma_count.get(dma, 0) + 1
            self.dma_count[dma] = c
            op.dma_val = 16 * c
            if bulk:
                self.bulk.setdefault(dma, []).append(op)
        for k in writes:
            self.state[k] = [op, {}]
        for k in reads:
            st = self.state.get(k)
            if st is None:
                st = [None, {}]
                self.state[k] = st
            st[1][eng] = op
        self.ops[eng].append(op)
        return op

    def finalize(self):
        for key, ops in self.bulk.items():
            tot = 16 * self.dma_count[key]
            for o in ops:
                o.dma_val = tot
        for e in ENGS:
            n = 0
            for op in self.ops[e]:
                if op.dma_key is None and op.marked:
                    n += 1
                    op.mark = n
        for e in ENGS:
            waited = {}
            for op in self.ops[e]:
                w = {}
                for d in op.deps:
                    if d.dma_key is not None:
                        key, val = ("dma", d.dma_key), d.dma_val
                    else:
                        key, val = ("eng", d.eng), d.mark
                    if val > w.get(key, 0):
                        w[key] = val
                op.waits = []
                for key, val in w.items():
                    if waited.get(key, 0) >= val:
                        continue
                    waited[key] = val
                    op.waits.append((key, val))


def AP(t, F, p0, npart, off, dims):
    return bass.AP(t, p0 * F + off, [[F, npart]] + [[s, c] for (s, c) in dims])


def _kc_layout(w):
    C = w.shape[1]
    return np.ascontiguousarray(w.reshape(8, 128, C).transpose(1, 0, 2)).reshape(128, 8 * C)


def _pad(a):
    out = np.zeros((128, CHE), np.float32)
    out[: a.shape[0], : a.shape[1]] = a
    return out


def _q_perm():
    cols = []
    for i in range(4):
        g, j = i // 2, i % 2
        lo, hi = 4 * g + j, 4 * g + 2 + j
        cols += list(range(64 * lo, 64 * lo + 64)) + list(range(64 * hi, 64 * hi + 64))
    return np.array(cols)


C_Q, C_KV, C_U, C_VG = 0, 1, 2, 3
C_M0 = 4
C_O0 = 12
C_G0 = 14
C_D0 = 25


def build_wsrc(w_in, w_a, w_g, w_out, w_gate, w_up, w_down):
    chunks = []
    wq = w_in[:, 0:512][:, _q_perm()]
    chunks.append(_pad(_kc_layout(wq)))
    wk = w_in[:, 512:640]
    wv = w_in[:, 640:768]
    kv = np.concatenate([wk[:, 0:64], wk[:, 0:64], wk[:, 64:128], wk[:, 64:128], wv], axis=1)
    chunks.append(_pad(_kc_layout(kv)))
    chunks.append(_pad(_kc_layout(w_in[:, 768:1280])))
    chunks.append(_pad(_kc_layout(w_in[:, 1280:1792])))
    for c in range(8):
        ga = w_in[:, 1792 + 128 * c: 1792 + 128 * (c + 1)]
        gb = w_in[:, 2816 + 128 * c: 2816 + 128 * (c + 1)]
        gab = _kc_layout(np.concatenate([ga, gb], axis=1))
        wa = w_a[:, 128 * c:128 * (c + 1)].reshape(4, 128, 128).transpose(1, 0, 2).reshape(128, 512)
        wg = w_g[:, 128 * c:128 * (c + 1)].reshape(4, 128, 128).transpose(1, 0, 2).reshape(128, 512)
        chunks.append(_pad(np.concatenate([gab, wa, wg], axis=1)))
    for hf in range(2):
        chunks.append(_pad(_kc_layout(w_out[:, 512 * hf:512 * (hf + 1)])))
    for f in range(11):
        gu = np.concatenate([w_gate[:, 256 * f:256 * (f + 1)], w_up[:, 256 * f:256 * (f + 1)]], axis=1)
        chunks.append(_pad(_kc_layout(gu)))
    for hf in range(2):
        for (f0, f1) in ((0, 8), (8, 16), (16, 22)):
            blk = w_down[128 * f0:128 * f1, 512 * hf:512 * (hf + 1)]
            n = f1 - f0
            chunks.append(_pad(blk.reshape(n, 128, 512).transpose(1, 0, 2).reshape(128, n * 512)))
    assert len(chunks) == NCH
    return np.stack(chunks, axis=0)


def build_consts(norm_mix_pre, norm_mix_post, norm_ffn_pre, norm_ffn_post, attn_sinks,
                 ln_g, ln_b, w_s, b_s):
    c = {}
    c["ident"] = np.eye(128, dtype=np.float32)
    tbl = np.zeros((128, 2, 2, 2, 2, 128), np.float32)
    j = np.arange(128)[:, None]
    a = np.arange(128)[None, :]
    for g in range(2):
        for half in range(2):
            for hh2 in range(2):
                h = 4 * g + 2 * half + hh2
                slope = 2.0 ** (-(h + 1))
                rel0 = 128 + a - j
                rel1 = a - j
                tbl[:, g, half, 0, hh2, :] = np.where(a < j, -slope * rel0 * 8.0, -30000.0)
                tbl[:, g, half, 1, hh2, :] = np.where(a >= j, -slope * rel1 * 8.0, -30000.0)
    c["tbl"] = tbl.reshape(128, 2048)
    c["g1"] = np.ascontiguousarray(norm_mix_pre.reshape(8, 128).T)
    c["g3"] = np.ascontiguousarray(norm_ffn_pre.reshape(8, 128).T)
    c["g2bc"] = np.ascontiguousarray(np.broadcast_to(norm_mix_post.reshape(1, D), (128, D)))
    c["g4bc"] = np.ascontiguousarray(np.broadcast_to(norm_ffn_post.reshape(1, D), (128, D)))
    c["lngpp"] = np.ascontiguousarray(ln_g.reshape(4, 128).T)
    c["lnbpp"] = np.ascontiguousarray(ln_b.reshape(4, 128).T)
    c["bsbc"] = np.ascontiguousarray(np.broadcast_to(b_s.reshape(1, 512), (128, 512)))
    c["sinkbc"] = np.ascontiguousarray(np.broadcast_to(attn_sinks.reshape(1, 8), (128, 8)))
    c["wsT"] = np.ascontiguousarray(w_s.transpose(2, 0, 1)).reshape(128, 512)
    s_ = np.arange(128)[:, None]
    t_ = np.arange(128)[None, :]
    cm = (t_ >= s_).astype(np.float32)
    c["cmask"] = np.ascontiguousarray(np.broadcast_to(cm[:, None, :], (128, 4, 128))).reshape(128, 512)
    return {k: np.ascontiguousarray(v, dtype=np.float32) for k, v in c.items()}


CONST_SHAPES = {"ident": 128, "tbl": 2048, "g1": 8, "g3": 8, "g2bc": 1024, "g4bc": 1024,
                "lngpp": 4, "lnbpp": 4, "bsbc": 512, "sinkbc": 8, "wsT": 512, "cmask": 512}


class Builder:
    def __init__(self, n_tiles, tiles_per_seq):
        self.n_tiles = n_tiles
        self.tps = tiles_per_seq
        self.S = Sched()
        self.nc = bass.Bass("TRN2", target_bir_lowering=False)
        self.ntok = n_tiles * TT

    def alloc(self, es):
        nc = self.nc
        self.x_d = nc.dram_tensor("x", [self.ntok, D], F32, kind="ExternalInput")
        self.y_d = nc.dram_tensor("y", [self.ntok, D], F32, kind="ExternalOutput")
        self.wsrc_d = nc.dram_tensor("wsrc", [NCH * 128, CHE], F32, kind="ExternalInput")
        self.wbf_d = nc.dram_tensor("wbf", [NCH * 128, CHE], BF16, kind="Internal")
        self.c_d = {k: nc.dram_tensor("c_" + k, [128, n], F32, kind="ExternalInput")
                    for k, n in CONST_SHAPES.items()}

        def sb(name, n, dt):
            return es.enter_context(nc.sbuf_tensor(name, [128, n], dt))

        self.xb = [sb("xb%d" % i, 4096, F32) for i in range(NXB)]
        self.xn = sb("xn", 4096, BF16)
        self.xnT = sb("xnT", 4096, BF16)
        self.hnT = sb("hnT", 4096, BF16)
        self.Q2 = sb("Q2", 2048, BF16)
        self.kdup = sb("kdup", 1280, BF16)
        self.vtok = sb("vtok", 650, BF16)
        self.uT = sb("uT", 2048, BF16)
        self.vln = sb("vln", 2048, BF16)
        self.PT = sb("PT", 2048, BF16)
        self.attnT = sb("attnT", 2048, BF16)
        self.atok = sb("atok", 2048, BF16)
        self.gmT = sb("gmT", 2048, BF16)
        self.mergedT = sb("mergedT", 4096, BF16)
        self.hT = sb("hT", NFC * 512, BF16)
        self.ring = [sb("ring%d" % i, CHE, BF16) for i in range(NSLOT)]
        self.stg = [sb("stg%d" % i, 1024, F32) for i in range(2)] if not CAST_DMA else []
        self.stg_i = 0
        self.tmp = [sb("tmp%d" % i, 512, F32) for i in range(NTMP)]
        self.tmp_i = 0
        self.tbl = sb("tbl", 2048, BF16)
        self.g2bc = sb("g2bc", 1024, F32)
        self.g4bc = sb("g4bc", 1024, F32)
        self.B2 = sb("B2", 512, F32)
        self.esink = sb("esink", 8, F32)
        self.wsT = sb("wsT", 512, BF16)
        self.ident = sb("ident", 128, BF16)
        self.ones = sb("ones", 128, BF16)
        self.g1 = sb("g1", 8, F32)
        self.g3 = sb("g3", 8, F32)
        self.lngpp = sb("lngpp", 4, F32)
        self.lnbpp = sb("lnbpp", 4, F32)
        self.neghalf = sb("neghalf", 8, F32)
        self.st = sb("stats", 64, F32)
        self.bnst = sb("bnst", 4 * 6, F32)
        self.ps = [es.enter_context(nc.psum_tensor("ps%d" % i, [128, 512], F32)) for i in range(8)]
        self.psb = [p.bitcast(BF16) for p in self.ps]
        self.ps_i = 0
        self.held = set()
        self.soft = set()
        self.soft_n = 0
        self.sem_eng = {e: es.enter_context(nc.semaphore("s_" + e)) for e in ("pe", "act", "dve", "pool")}
        self.sem_dma = {}
        self.es = es

    def nbank(self):
        pick = None
        for k in range(8):
            i = (self.ps_i + k) % 8
            if i in self.held:
                continue
            if self.soft_n > 0 and i in self.soft:
                continue
            pick = i
            break
        if pick is None:
            for k in range(8):
                i = (self.ps_i + k) % 8
                if i not in self.held:
                    pick = i
                    break
        self.ps_i = (pick + 1) % 8
        if self.soft_n > 0:
            self.soft_n -= 1
        return pick

    def ntmp(self):
        i = self.tmp_i
        self.tmp_i = (i + 1) % NTMP
        return i

    def T(self, i, n=512, p0=0, npart=128, off=0):
        return AP(self.tmp[i], 512, p0, npart, off, [(1, n)])

    def P(self, bk, n=512, off=0, p0=0, npart=128):
        return AP(self.ps[bk], 512, p0, npart, off, [(1, n)])

    def op(self, eng, fn, reads=(), writes=(), dma=None, bulk=False):
        return self.S.add(eng, fn, reads, writes, dma, bulk)

    def mm(self, out, lhsT, rhs, start, stop, reads, writes):
        self.op("pe", lambda e: e.matmul(out, lhsT, rhs, start=start, stop=stop), reads, writes)

    def dma(self, out, in_, reads, writes, key, bulk=False, q="sp", max_last=None):
        if max_last is None:
            fn = lambda e: e.dma_start(out=out, in_=in_)
        else:
            fn = lambda e: e.dma_start(out=out, in_=in_, max_dma_last_dim=max_last)
        self.op(q, fn, reads, writes, dma=key, bulk=bulk)

    def load_consts(self, part):
        ckey = ("C%d" % part,)
        def cdram(name, off=0, n=None):
            F = CONST_SHAPES[name]
            return AP(self.c_d[name], F, 0, 128, off, [(1, n or F)])

        def direct(name, t):
            n = CONST_SHAPES[name]
            self.dma(AP(t, n, 0, 128, 0, [(1, n)]), cdram(name), [], [("c", name)], ckey, bulk=True)

        def via_tmp(name, off, n):
            ti = self.ntmp()
            self.dma(self.T(ti, n), cdram(name, off, n), [], [("tmp", ti)], ckey, bulk=True)
            return ti

        if part == 0:
            for name, t in (("g1", self.g1), ("g3", self.g3), ("lngpp", self.lngpp), ("lnbpp", self.lnbpp)):
                direct(name, t)
        else:
            for name, t in (("g2bc", self.g2bc), ("g4bc", self.g4bc)):
                direct(name, t)
        if part == 0:
            self.op("dve", lambda e, o=AP(self.ones, 128, 0, 128, 0, [(1, 128)]): e.memset(o, 1.0), [], [("c", "ones")])
            self.op("dve", lambda e, o=AP(self.neghalf, 8, 0, 128, 0, [(1, 8)]): e.memset(o, -0.5), [],
                    [("c", "neghalf")])
            ti = via_tmp("ident", 0, 128)
            self.op("dve", lambda e, o=AP(self.ident, 128, 0, 128, 0, [(1, 128)]), s=self.T(ti, 128):
                    e.tensor_copy(out=o, in_=s), [("tmp", ti)], [("c", "ident")])
        else:
            for q4 in range(4):
                ti = via_tmp("tbl", 512 * q4, 512)
                self.op("dve", lambda e, o=AP(self.tbl, 2048, 0, 128, 512 * q4, [(1, 512)]), s=self.T(ti):
                        e.tensor_copy(out=o, in_=s), [("tmp", ti)], [("c", "tbl")])
            ti = via_tmp("sinkbc", 0, 8)
            self.op("act", lambda e, o=AP(self.esink, 8, 0, 128, 0, [(1, 8)]), s=self.T(ti, 8):
                    e.activation(out=o, in_=s, func=AF.Exp), [("tmp", ti)], [("c", "esink")])
            self.op("dve", lambda e, o=AP(self.vtok, 650, 0, 128, 0, [(1, 650)]): e.memset(o, 1.0), [],
                    [("v", i) for i in range(5)])
            t3 = via_tmp("wsT", 0, 512)
            t4 = via_tmp("cmask", 0, 512)
            self.op("dve", lambda e, o=AP(self.wsT, 512, 0, 128, 0, [(1, 512)]), a=self.T(t3), b=self.T(t4):
                    e.tensor_tensor(out=o, in0=a, in1=b, op=ALU.mult), [("tmp", t3), ("tmp", t4)], [("c", "wsT")])
            t5 = via_tmp("bsbc", 0, 512)
            bk = self.nbank()
            self.mm(self.P(bk), AP(self.ones, 128, 0, 128, 0, [(1, 128)]), AP(self.wsT, 512, 0, 128, 0, [(1, 512)]),
                    True, True, [("c", "ones"), ("c", "wsT")], [("ps", bk)])
            for g in range(4):
                self.op("dve", lambda e, o=AP(self.B2, 512, 0, 128, 128 * g, [(1, 128)]), i=self.P(bk, 128, 128 * g),
                        s=AP(self.lnbpp, 4, 0, 128, g, [(1, 1)]), b=self.T(t5, 128, off=128 * g):
                        e.scalar_tensor_tensor(out=o, in0=i, scalar=s, in1=b, op0=ALU.mult, op1=ALU.add),
                        [("ps", bk), ("c", "lnbpp"), ("tmp", t5)], [("c", "B2")])

    def stream_init(self):
        order = [C_Q, C_KV, C_U, C_VG]
        for t in range(self.n_tiles):
            order += [C_M0 + i for i in range(8)]
            order += [C_O0, C_O0 + 1]
            if t + 1 < self.n_tiles:
                order += [C_Q, C_KV, C_U, C_VG]
            order += [C_G0 + i for i in range(11)]
            order += [C_D0 + i for i in range(6)]
            order += [C_D0 + i for i in range(6)]
        self.order = order
        self.cast_done = set()
        self.n_loaded = 0
        self.n_used = 0

    def record_load(self):
        n = self.n_loaded
        if n >= len(self.order):
            return
        self.n_loaded += 1
        c = self.order[n]
        s = n % NSLOT
        slot = self.ring[s]
        rkeys = [("ring", s, q4) for q4 in range(4)]
        if c not in self.cast_done:
            self.cast_done.add(c)
            if CAST_DMA:
                self.dma(AP(slot, CHE, 0, 128, 0, [(1, CHE)]), AP(self.wsrc_d, CHE, c * 128, 128, 0, [(1, CHE)]),
                         [], rkeys, ("WC", s), q="pool", max_last=8192)
            for q4 in (range(4) if not CAST_DMA else ()):
                si = self.stg_i
                self.stg_i = 1 - si
                stg = AP(self.stg[si], 1024, 0, 128, 0, [(1, 1024)])
                src = AP(self.wsrc_d, CHE, c * 128, 128, 1024 * q4, [(1, 1024)])
                self.dma(stg, src, [], [("stg", si)], ("ST", si))
                dst = AP(slot, CHE, 0, 128, 1024 * q4, [(1, 1024)])
                eng = ("dve", "act", "pool", "dve")[q4]
                if eng == "act":
                    fn = lambda e, o=dst, i=stg: e.activation(out=o, in_=i, func=AF.Copy)
                else:
                    fn = lambda e, o=dst, i=stg: e.tensor_copy(out=o, in_=i)
                self.op(eng, fn, [("stg", si)], [("ring", s, q4)])
            self.dma(AP(self.wbf_d, CHE, c * 128, 128, 0, [(1, CHE)]), AP(slot, CHE, 0, 128, 0, [(1, CHE)]),
                     rkeys, [("wbf", c)], ("WS", s))
        else:
            self.dma(AP(slot, CHE, 0, 128, 0, [(1, CHE)]), AP(self.wbf_d, CHE, c * 128, 128, 0, [(1, CHE)]),
                     [("wbf", c)], rkeys, ("W", s))

    def next_chunk(self, expect):
        n = self.n_used
        assert self.order[n] == expect, (n, self.order[n], expect)
        self.n_used += 1
        while self.n_loaded <= n:
            self.record_load()
        s = n % NSLOT
        return self.ring[s], [("ring", s, q4) for q4 in range(4)]

    def chunk_done(self):
        while self.n_loaded < min(self.n_used + NSLOT - 1, len(self.order)):
            self.record_load()

    def load_x(self, t):
        xb = self.xb[t % NXB]
        src = bass.AP(self.x_d, t * TT * D, [[D, 128], [128 * D, NB], [1, D]])
        dst = AP(xb, 4096, 0, 128, 0, [(D, NB), (1, D)])
        self.dma(dst, src, [], [("xb", t % NXB, b) for b in range(NB)], ("X", t % NXB))

    def rms_scale(self, xi, which, blocks=(0, 1, 2, 3)):
        xb = self.xb[xi]
        base = 8 * which
        nb_ = len(blocks)
        b0 = blocks[0]
        mkey = ("stm", which, b0)
        for b in blocks:
            src = AP(xb, 4096, 0, 128, D * b, [(1, D)])
            junk = AP(self.xn, 4096, 0, 128, D * b, [(1, D)])
            acc = AP(self.st, 64, 0, 128, base + b, [(1, 1)])
            self.op("act", lambda e, o=junk, i=src, a=acc: e.activation(out=o, in_=i, func=AF.Square, accum_out=a),
                    [("xb", xi, b)], [("xn", b), ("st", base + b)])
            yield
        ss = AP(self.st, 64, 0, 128, base + b0, [(1, nb_)])
        ms = AP(self.st, 64, 0, 128, base + 4 + b0, [(1, nb_)])
        self.op("dve", lambda e, o=ms, i=ss: e.tensor_scalar(out=o, in0=i, scalar1=1.0 / D, scalar2=EPS,
                                                             op0=ALU.mult, op1=ALU.add),
                [("st", base + b) for b in blocks], [mkey])
        nh = AP(self.neghalf, 8, 0, 128, 0, [(1, nb_)])
        self.op("pool", lambda e, o=ms, i=ms, h=nh: e.tensor_tensor(out=o, in0=i, in1=h, op=ALU.pow),
                [mkey, ("c", "neghalf")], [mkey])
        for b in blocks:
            src = AP(xb, 4096, 0, 128, D * b, [(1, D)])
            dst = AP(self.xn, 4096, 0, 128, D * b, [(1, D)])
            sc = AP(self.st, 64, 0, 128, base + 4 + b, [(1, 1)])
            eng = "dve" if b % 2 == 0 else "pool"
            self.op(eng, lambda e, o=dst, i=src, s=sc: e.tensor_scalar(out=o, in0=i, scalar1=s, scalar2=0.0,
                                                                       op0=ALU.mult, op1=ALU.add),
                    [("xb", xi, b), mkey], [("xn", b)])
            yield

    def rms_scale_all(self, xi, which, blocks=(0, 1, 2, 3)):
        for _ in self.rms_scale(xi, which, blocks):
            pass

    def transposes(self, dst_t, dkey, gain, gname):
        for kc in range(8):
            bk = self.nbank()
            for b in range(NB):
                o = AP(self.psb[bk], 1024, 0, 128, 128 * b, [(1, 128)])
                i = AP(self.xn, 4096, 0, 128, D * b + 128 * kc, [(1, 128)])
                idn = AP(self.ident, 128, 0, 128, 0, [(1, 128)])
                self.op("pe", lambda e, o=o, i=i, idn=idn: e.transpose(out=o, in_=i, identity=idn),
                        [("xn", b), ("c", "ident")], [("ps", bk)])
            src = AP(self.psb[bk], 1024, 0, 128, 0, [(1, 512)])
            dst = AP(dst_t, 4096, 0, 128, 512 * kc, [(1, 512)])
            gsc = AP(gain, 8, 0, 128, kc, [(1, 1)])
            self.op("dve", lambda e, o=dst, i=src, s=gsc: e.tensor_scalar(out=o, in0=i, scalar1=s, scalar2=None,
                                                                          op0=ALU.mult),
                    [("ps", bk), ("c", gname)], [(dkey, kc)])

    def featT_mm(self, slot, rkeys, col0, bk, act_t, akey, wstride, nk=8, woff=0):
        F = act_t.shape[1]
        for kc in range(nk):
            lhsT = AP(slot, CHE, 0, 128, woff + kc * wstride + col0, [(1, 128)])
            rhs = AP(act_t, F, 0, 128, 512 * kc, [(1, 512)])
            self.mm(self.P(bk), lhsT, rhs, kc == 0, kc == nk - 1, rkeys + [(akey, kc)], [("ps", bk)])

    def stage_front(self, t, mid=None):
        ts = t % self.tps
        if ts > 0:
            o = AP(self.kdup, 1280, 0, 128, 0, [(640, 2), (1, 128)])
            i = AP(self.kdup, 1280, 0, 128, 512, [(640, 2), (1, 128)])
            self.op("pool", lambda e, o=o, i=i: e.tensor_copy(out=o, in_=i), [("k", 4)], [("k", 0)])
            o = AP(self.vtok, 650, 0, 128, 0, [(1, 130)])
            i = AP(self.vtok, 650, 0, 128, 520, [(1, 130)])
            self.op("pool", lambda e, o=o, i=i: e.tensor_copy(out=o, in_=i), [("v", 4)], [("v", 0)])
        slot, rk = self.next_chunk(C_Q)
        for i in range(4):
            bk = self.nbank()
            self.featT_mm(slot, rk, 128 * i, bk, self.xnT, "xnT", 512)
            dst = AP(self.Q2, 2048, 0, 128, 512 * i, [(1, 512)])
            if i % 2 == 0:
                self.op("act", lambda e, o=dst, s=self.P(bk): e.activation(out=o, in_=s, func=AF.Copy),
                        [("ps", bk)], [("q", i)])
            else:
                self.op("dve", lambda e, o=dst, s=self.P(bk): e.tensor_copy(out=o, in_=s), [("ps", bk)],
                        [("q", i)])
        self.chunk_done()
        slot, rk = self.next_chunk(C_KV)
        for g in range(2):
            bk = self.nbank()
            self.featT_mm(slot, rk, 128 * g, bk, self.xnT, "xnT", 384)
            dst = AP(self.kdup, 1280, 0, 128, 640 * g + 128, [(1, 512)])
            self.op("dve", lambda e, o=dst, s=self.P(bk): e.tensor_copy(out=o, in_=s), [("ps", bk)],
                    [("k", 1), ("k", 2), ("k", 3), ("k", 4)])
        bk = self.nbank()
        for b in range(NB):
            for kc in range(8):
                lhsT = AP(self.xnT, 4096, 0, 128, 512 * kc + 128 * b, [(1, 128)])
                rhs = AP(slot, CHE, 0, 128, kc * 384 + 256, [(1, 128)])
                self.mm(self.P(bk, 128, 128 * b), lhsT, rhs, kc == 0, kc == 7, rk + [("xnT", kc)], [("ps", bk)])
        dst = AP(self.vtok, 650, 0, 128, 130, [(130, 4), (65, 2), (1, 64)])
        src = AP(self.ps[bk], 512, 0, 128, 0, [(128, 4), (64, 2), (1, 64)])
        self.op("act", lambda e, o=dst, s=src: e.activation(out=o, in_=s, func=AF.Copy), [("ps", bk)],
                [("v", 1), ("v", 2), ("v", 3), ("v", 4)])
        self.chunk_done()
        slot, rk = self.next_chunk(C_U)
        for i in range(4):
            bk = self.nbank()
            self.featT_mm(slot, rk, 128 * i, bk, self.xnT, "xnT", 512)
            dst = AP(self.uT, 2048, 0, 128, 512 * i, [(1, 512)])
            self.op("act", lambda e, o=dst, s=self.P(bk): e.activation(out=o, in_=s, func=AF.Gelu), [("ps", bk)],
                    [("u", i)])
        self.chunk_done()
        if mid is not None:
            mid()
        slot, rk = self.next_chunk(C_VG)
        for b in range(NB):
            bk = self.nbank()
            for kc in range(8):
                lhsT = AP(self.xnT, 4096, 0, 128, 512 * kc + 128 * b, [(1, 128)])
                rhs = AP(slot, CHE, 0, 128, kc * 512, [(1, 512)])
                self.mm(self.P(bk), lhsT, rhs, kc == 0, kc == 7, rk + [("xnT", kc)], [("ps", bk)])
            tv = self.ntmp()
            vg = self.T(tv)
            self.op("act", lambda e, o=vg, s=self.P(bk): e.activation(out=o, in_=s, func=AF.Gelu), [("ps", bk)],
                    [("tmp", tv)])
            bst = AP(self.bnst, 24, 0, 128, 6 * b, [(1, 6)])
            self.op("dve", lambda e, o=bst, i=vg: e.bn_stats(out=o, in_=i), [("tmp", tv)], [("bn", b)])
            mv = AP(self.st, 64, 0, 128, 32 + 2 * b, [(1, 2)])
            self.op("dve", lambda e, o=mv, i=bst: e.bn_aggr(out=o, in_=i), [("bn", b)], [("mv", b)])
            var = AP(self.st, 64, 0, 128, 32 + 2 * b + 1, [(1, 1)])
            mean = AP(self.st, 64, 0, 128, 32 + 2 * b, [(1, 1)])
            rs = AP(self.st, 64, 0, 128, 48 + b, [(1, 1)])
            self.op("dve", lambda e, o=rs, i=var: e.tensor_scalar(out=o, in0=i, scalar1=LN_EPS, scalar2=None,
                                                                  op0=ALU.add),
                    [("mv", b)], [("lnr", b)])
            nh = AP(self.neghalf, 8, 0, 128, 0, [(1, 1)])
            self.op("pool", lambda e, o=rs, i=rs, h=nh: e.tensor_tensor(out=o, in0=i, in1=h, op=ALU.pow),
                    [("lnr", b), ("c", "neghalf")], [("lnr", b)])
            dst = AP(self.vln, 2048, 0, 128, 512 * b, [(1, 512)])
            self.op("dve", lambda e, o=dst, i=vg, m=mean, r=rs: e.tensor_scalar(out=o, in0=i, scalar1=m, scalar2=r,
                                                                                 op0=ALU.subtract, op1=ALU.mult),
                    [("tmp", tv), ("mv", b), ("lnr", b)], [("vln", b)])
        self.chunk_done()

    def stage_attn(self, t):
        ts = t % self.tps
        units = [(b, g) for b in range(NB) for g in range(2)]
        pend = None
        idn = AP(self.ident, 128, 0, 128, 0, [(1, 128)])
        ones = AP(self.ones, 128, 0, 128, 0, [(1, 128)])

        def pv(u):
            b, g, parts, pti, ui = u
            bo = self.nbank()
            for hh in range(4):
                for idx, p in enumerate(parts):
                    lhsT = AP(self.PT, 2048, 0, 128, 1024 * pti + 512 * (hh // 2) + 256 * p + 128 * (hh % 2),
                              [(1, 128)])
                    rhs = AP(self.vtok, 650, 0, 128, 130 * (b + p) + 65 * g, [(1, 65)])
                    self.mm(self.P(bo, 65, 65 * hh), lhsT, rhs, idx == 0, idx == len(parts) - 1,
                            [("v", b + p), ("PT", pti)], [("ps", bo)])
            ro = 56 + 4 * (ui % 2)
            r = AP(self.st, 64, 0, 128, ro, [(1, 4)])
            den = AP(self.ps[bo], 512, 0, 128, 64, [(65, 4)])
            es_ = AP(self.esink, 8, 0, 128, 4 * g, [(1, 4)])
            self.op("dve", lambda e, o=r, a=den, c=es_: e.tensor_tensor(out=o, in0=a, in1=c, op=ALU.add),
                    [("ps", bo), ("c", "esink")], [("st", ro)])
            self.op("dve", lambda e, o=r: e.reciprocal(out=o, in_=o), [("st", ro)], [("st", ro)])
            for hh in range(4):
                dst = AP(self.atok, 2048, 0, 128, 512 * b + 64 * (4 * g + hh), [(1, 64)])
                rs = AP(self.st, 64, 0, 128, ro + hh, [(1, 1)])
                if hh % 2 == 0:
                    self.op("act", lambda e, o=dst, i=self.P(bo, 64, 65 * hh), s_=rs:
                            e.activation(out=o, in_=i, func=AF.Copy, scale=s_),
                            [("ps", bo), ("st", ro)], [("atok", b, 4 * g + hh)])
                else:
                    self.op("dve", lambda e, o=dst, i=self.P(bo, 64, 65 * hh), s_=rs:
                            e.tensor_scalar(out=o, in0=i, scalar1=s_, scalar2=None, op0=ALU.mult),
                            [("ps", bo), ("st", ro)], [("atok", b, 4 * g + hh)])
            return b if g == 1 else None

        def tr_block(b):
            bt = self.nbank()
            for c4 in range(4):
                o = AP(self.psb[bt], 1024, 0, 128, 128 * c4, [(1, 128)])
                i = AP(self.atok, 2048, 0, 128, 512 * b + 128 * c4, [(1, 128)])
                self.op("pe", lambda e, o=o, i=i: e.transpose(out=o, in_=i, identity=idn),
                        [("atok", b, 2 * c4), ("atok", b, 2 * c4 + 1), ("c", "ident")], [("ps", bt)])
            src = AP(self.psb[bt], 1024, 0, 128, 0, [(128, 4), (1, 128)])
            dst = AP(self.attnT, 2048, 0, 128, 128 * b, [(512, 4), (1, 128)])
            self.op("dve", lambda e, o=dst, s_=src: e.tensor_copy(out=o, in_=s_), [("ps", bt)],
                    [("attn", c4) for c4 in range(4)])

        trp = None
        for ui, (b, g) in enumerate(units):
            parts = [1] if (ts == 0 and b == 0) else [0, 1]
            pti = ui % 2
            banks = (self.nbank(), self.nbank())
            c0 = 256 * parts[0]
            n = 256 * len(parts)
            for half in range(2):
                tb = AP(self.tbl, 2048, 0, 128, 1024 * g + 512 * half + c0, [(1, n)])
                self.mm(self.P(banks[half], n, c0), idn, tb, True, False, [("c", "ident"), ("c", "tbl")],
                        [("ps", banks[half])])
            for pi, p in enumerate(parts):
                kslot = b + p
                for half in range(2):
                    lhsT = AP(self.kdup, 1280, 64 * half, 64, 640 * g + 128 * kslot, [(1, 128)])
                    rhs = AP(self.Q2, 2048, 64 * half, 64, 512 * (2 * g) + 128 * b, [(512, 2), (1, 128)])
                    self.mm(self.P(banks[half], 256, 256 * p), lhsT, rhs, False, pi == len(parts) - 1,
                            [("k", kslot), ("q", 2 * g), ("q", 2 * g + 1)], [("ps", banks[half])])
            for half in range(2):
                dst = AP(self.PT, 2048, 0, 128, 1024 * pti + 512 * half + c0, [(1, n)])
                self.op("act", lambda e, o=dst, s=self.P(banks[half], n, c0):
                        e.activation(out=o, in_=s, func=AF.Exp, scale=0.125),
                        [("ps", banks[half])], [("PT", pti)])
            done = pv(pend) if pend is not None else None
            if trp is not None:
                tr_block(trp)
            trp = done
            pend = (b, g, parts, pti, ui)
            yield
        done = pv(pend)
        if trp is not None:
            tr_block(trp)
        yield
        tr_block(done)
        yield

    def stage_spatial(self, t):
        for b in range(NB):
            bk = self.nbank()
            for gq in range(4):
                lhsT = AP(self.vln, 2048, 0, 128, 512 * b + 128 * gq, [(1, 128)])
                rhs = AP(self.wsT, 512, 0, 128, 128 * gq, [(1, 128)])
                self.mm(self.P(bk, 128, 128 * gq), lhsT, rhs, True, True, [("vln", b), ("c", "wsT")], [("ps", bk)])
            tq = self.ntmp()
            for gq in range(4):
                self.op("dve", lambda e, o=self.T(tq, 128, off=128 * gq), i=self.P(bk, 128, 128 * gq),
                        s=AP(self.lngpp, 4, 0, 128, gq, [(1, 1)]), c=AP(self.B2, 512, 0, 128, 128 * gq, [(1, 128)]):
                        e.scalar_tensor_tensor(out=o, in0=i, scalar=s, in1=c, op0=ALU.mult, op1=ALU.add),
                        [("ps", bk), ("c", "lngpp"), ("c", "B2")], [("tmp", tq)])
            t4 = AP(self.tmp[tq], 512, 0, 128, 0, [(128, 4), (1, 128)])
            u4 = AP(self.uT, 2048, 0, 128, 128 * b, [(512, 4), (1, 128)])
            dst = AP(self.gmT, 2048, 0, 128, 128 * b, [(512, 4), (1, 128)])
            self.op("dve", lambda e, o=dst, a=t4, c=u4: e.tensor_tensor(out=o, in0=a, in1=c, op=ALU.mult),
                    [("tmp", tq)] + [("u", i) for i in range(4)], [("gm", i) for i in range(4)])

    def stage_merge(self, t, inter=None, per_chunk=2):
        for c in range(8):
            slot, rk = self.next_chunk(C_M0 + c)
            bga, bgb, bra, brb = self.nbank(), self.nbank(), self.nbank(), self.nbank()
            self.featT_mm(slot, rk, 0, bga, self.xnT, "xnT", 256)
            self.featT_mm(slot, rk, 128, bgb, self.xnT, "xnT", 256)
            self.featT_mm(slot, rk, 0, bra, self.attnT, "attn", 128, nk=4, woff=2048)
            self.featT_mm(slot, rk, 0, brb, self.gmT, "gm", 128, nk=4, woff=2560)
            self.chunk_done()
            ta, tb_, t1, t2 = self.ntmp(), self.ntmp(), self.ntmp(), self.ntmp()
            self.op("act", lambda e, o=self.T(ta), s=self.P(bga): e.activation(out=o, in_=s, func=AF.Sigmoid),
                    [("ps", bga)], [("tmp", ta)])
            self.op("act", lambda e, o=self.T(tb_), s=self.P(bgb): e.activation(out=o, in_=s, func=AF.Sigmoid),
                    [("ps", bgb)], [("tmp", tb_)])
            self.op("dve", lambda e, o=self.T(t1), a=self.P(bra), c=self.T(ta):
                    e.tensor_tensor(out=o, in0=a, in1=c, op=ALU.mult), [("ps", bra), ("tmp", ta)], [("tmp", t1)])
            self.op("dve", lambda e, o=self.T(t2), a=self.P(brb), c=self.T(tb_):
                    e.tensor_tensor(out=o, in0=a, in1=c, op=ALU.mult), [("ps", brb), ("tmp", tb_)], [("tmp", t2)])
            dst = AP(self.mergedT, 4096, 0, 128, 512 * c, [(1, 512)])
            self.op("pool", lambda e, o=dst, a=self.T(t1), c_=self.T(t2):
                    e.tensor_tensor(out=o, in0=a, in1=c_, op=ALU.add), [("tmp", t1), ("tmp", t2)], [("mg", c)])
            if inter is not None and c >= 1:
                for _ in range(per_chunk):
                    next(inter, None)
        if inter is not None:
            for _ in inter:
                pass

    def tokmajor_out(self, xi, act_t, akey, chunk_groups, gbc, gname, stbase, inter=None, after_pair=None):
        xb = self.xb[xi]
        F = act_t.shape[1]
        ktot = sum(nk for _, nk in chunk_groups[0])
        resident = all(len(g) == 1 for g in chunk_groups)
        held = None
        for pair in range(2):
            blocks = (2 * pair, 2 * pair + 1)
            banks = {}
            for hf in range(2):
                for b in blocks:
                    banks[(hf, b)] = self.nbank()
            self.held |= set(banks.values())
            if resident:
                if pair == 0:
                    held = [self.next_chunk(chunk_groups[hf][0][0]) for hf in range(2)]
            for hf in range(2):
                k0 = 0
                for gi, (cid, nk) in enumerate(chunk_groups[hf]):
                    slot, rk = held[hf] if resident else self.next_chunk(cid)
                    for b in blocks:
                        for kk in range(nk):
                            kg = k0 + kk
                            lhsT = AP(act_t, F, 0, 128, 512 * kg + 128 * b, [(1, 128)])
                            rhs = AP(slot, CHE, 0, 128, 512 * kk, [(1, 512)])
                            self.mm(self.P(banks[(hf, b)]), lhsT, rhs, kg == 0, kg == ktot - 1,
                                    rk + [(akey, kg)], [("ps", banks[(hf, b)])])
                    k0 += nk
                    if not resident:
                        self.chunk_done()
                    if inter is not None:
                        next(inter, None)
            if resident and pair == 1:
                self.chunk_done()
            self.held -= set(banks.values())
            self.soft = set(banks.values())
            self.soft_n = 4
            tys = {}
            for b in blocks:
                for hf in range(2):
                    bk = banks[(hf, b)]
                    tj = self.ntmp()
                    acc = AP(self.st, 64, 0, 128, stbase + 2 * b + hf, [(1, 1)])
                    self.op("act", lambda e, o=self.T(tj), i=self.P(bk), a=acc:
                            e.activation(out=o, in_=i, func=AF.Square, accum_out=a),
                            [("ps", bk)], [("tmp", tj), ("st", stbase + 2 * b + hf)])
                    ty = self.ntmp()
                    tys[(hf, b)] = ty
                    g = AP(gbc, 1024, 0, 128, 512 * hf, [(1, 512)])
                    self.op("dve", lambda e, o=self.T(ty), i=self.P(bk), g=g:
                            e.tensor_tensor(out=o, in0=i, in1=g, op=ALU.mult),
                            [("ps", bk), ("c", gname)], [("tmp", ty)])
            b0 = blocks[0]
            sse = AP(self.st, 64, 0, 128, stbase + 2 * b0, [(2, 2)])
            sso = AP(self.st, 64, 0, 128, stbase + 2 * b0 + 1, [(2, 2)])
            ms = AP(self.st, 64, 0, 128, stbase + 8 + b0, [(1, 2)])
            mkey = ("st", stbase + 8 + pair)
            self.op("dve", lambda e, o=ms, a=sse, c=sso: e.tensor_tensor(out=o, in0=a, in1=c, op=ALU.add),
                    [("st", stbase + 2 * b + hf) for b in blocks for hf in range(2)], [mkey])
            self.op("dve", lambda e, o=ms: e.tensor_scalar(out=o, in0=o, scalar1=1.0 / D, scalar2=EPS,
                                                           op0=ALU.mult, op1=ALU.add), [mkey], [mkey])
            nh = AP(self.neghalf, 8, 0, 128, 0, [(1, 2)])
            self.op("pool", lambda e, o=ms, h=nh: e.tensor_tensor(out=o, in0=o, in1=h, op=ALU.pow),
                    [mkey, ("c", "neghalf")], [mkey])
            for b in blocks:
                for hf in range(2):
                    ty = tys[(hf, b)]
                    rs = AP(self.st, 64, 0, 128, stbase + 8 + b, [(1, 1)])
                    xh = AP(xb, 4096, 0, 128, D * b + 512 * hf, [(1, 512)])
                    self.op("dve", lambda e, o=xh, i=self.T(ty), s=rs, x_=xh:
                            e.scalar_tensor_tensor(out=o, in0=i, scalar=s, in1=x_, op0=ALU.mult, op1=ALU.add),
                            [("tmp", ty), mkey, ("xb", xi, b)], [("xb", xi, b)])
            if after_pair is not None:
                after_pair(pair)
        if inter is not None:
            for _ in inter:
                pass

    def stage_ffn_gu(self, t, mid=None, mid_at=2):
        for f in range(11):
            if mid is not None and f == mid_at:
                mid()
            slot, rk = self.next_chunk(C_G0 + f)
            for j in range(2):
                fc = 2 * f + j
                bg = self.nbank()
                bu = self.nbank()
                self.featT_mm(slot, rk, 128 * j, bg, self.hnT, "hnT", 512)
                self.featT_mm(slot, rk, 256 + 128 * j, bu, self.hnT, "hnT", 512)
                tg = self.ntmp()
                self.op("act", lambda e, o=self.T(tg), s=self.P(bg): e.activation(out=o, in_=s, func=AF.Silu),
                        [("ps", bg)], [("tmp", tg)])
                dst = AP(self.hT, NFC * 512, 0, 128, 512 * fc, [(1, 512)])
                self.op("dve", lambda e, o=dst, a=self.P(bu), c=self.T(tg):
                        e.tensor_tensor(out=o, in0=a, in1=c, op=ALU.mult),
                        [("ps", bu), ("tmp", tg)], [("h", fc)])
            self.chunk_done()

    def store_y(self, t):
        xi = t % NXB
        xb = self.xb[xi]
        dst = bass.AP(self.y_d, t * TT * D, [[D, 128], [128 * D, NB], [1, D]])
        src = AP(xb, 4096, 0, 128, 0, [(D, NB), (1, D)])
        self.dma(dst, src, [("xb", xi, b) for b in range(NB)], [("y", t)], ("Y", xi))

    def record(self):
        n = self.n_tiles
        self.stream_init()
        self.load_x(0)
        self.load_consts(0)
        for _ in range(NSLOT - 1):
            self.record_load()
        self.rms_scale_all(0, 0)
        self.transposes(self.xnT, "xnT", self.g1, "g1")
        self.load_consts(1)
        self.stage_front(0)
        for t in range(1, min(n, NXB)):
            self.load_x(t)
        wo_groups = [[(C_O0, 8)], [(C_O0 + 1, 8)]]
        dn_groups = [[(C_D0, 8), (C_D0 + 1, 8), (C_D0 + 2, 6)], [(C_D0 + 3, 8), (C_D0 + 4, 8), (C_D0 + 5, 6)]]
        for _ in self.stage_attn(0):
            pass
        self.stage_spatial(0)
        for t in range(n):
            xi = t % NXB
            nxt = t + 1 < n
            self.stage_merge(t, inter=self.rms_scale((t + 1) % NXB, 0) if nxt else None)
            if t >= 1 and t - 1 + NXB < n:
                self.load_x(t - 1 + NXB)
            if nxt:
                self.transposes(self.xnT, "xnT", self.g1, "g1")
            self.tokmajor_out(xi, self.mergedT, "mg", wo_groups, self.g2bc, "g2bc", 16,
                              after_pair=lambda pair, xi=xi: self.rms_scale_all(xi, 1, (2 * pair, 2 * pair + 1)))
            hT = lambda: self.transposes(self.hnT, "hnT", self.g3, "g3")
            if nxt:
                self.stage_front(t + 1, mid=hT)
                self.stage_ffn_gu(t, mid=lambda t=t: self.stage_spatial(t + 1))
            else:
                hT()
                self.stage_ffn_gu(t)
            inter = self.stage_attn(t + 1) if nxt else None
            self.tokmajor_out(xi, self.hT, "h", dn_groups, self.g4bc, "g4bc", 16, inter=inter)
            self.store_y(t)
        self.op("sp", None, [("y", t) for t in range(n)], [])
        self.S.finalize()

    def emit(self):
        nc = self.nc
        S = self.S
        for key in S.dma_count:
            self.sem_dma[key] = self.es.enter_context(nc.semaphore("d_" + "_".join(str(k) for k in key)))

        def semof(key):
            return self.sem_eng[key[1]] if key[0] == "eng" else self.sem_dma[key[1]]

        def run(eng_name):
            def body(e):
                for op in S.ops[eng_name]:
                    for key, val in op.waits:
                        e.wait_ge(semof(key), val)
                    if op.fn is None:
                        continue
                    ins = op.fn(e)
                    if op.dma_key is not None:
                        ins.then_inc(self.sem_dma[op.dma_key], 16)
                    elif op.marked:
                        ins.then_inc(self.sem_eng[eng_name], 1)
            return body

        with nc.Block() as block:
            block.sync(run("sp"))
            block.tensor(run("pe"))
            block.scalar(run("act"))
            block.vector(run("dve"))
            block.gpsimd(run("pool"))


def build(n_tiles, tiles_per_seq):
    b = Builder(n_tiles, tiles_per_seq)
    es = ExitStack()
    with es:
        b.alloc(es)
        b.record()
        b.emit()
    return b.nc


def prepare_weights(inp):
    wsrc = build_wsrc(inp["w_in"][0], inp["w_attn_branch"][0], inp["w_gmlp_branch"][0], inp["w_out"][0],
                      inp["w_ffn_gate"][0], inp["w_ffn_up"][0], inp["w_ffn_down"][0])
    consts = build_consts(inp["norm_mix_pre"][0], inp["norm_mix_post"][0], inp["norm_ffn_pre"][0],
                          inp["norm_ffn_post"][0], inp["attn_sinks"][0], inp["gmlp_ln_g"][0],
                          inp["gmlp_ln_b"][0], inp["gmlp_w_s"][0], inp["gmlp_b_s"][0])
    m = {"wsrc": wsrc.reshape(NCH * 128, CHE)}
    for k, v in consts.items():
        m["c_" + k] = v
    return m


def kernel(**inputs):
    inp = {k: np.asarray(v, dtype=np.float32) for k, v in inputs.items()}
    x = inp["x"]
    B, S, _ = x.shape
    tps = S // TT
    seq_per_core = B // N_CORES
    n_tiles = seq_per_core * tps
    shared = prepare_weights(inp)
    xs = np.ascontiguousarray(x).reshape(N_CORES, seq_per_core * S, D)
    nc = build(n_tiles, tps)
    in_maps = []
    for c in range(N_CORES):
        m = dict(shared)
        m["x"] = xs[c]
        in_maps.append(m)
    res = run_bass_kernel_spmd(nc, in_maps, core_ids=list(range(N_CORES)))
    out = np.stack([np.asarray(r["y"], dtype=np.float32) for r in res.results], axis=0)
    return out.reshape(B, S, D)
```

```python
import numpy as np
from contextlib import ExitStack
import concourse.bass as bass
import concourse.mybir as mybir
from concourse.bass_utils import run_bass_kernel_spmd

F32 = mybir.dt.float32
BF16 = mybir.dt.bfloat16
AF = mybir.ActivationFunctionType
ALU = mybir.AluOpType

D = 1024
DFF = 2816
NFC = 22
TT = 512
NB = 4
EPS = 1e-6
LN_EPS = 1e-5
NCH = 31
CHE = 4096
NSLOT = 4
NTMP = 10
N_CORES = 8
CAST_DMA = True
NXB = 3

ENGS = ("pe", "act", "dve", "pool", "sp")


class Op:
    __slots__ = ("eng", "fn", "deps", "marked", "mark", "dma_key", "dma_val", "waits")

    def __init__(self, eng, fn):
        self.eng = eng
        self.fn = fn
        self.deps = []
        self.marked = False
        self.mark = 0
        self.dma_key = None
        self.dma_val = 0
        self.waits = []


class Sched:
    def __init__(self):
        self.ops = {e: [] for e in ENGS}
        self.state = {}
        self.dma_count = {}
        self.bulk = {}

    def add(self, eng, fn, reads=(), writes=(), dma=None, bulk=False):
        op = Op(eng, fn)
        deps = {}
        tmpkey = False

        def dep(o, kind):
            nonlocal tmpkey
            if o is op:
                return
            if o.dma_key is None and o.eng == eng and kind != "raw" and not (kind == "waw" and tmpkey):
                return
            k = id(o)
            if k not in deps:
                deps[k] = o

        for k in reads:
            st = self.state.get(k)
            if st is None:
                continue
            if st[0] is not None:
                dep(st[0], "raw")
            if isinstance(k, tuple) and k[0] == "ps":
                for r in st[1].values():
                    if r.eng != eng:
                        dep(r, "rar")
        for k in writes:
            st = self.state.get(k)
            if st is None:
                continue
            tmpkey = isinstance(k, tuple) and k[0] == "tmp" and eng != "pe"
            if st[0] is not None:
                dep(st[0], "waw")
            tmpkey = False
            for r in st[1].values():
                dep(r, "war")
        op.deps = list(deps.values())
        for o in op.deps:
            o.marked = True
        if dma is not None:
            op.dma_key = dma
            c = self.dma_count.get(dma, 0) + 1
            self.dma_count[dma] = c
            op.dma_val = 16 * c
            if bulk:
                self.bulk.setdefault(dma, []).append(op)
        for k in writes:
            self.state[k] = [op, {}]
        for k in reads:
            st = self.state.get(k)
            if st is None:
                st = [None, {}]
                self.state[k] = st
            st[1][eng] = op
        self.ops[eng].append(op)
        return op

    def finalize(self):
        for key, ops in self.bulk.items():
            tot = 16 * self.dma_count[key]
            for o in ops:
                o.dma_val = tot
        for e in ENGS:
            n = 0
            for op in self.ops[e]:
                if op.dma_key is None and op.marked:
                    n += 1
                    op.mark = n
        for e in ENGS:
            waited = {}
            for op in self.ops[e]:
                w = {}
                for d in op.deps:
                    if d.dma_key is not None:
                        key, val = ("dma", d.dma_key), d.dma_val
                    else:
                        key, val = ("eng", d.eng), d.mark
                    if val > w.get(key, 0):
                        w[key] = val
                op.waits = []
                for key, val in w.items():
                    if waited.get(key, 0) >= val:
                        continue
                    waited[key] = val
                    op.waits.append((key, val))


def AP(t, F, p0, npart, off, dims):
    return bass.AP(t, p0 * F + off, [[F, npart]] + [[s, c] for (s, c) in dims])


def _kc_layout(w):
    C = w.shape[1]
    return np.ascontiguousarray(w.reshape(8, 128, C).transpose(1, 0, 2)).reshape(128, 8 * C)


def _pad(a):
    out = np.zeros((128, CHE), np.float32)
    out[: a.shape[0], : a.shape[1]] = a
    return out


def _q_perm():
    cols = []
    for i in range(4):
        g, j = i // 2, i % 2
        lo, hi = 4 * g + j, 4 * g + 2 + j
        cols += list(range(64 * lo, 64 * lo + 64)) + list(range(64 * hi, 64 * hi + 64))
    return np.array(cols)


C_Q, C_KV, C_U, C_VG = 0, 1, 2, 3
C_M0 = 4
C_O0 = 12
C_G0 = 14
C_D0 = 25


def build_wsrc(w_in, w_a, w_g, w_out, w_gate, w_up, w_down):
    chunks = []
    wq = w_in[:, 0:512][:, _q_perm()]
    chunks.append(_pad(_kc_layout(wq)))
    wk = w_in[:, 512:640]
    wv = w_in[:, 640:768]
    kv = np.concatenate([wk[:, 0:64], wk[:, 0:64], wk[:, 64:128], wk[:, 64:128], wv], axis=1)
    chunks.append(_pad(_kc_layout(kv)))
    chunks.append(_pad(_kc_layout(w_in[:, 768:1280])))
    chunks.append(_pad(_kc_layout(w_in[:, 1280:1792])))
    for c in range(8):
        ga = w_in[:, 1792 + 128 * c: 1792 + 128 * (c + 1)]
        gb = w_in[:, 2816 + 128 * c: 2816 + 128 * (c + 1)]
        gab = _kc_layout(np.concatenate([ga, gb], axis=1))
        wa = w_a[:, 128 * c:128 * (c + 1)].reshape(4, 128, 128).transpose(1, 0, 2).reshape(128, 512)
        wg = w_g[:, 128 * c:128 * (c + 1)].reshape(4, 128, 128).transpose(1, 0, 2).reshape(128, 512)
        chunks.append(_pad(np.concatenate([gab, wa, wg], axis=1)))
    for hf in range(2):
        chunks.append(_pad(_kc_layout(w_out[:, 512 * hf:512 * (hf + 1)])))
    for f in range(11):
        gu = np.concatenate([w_gate[:, 256 * f:256 * (f + 1)], w_up[:, 256 * f:256 * (f + 1)]], axis=1)
        chunks.append(_pad(_kc_layout(gu)))
    for hf in range(2):
        for (f0, f1) in ((0, 8), (8, 16), (16, 22)):
            blk = w_down[128 * f0:128 * f1, 512 * hf:512 * (hf + 1)]
            n = f1 - f0
            chunks.append(_pad(blk.reshape(n, 128, 512).transpose(1, 0, 2).reshape(128, n * 512)))
    assert len(chunks) == NCH
    return np.stack(chunks, axis=0)


def build_consts(norm_mix_pre, norm_mix_post, norm_ffn_pre, norm_ffn_post, attn_sinks,
                 ln_g, ln_b, w_s, b_s):
    c = {}
    c["ident"] = np.eye(128, dtype=np.float32)
    tbl = np.zeros((128, 2, 2, 2, 2, 128), np.float32)
    j = np.arange(128)[:, None]
    a = np.arange(128)[None, :]
    for g in range(2):
        for half in range(2):
            for hh2 in range(2):
                h = 4 * g + 2 * half + hh2
                slope = 2.0 ** (-(h + 1))
                rel0 = 128 + a - j
                rel1 = a - j
                tbl[:, g, half, 0, hh2, :] = np.where(a < j, -slope * rel0 * 8.0, -30000.0)
                tbl[:, g, half, 1, hh2, :] = np.where(a >= j, -slope * rel1 * 8.0, -30000.0)
    c["tbl"] = tbl.reshape(128, 2048)
    c["g1"] = np.ascontiguousarray(norm_mix_pre.reshape(8, 128).T)
    c["g3"] = np.ascontiguousarray(norm_ffn_pre.reshape(8, 128).T)
    c["g2bc"] = np.ascontiguousarray(np.broadcast_to(norm_mix_post.reshape(1, D), (128, D)))
    c["g4bc"] = np.ascontiguousarray(np.broadcast_to(norm_ffn_post.reshape(1, D), (128, D)))
    c["lngpp"] = np.ascontiguousarray(ln_g.reshape(4, 128).T)
    c["lnbpp"] = np.ascontiguousarray(ln_b.reshape(4, 128).T)
    c["bsbc"] = np.ascontiguousarray(np.broadcast_to(b_s.reshape(1, 512), (128, 512)))
    c["sinkbc"] = np.ascontiguousarray(np.broadcast_to(attn_sinks.reshape(1, 8), (128, 8)))
    c["wsT"] = np.ascontiguousarray(w_s.transpose(2, 0, 1)).reshape(128, 512)
    s_ = np.arange(128)[:, None]
    t_ = np.arange(128)[None, :]
    cm = (t_ >= s_).astype(np.float32)
    c["cmask"] = np.ascontiguousarray(np.broadcast_to(cm[:, None, :], (128, 4, 128))).reshape(128, 512)
    return {k: np.ascontiguousarray(v, dtype=np.float32) for k, v in c.items()}


CONST_SHAPES = {"ident": 128, "tbl": 2048, "g1": 8, "g3": 8, "g2bc": 1024, "g4bc": 1024,
                "lngpp": 4, "lnbpp": 4, "bsbc": 512, "sinkbc": 8, "wsT": 512, "cmask": 512}


class Builder:
    def __init__(self, n_tiles, tiles_per_seq):
        self.n_tiles = n_tiles
        self.tps = tiles_per_seq
        self.S = Sched()
        self.nc = bass.Bass("TRN2", target_bir_lowering=False)
        self.ntok = n_tiles * TT

    def alloc(self, es):
        nc = self.nc
        self.x_d = nc.dram_tensor("x", [self.ntok, D], F32, kind="ExternalInput")
        self.y_d = nc.dram_tensor("y", [self.ntok, D], F32, kind="ExternalOutput")
        self.wsrc_d = nc.dram_tensor("wsrc", [NCH * 128, CHE], F32, kind="ExternalInput")
        self.wbf_d = nc.dram_tensor("wbf", [NCH * 128, CHE], BF16, kind="Internal")
        self.c_d = {k: nc.dram_tensor("c_" + k, [128, n], F32, kind="ExternalInput")
                    for k, n in CONST_SHAPES.items()}

        def sb(name, n, dt):
            return es.enter_context(nc.sbuf_tensor(name, [128, n], dt))

        self.xb = [sb("xb%d" % i, 4096, F32) for i in range(NXB)]
        self.xn = sb("xn", 4096, BF16)
        self.xnT = sb("xnT", 4096, BF16)
        self.hnT = sb("hnT", 4096, BF16)
        self.Q2 = sb("Q2", 2048, BF16)
        self.kdup = sb("kdup", 1280, BF16)
        self.vtok = sb("vtok", 650, BF16)
        self.uT = sb("uT", 2048, BF16)
        self.vln = sb("vln", 2048, BF16)
        self.PT = sb("PT", 2048, BF16)
        self.attnT = sb("attnT", 2048, BF16)
        self.atok = sb("atok", 2048, BF16)
        self.gmT = sb("gmT", 2048, BF16)
        self.mergedT = sb("mergedT", 4096, BF16)
        self.hT = sb("hT", NFC * 512, BF16)
        self.ring = [sb("ring%d" % i, CHE, BF16) for i in range(NSLOT)]
        self.stg = [sb("stg%d" % i, 1024, F32) for i in range(2)] if not CAST_DMA else []
        self.stg_i = 0
        self.tmp = [sb("tmp%d" % i, 512, F32) for i in range(NTMP)]
        self.tmp_i = 0
        self.tbl = sb("tbl", 2048, BF16)
        self.g2bc = sb("g2bc", 1024, F32)
        self.g4bc = sb("g4bc", 1024, F32)
        self.B2 = sb("B2", 512, F32)
        self.esink = sb("esink", 8, F32)
        self.wsT = sb("wsT", 512, BF16)
        self.ident = sb("ident", 128, BF16)
        self.ones = sb("ones", 128, BF16)
        self.g1 = sb("g1", 8, F32)
        self.g3 = sb("g3", 8, F32)
        self.lngpp = sb("lngpp", 4, F32)
        self.lnbpp = sb("lnbpp", 4, F32)
        self.neghalf = sb("neghalf", 8, F32)
        self.st = sb("stats", 64, F32)
        self.bnst = sb("bnst", 4 * 6, F32)
        self.ps = [es.enter_context(nc.psum_tensor("ps%d" % i, [128, 512], F32)) for i in range(8)]
        self.psb = [p.bitcast(BF16) for p in self.ps]
        self.ps_i = 0
        self.held = set()
        self.soft = set()
        self.soft_n = 0
        self.sem_eng = {e: es.enter_context(nc.semaphore("s_" + e)) for e in ("pe", "act", "dve", "pool")}
        self.sem_dma = {}
        self.es = es

    def nbank(self):
        pick = None
        for k in range(8):
            i = (self.ps_i + k) % 8
            if i in self.held:
                continue
            if self.soft_n > 0 and i in self.soft:
                continue
            pick = i
            break
        if pick is None:
            for k in range(8):
                i = (self.ps_i + k) % 8
                if i not in self.held:
                    pick = i
                    break
        self.ps_i = (pick + 1) % 8
        if self.soft_n > 0:
            self.soft_n -= 1
        return pick

    def ntmp(self):
        i = self.tmp_i
        self.tmp_i = (i + 1) % NTMP
        return i

    def T(self, i, n=512, p0=0, npart=128, off=0):
        return AP(self.tmp[i], 512, p0, npart, off, [(1, n)])

    def P(self, bk, n=512, off=0, p0=0, npart=128):
        return AP(self.ps[bk], 512, p0, npart, off, [(1, n)])

    def op(self, eng, fn, reads=(), writes=(), dma=None, bulk=False):
        return self.S.add(eng, fn, reads, writes, dma, bulk)

    def mm(self, out, lhsT, rhs, start, stop, reads, writes):
        self.op("pe", lambda e: e.matmul(out, lhsT, rhs, start=start, stop=stop), reads, writes)

    def dma(self, out, in_, reads, writes, key, bulk=False, q="sp", max_last=None):
        if max_last is None:
            fn = lambda e: e.dma_start(out=out, in_=in_)
        else:
            fn = lambda e: e.dma_start(out=out, in_=in_, max_dma_last_dim=max_last)
        self.op(q, fn, reads, writes, dma=key, bulk=bulk)

    def load_consts(self, part):
        ckey = ("C%d" % part,)
        def cdram(name, off=0, n=None):
            F = CONST_SHAPES[name]
            return AP(self.c_d[name], F, 0, 128, off, [(1, n or F)])

        def direct(name, t):
            n = CONST_SHAPES[name]
            self.dma(AP(t, n, 0, 128, 0, [(1, n)]), cdram(name), [], [("c", name)], ckey, bulk=True)

        def via_tmp(name, off, n):
            ti = self.ntmp()
            self.dma(self.T(ti, n), cdram(name, off, n), [], [("tmp", ti)], ckey, bulk=True)
            return ti

        if part == 0:
            for name, t in (("g1", self.g1), ("g3", self.g3), ("lngpp", self.lngpp), ("lnbpp", self.lnbpp)):
                direct(name, t)
        else:
            for name, t in (("g2bc", self.g2bc), ("g4bc", self.g4bc)):
                direct(name, t)
        if part == 0:
            self.op("dve", lambda e, o=AP(self.ones, 128, 0, 128, 0, [(1, 128)]): e.memset(o, 1.0), [], [("c", "ones")])
            self.op("dve", lambda e, o=AP(self.neghalf, 8, 0, 128, 0, [(1, 8)]): e.memset(o, -0.5), [],
                    [("c", "neghalf")])
            ti = via_tmp("ident", 0, 128)
            self.op("dve", lambda e, o=AP(self.ident, 128, 0, 128, 0, [(1, 128)]), s=self.T(ti, 128):
                    e.tensor_copy(out=o, in_=s), [("tmp", ti)], [("c", "ident")])
        else:
            for q4 in range(4):
                ti = via_tmp("tbl", 512 * q4, 512)
                self.op("dve", lambda e, o=AP(self.tbl, 2048, 0, 128, 512 * q4, [(1, 512)]), s=self.T(ti):
                        e.tensor_copy(out=o, in_=s), [("tmp", ti)], [("c", "tbl")])
            ti = via_tmp("sinkbc", 0, 8)
            self.op("act", lambda e, o=AP(self.esink, 8, 0, 128, 0, [(1, 8)]), s=self.T(ti, 8):
                    e.activation(out=o, in_=s, func=AF.Exp), [("tmp", ti)], [("c", "esink")])
            self.op("dve", lambda e, o=AP(self.vtok, 650, 0, 128, 0, [(1, 650)]): e.memset(o, 1.0), [],
                    [("v", i) for i in range(5)])
            t3 = via_tmp("wsT", 0, 512)
            t4 = via_tmp("cmask", 0, 512)
            self.op("dve", lambda e, o=AP(self.wsT, 512, 0, 128, 0, [(1, 512)]), a=self.T(t3), b=self.T(t4):
                    e.tensor_tensor(out=o, in0=a, in1=b, op=ALU.mult), [("tmp", t3), ("tmp", t4)], [("c", "wsT")])
            t5 = via_tmp("bsbc", 0, 512)
            bk = self.nbank()
            self.mm(self.P(bk), AP(self.ones, 128, 0, 128, 0, [(1, 128)]), AP(self.wsT, 512, 0, 128, 0, [(1, 512)]),
                    True, True, [("c", "ones"), ("c", "wsT")], [("ps", bk)])
            for g in range(4):
                self.op("dve", lambda e, o=AP(self.B2, 512, 0, 128, 128 * g, [(1, 128)]), i=self.P(bk, 128, 128 * g),
                        s=AP(self.lnbpp, 4, 0, 128, g, [(1, 1)]), b=self.T(t5, 128, off=128 * g):
                        e.scalar_tensor_tensor(out=o, in0=i, scalar=s, in1=b, op0=ALU.mult, op1=ALU.add),
                        [("ps", bk), ("c", "lnbpp"), ("tmp", t5)], [("c", "B2")])

    def stream_init(self):
        order = [C_Q, C_KV, C_U, C_VG]
        for t in range(self.n_tiles):
            order += [C_M0 + i for i in range(8)]
            order += [C_O0, C_O0 + 1]
            if t + 1 < self.n_tiles:
                order += [C_Q, C_KV, C_U, C_VG]
            order += [C_G0 + i for i in range(11)]
            order += [C_D0 + i for i in range(6)]
            order += [C_D0 + i for i in range(6)]
        self.order = order
        self.cast_done = set()
        self.n_loaded = 0
        self.n_used = 0

    def record_load(self):
        n = self.n_loaded
        if n >= len(self.order):
            return
        self.n_loaded += 1
        c = self.order[n]
        s = n % NSLOT
        slot = self.ring[s]
        rkeys = [("ring", s, q4) for q4 in range(4)]
        if c not in self.cast_done:
            self.cast_done.add(c)
            if CAST_DMA:
                self.dma(AP(slot, CHE, 0, 128, 0, [(1, CHE)]), AP(self.wsrc_d, CHE, c * 128, 128, 0, [(1, CHE)]),
                         [], rkeys, ("WC", s), q="pool", max_last=8192)
            for q4 in (range(4) if not CAST_DMA else ()):
                si = self.stg_i
                self.stg_i = 1 - si
                stg = AP(self.stg[si], 1024, 0, 128, 0, [(1, 1024)])
                src = AP(self.wsrc_d, CHE, c * 128, 128, 1024 * q4, [(1, 1024)])
                self.dma(stg, src, [], [("stg", si)], ("ST", si))
                dst = AP(slot, CHE, 0, 128, 1024 * q4, [(1, 1024)])
                eng = ("dve", "act", "pool", "dve")[q4]
                if eng == "act":
                    fn = lambda e, o=dst, i=stg: e.activation(out=o, in_=i, func=AF.Copy)
                else:
                    fn = lambda e, o=dst, i=stg: e.tensor_copy(out=o, in_=i)
                self.op(eng, fn, [("stg", si)], [("ring", s, q4)])
            self.dma(AP(self.wbf_d, CHE, c * 128, 128, 0, [(1, CHE)]), AP(slot, CHE, 0, 128, 0, [(1, CHE)]),
                     rkeys, [("wbf", c)], ("WS", s))
        else:
            self.dma(AP(slot, CHE, 0, 128, 0, [(1, CHE)]), AP(self.wbf_d, CHE, c * 128, 128, 0, [(1, CHE)]),
                     [("wbf", c)], rkeys, ("W", s))

    def next_chunk(self, expect):
        n = self.n_used
        assert self.order[n] == expect, (n, self.order[n], expect)
        self.n_used += 1
        while self.n_loaded <= n:
            self.record_load()
        s = n % NSLOT
        return self.ring[s], [("ring", s, q4) for q4 in range(4)]

    def chunk_done(self):
        while self.n_loaded < min(self.n_used + NSLOT - 1, len(self.order)):
            self.record_load()

    def load_x(self, t):
        xb = self.xb[t % NXB]
        src = bass.AP(self.x_d, t * TT * D, [[D, 128], [128 * D, NB], [1, D]])
        dst = AP(xb, 4096, 0, 128, 0, [(D, NB), (1, D)])
        self.dma(dst, src, [], [("xb", t % NXB, b) for b in range(NB)], ("X", t % NXB))

    def rms_scale(self, xi, which, blocks=(0, 1, 2, 3)):
        xb = self.xb[xi]
        base = 8 * which
        nb_ = len(blocks)
        b0 = blocks[0]
        mkey = ("stm", which, b0)
        for b in blocks:
            src = AP(xb, 4096, 0, 128, D * b, [(1, D)])
            junk = AP(self.xn, 4096, 0, 128, D * b, [(1, D)])
            acc = AP(self.st, 64, 0, 128, base + b, [(1, 1)])
            self.op("act", lambda e, o=junk, i=src, a=acc: e.activation(out=o, in_=i, func=AF.Square, accum_out=a),
                    [("xb", xi, b)], [("xn", b), ("st", base + b)])
            yield
        ss = AP(self.st, 64, 0, 128, base + b0, [(1, nb_)])
        ms = AP(self.st, 64, 0, 128, base + 4 + b0, [(1, nb_)])
        self.op("dve", lambda e, o=ms, i=ss: e.tensor_scalar(out=o, in0=i, scalar1=1.0 / D, scalar2=EPS,
                                                             op0=ALU.mult, op1=ALU.add),
                [("st", base + b) for b in blocks], [mkey])
        nh = AP(self.neghalf, 8, 0, 128, 0, [(1, nb_)])
        self.op("pool", lambda e, o=ms, i=ms, h=nh: e.tensor_tensor(out=o, in0=i, in1=h, op=ALU.pow),
                [mkey, ("c", "neghalf")], [mkey])
        for b in blocks:
            src = AP(xb, 4096, 0, 128, D * b, [(1, D)])
            dst = AP(self.xn, 4096, 0, 128, D * b, [(1, D)])
            sc = AP(self.st, 64, 0, 128, base + 4 + b, [(1, 1)])
            eng = "dve" if b % 2 == 0 else "pool"
            self.op(eng, lambda e, o=dst, i=src, s=sc: e.tensor_scalar(out=o, in0=i, scalar1=s, scalar2=0.0,
                                                                       op0=ALU.mult, op1=ALU.add),
                    [("xb", xi, b), mkey], [("xn", b)])
            yield

    def rms_scale_all(self, xi, which, blocks=(0, 1, 2, 3)):
        for _ in self.rms_scale(xi, which, blocks):
            pass

    def transposes(self, dst_t, dkey, gain, gname):
        for kc in range(8):
            bk = self.nbank()
            for b in range(NB):
                o = AP(self.psb[bk], 1024, 0, 128, 128 * b, [(1, 128)])
                i = AP(self.xn, 4096, 0, 128, D * b + 128 * kc, [(1, 128)])
                idn = AP(self.ident, 128, 0, 128, 0, [(1, 128)])
                self.op("pe", lambda e, o=o, i=i, idn=idn: e.transpose(out=o, in_=i, identity=idn),
                        [("xn", b), ("c", "ident")], [("ps", bk)])
            src = AP(self.psb[bk], 1024, 0, 128, 0, [(1, 512)])
            dst = AP(dst_t, 4096, 0, 128, 512 * kc, [(1, 512)])
            gsc = AP(gain, 8, 0, 128, kc, [(1, 1)])
            self.op("dve", lambda e, o=dst, i=src, s=gsc: e.tensor_scalar(out=o, in0=i, scalar1=s, scalar2=None,
                                                                          op0=ALU.mult),
                    [("ps", bk), ("c", gname)], [(dkey, kc)])

    def featT_mm(self, slot, rkeys, col0, bk, act_t, akey, wstride, nk=8, woff=0):
        F = act_t.shape[1]
        for kc in range(nk):
            lhsT = AP(slot, CHE, 0, 128, woff + kc * wstride + col0, [(1, 128)])
            rhs = AP(act_t, F, 0, 128, 512 * kc, [(1, 512)])
            self.mm(self.P(bk), lhsT, rhs, kc == 0, kc == nk - 1, rkeys + [(akey, kc)], [("ps", bk)])

    def stage_front(self, t, mid=None):
        ts = t % self.tps
        if ts > 0:
            o = AP(self.kdup, 1280, 0, 128, 0, [(640, 2), (1, 128)])
            i = AP(self.kdup, 1280, 0, 128, 512, [(640, 2), (1, 128)])
            self.op("pool", lambda e, o=o, i=i: e.tensor_copy(out=o, in_=i), [("k", 4)], [("k", 0)])
            o = AP(self.vtok, 650, 0, 128, 0, [(1, 130)])
            i = AP(self.vtok, 650, 0, 128, 520, [(1, 130)])
            self.op("pool", lambda e, o=o, i=i: e.tensor_copy(out=o, in_=i), [("v", 4)], [("v", 0)])
        slot, rk = self.next_chunk(C_Q)
        for i in range(4):
            bk = self.nbank()
            self.featT_mm(slot, rk, 128 * i, bk, self.xnT, "xnT", 512)
            dst = AP(self.Q2, 2048, 0, 128, 512 * i, [(1, 512)])
            if i % 2 == 0:
                self.op("act", lambda e, o=dst, s=self.P(bk): e.activation(out=o, in_=s, func=AF.Copy),
                        [("ps", bk)], [("q", i)])
            else:
                self.op("dve", lambda e, o=dst, s=self.P(bk): e.tensor_copy(out=o, in_=s), [("ps", bk)],
                        [("q", i)])
        self.chunk_done()
        slot, rk = self.next_chunk(C_KV)
        for g in range(2):
            bk = self.nbank()
            self.featT_mm(slot, rk, 128 * g, bk, self.xnT, "xnT", 384)
            dst = AP(self.kdup, 1280, 0, 128, 640 * g + 128, [(1, 512)])
            self.op("dve", lambda e, o=dst, s=self.P(bk): e.tensor_copy(out=o, in_=s), [("ps", bk)],
                    [("k", 1), ("k", 2), ("k", 3), ("k", 4)])
        bk = self.nbank()
        for b in range(NB):
            for kc in range(8):
                lhsT = AP(self.xnT, 4096, 0, 128, 512 * kc + 128 * b, [(1, 128)])
                rhs = AP(slot, CHE, 0, 128, kc * 384 + 256, [(1, 128)])
                self.mm(self.P(bk, 128, 128 * b), lhsT, rhs, kc == 0, kc == 7, rk + [("xnT", kc)], [("ps", bk)])
        dst = AP(self.vtok, 650, 0, 128, 130, [(130, 4), (65, 2), (1, 64)])
        src = AP(self.ps[bk], 512, 0, 128, 0, [(128, 4), (64, 2), (1, 64)])
        self.op("act", lambda e, o=dst, s=src: e.activation(out=o, in_=s, func=AF.Copy), [("ps", bk)],
                [("v", 1), ("v", 2), ("v", 3), ("v", 4)])
        self.chunk_done()
        slot, rk = self.next_chunk(C_U)
        for i in range(4):
            bk = self.nbank()
            self.featT_mm(slot, rk, 128 * i, bk, self.xnT, "xnT", 512)
            dst = AP(self.uT, 2048, 0, 128, 512 * i, [(1, 512)])
            self.op("act", lambda e, o=dst, s=self.P(bk): e.activation(out=o, in_=s, func=AF.Gelu), [("ps", bk)],
                    [("u", i)])
        self.chunk_done()
        if mid is not None:
            mid()
        slot, rk = self.next_chunk(C_VG)
        for b in range(NB):
            bk = self.nbank()
            for kc in range(8):
                lhsT = AP(self.xnT, 4096, 0, 128, 512 * kc + 128 * b, [(1, 128)])
                rhs = AP(slot, CHE, 0, 128, kc * 512, [(1, 512)])
                self.mm(self.P(bk), lhsT, rhs, kc == 0, kc == 7, rk + [("xnT", kc)], [("ps", bk)])
            tv = self.ntmp()
            vg = self.T(tv)
            self.op("act", lambda e, o=vg, s=self.P(bk): e.activation(out=o, in_=s, func=AF.Gelu), [("ps", bk)],
                    [("tmp", tv)])
            bst = AP(self.bnst, 24, 0, 128, 6 * b, [(1, 6)])
            self.op("dve", lambda e, o=bst, i=vg: e.bn_stats(out=o, in_=i), [("tmp", tv)], [("bn", b)])
            mv = AP(self.st, 64, 0, 128, 32 + 2 * b, [(1, 2)])
            self.op("dve", lambda e, o=mv, i=bst: e.bn_aggr(out=o, in_=i), [("bn", b)], [("mv", b)])
            var = AP(self.st, 64, 0, 128, 32 + 2 * b + 1, [(1, 1)])
            mean = AP(self.st, 64, 0, 128, 32 + 2 * b, [(1, 1)])
            rs = AP(self.st, 64, 0, 128, 48 + b, [(1, 1)])
            self.op("dve", lambda e, o=rs, i=var: e.tensor_scalar(out=o, in0=i, scalar1=LN_EPS, scalar2=None,
                                                                  op0=ALU.add),
                    [("mv", b)], [("lnr", b)])
            nh = AP(self.neghalf, 8, 0, 128, 0, [(1, 1)])
            self.op("pool", lambda e, o=rs, i=rs, h=nh: e.tensor_tensor(out=o, in0=i, in1=h, op=ALU.pow),
                    [("lnr", b), ("c", "neghalf")], [("lnr", b)])
            dst = AP(self.vln, 2048, 0, 128, 512 * b, [(1, 512)])
            self.op("dve", lambda e, o=dst, i=vg, m=mean, r=rs: e.tensor_scalar(out=o, in0=i, scalar1=m, scalar2=r,
                                                                                 op0=ALU.subtract, op1=ALU.mult),
                    [("tmp", tv), ("mv", b), ("lnr", b)], [("vln", b)])
        self.chunk_done()

    def stage_attn(self, t):
        ts = t % self.tps
        units = [(b, g) for b in range(NB) for g in range(2)]
        pend = None
        idn = AP(self.ident, 128, 0, 128, 0, [(1, 128)])
        ones = AP(self.ones, 128, 0, 128, 0, [(1, 128)])

        def pv(u):
            b, g, parts, pti, ui = u
            bo = self.nbank()
            for hh in range(4):
                for idx, p in enumerate(parts):
                    lhsT = AP(self.PT, 2048, 0, 128, 1024 * pti + 512 * (hh // 2) + 256 * p + 128 * (hh % 2),
                              [(1, 128)])
                    rhs = AP(self.vtok, 650, 0, 128, 130 * (b + p) + 65 * g, [(1, 65)])
                    self.mm(self.P(bo, 65, 65 * hh), lhsT, rhs, idx == 0, idx == len(parts) - 1,
                            [("v", b + p), ("PT", pti)], [("ps", bo)])
            ro = 56 + 4 * (ui % 2)
            r = AP(self.st, 64, 0, 128, ro, [(1, 4)])
            den = AP(self.ps[bo], 512, 0, 128, 64, [(65, 4)])
            es_ = AP(self.esink, 8, 0, 128, 4 * g, [(1, 4)])
            self.op("dve", lambda e, o=r, a=den, c=es_: e.tensor_tensor(out=o, in0=a, in1=c, op=ALU.add),
                    [("ps", bo), ("c", "esink")], [("st", ro)])
            self.op("dve", lambda e, o=r: e.reciprocal(out=o, in_=o), [("st", ro)], [("st", ro)])
            for hh in range(4):
                dst = AP(self.atok, 2048, 0, 128, 512 * b + 64 * (4 * g + hh), [(1, 64)])
                rs = AP(self.st, 64, 0, 128, ro + hh, [(1, 1)])
                if hh % 2 == 0:
                    self.op("act", lambda e, o=dst, i=self.P(bo, 64, 65 * hh), s_=rs:
                            e.activation(out=o, in_=i, func=AF.Copy, scale=s_),
                            [("ps", bo), ("st", ro)], [("atok", b, g)])
                else:
                    self.op("dve", lambda e, o=dst, i=self.P(bo, 64, 65 * hh), s_=rs:
                            e.tensor_scalar(out=o, in0=i, scalar1=s_, scalar2=None, op0=ALU.mult),
                            [("ps", bo), ("st", ro)], [("atok", b, g)])
            return b if g == 1 else None

        def tr_block(b):
            bt = self.nbank()
            for c4 in range(4):
                o = AP(self.psb[bt], 1024, 0, 128, 128 * c4, [(1, 128)])
                i = AP(self.atok, 2048, 0, 128, 512 * b + 128 * c4, [(1, 128)])
                self.op("pe", lambda e, o=o, i=i: e.transpose(out=o, in_=i, identity=idn),
                        [("atok", b, 0), ("atok", b, 1), ("c", "ident")], [("ps", bt)])
            src = AP(self.psb[bt], 1024, 0, 128, 0, [(128, 4), (1, 128)])
            dst = AP(self.attnT, 2048, 0, 128, 128 * b, [(512, 4), (1, 128)])
            self.op("dve", lambda e, o=dst, s_=src: e.tensor_copy(out=o, in_=s_), [("ps", bt)],
                    [("attn", c4) for c4 in range(4)])

        trp = None
        for ui, (b, g) in enumerate(units):
            parts = [1] if (ts == 0 and b == 0) else [0, 1]
            pti = ui % 2
            banks = (self.nbank(), self.nbank())
            c0 = 256 * parts[0]
            n = 256 * len(parts)
            for half in range(2):
                tb = AP(self.tbl, 2048, 0, 128, 1024 * g + 512 * half + c0, [(1, n)])
                self.mm(self.P(banks[half], n, c0), idn, tb, True, False, [("c", "ident"), ("c", "tbl")],
                        [("ps", banks[half])])
            for pi, p in enumerate(parts):
                kslot = b + p
                for half in range(2):
                    lhsT = AP(self.kdup, 1280, 64 * half, 64, 640 * g + 128 * kslot, [(1, 128)])
                    rhs = AP(self.Q2, 2048, 64 * half, 64, 512 * (2 * g) + 128 * b, [(512, 2), (1, 128)])
                    self.mm(self.P(banks[half], 256, 256 * p), lhsT, rhs, False, pi == len(parts) - 1,
                            [("k", kslot), ("q", 2 * g), ("q", 2 * g + 1)], [("ps", banks[half])])
            for half in range(2):
                dst = AP(self.PT, 2048, 0, 128, 1024 * pti + 512 * half + c0, [(1, n)])
                self.op("act", lambda e, o=dst, s=self.P(banks[half], n, c0):
                        e.activation(out=o, in_=s, func=AF.Exp, scale=0.125),
                        [("ps", banks[half])], [("PT", pti)])
            done = pv(pend) if pend is not None else None
            if trp is not None:
                tr_block(trp)
            trp = done
            pend = (b, g, parts, pti, ui)
            yield
        done = pv(pend)
        if trp is not None:
            tr_block(trp)
        yield
        tr_block(done)
        yield

    def stage_spatial(self, t):
        for b in range(NB):
            bk = self.nbank()
            for gq in range(4):
                lhsT = AP(self.vln, 2048, 0, 128, 512 * b + 128 * gq, [(1, 128)])
                rhs = AP(self.wsT, 512, 0, 128, 128 * gq, [(1, 128)])
                self.mm(self.P(bk, 128, 128 * gq), lhsT, rhs, True, True, [("vln", b), ("c", "wsT")], [("ps", bk)])
            tq = self.ntmp()
            for gq in range(4):
                self.op("dve", lambda e, o=self.T(tq, 128, off=128 * gq), i=self.P(bk, 128, 128 * gq),
                        s=AP(self.lngpp, 4, 0, 128, gq, [(1, 1)]), c=AP(self.B2, 512, 0, 128, 128 * gq, [(1, 128)]):
                        e.scalar_tensor_tensor(out=o, in0=i, scalar=s, in1=c, op0=ALU.mult, op1=ALU.add),
                        [("ps", bk), ("c", "lngpp"), ("c", "B2")], [("tmp", tq)])
            t4 = AP(self.tmp[tq], 512, 0, 128, 0, [(128, 4), (1, 128)])
            u4 = AP(self.uT, 2048, 0, 128, 128 * b, [(512, 4), (1, 128)])
            dst = AP(self.gmT, 2048, 0, 128, 128 * b, [(512, 4), (1, 128)])
            self.op("dve", lambda e, o=dst, a=t4, c=u4: e.tensor_tensor(out=o, in0=a, in1=c, op=ALU.mult),
                    [("tmp", tq)] + [("u", i) for i in range(4)], [("gm", i) for i in range(4)])

    def stage_merge(self, t, inter=None, per_chunk=2):
        for c in range(8):
            slot, rk = self.next_chunk(C_M0 + c)
            bga, bgb, bra, brb = self.nbank(), self.nbank(), self.nbank(), self.nbank()
            self.featT_mm(slot, rk, 0, bga, self.xnT, "xnT", 256)
            self.featT_mm(slot, rk, 128, bgb, self.xnT, "xnT", 256)
            self.featT_mm(slot, rk, 0, bra, self.attnT, "attn", 128, nk=4, woff=2048)
            self.featT_mm(slot, rk, 0, brb, self.gmT, "gm", 128, nk=4, woff=2560)
            self.chunk_done()
            ta, tb_, t1, t2 = self.ntmp(), self.ntmp(), self.ntmp(), self.ntmp()
            self.op("act", lambda e, o=self.T(ta), s=self.P(bga): e.activation(out=o, in_=s, func=AF.Sigmoid),
                    [("ps", bga)], [("tmp", ta)])
            self.op("act", lambda e, o=self.T(tb_), s=self.P(bgb): e.activation(out=o, in_=s, func=AF.Sigmoid),
                    [("ps", bgb)], [("tmp", tb_)])
            self.op("dve", lambda e, o=self.T(t1), a=self.P(bra), c=self.T(ta):
                    e.tensor_tensor(out=o, in0=a, in1=c, op=ALU.mult), [("ps", bra), ("tmp", ta)], [("tmp", t1)])
            self.op("dve", lambda e, o=self.T(t2), a=self.P(brb), c=self.T(tb_):
                    e.tensor_tensor(out=o, in0=a, in1=c, op=ALU.mult), [("ps", brb), ("tmp", tb_)], [("tmp", t2)])
            dst = AP(self.mergedT, 4096, 0, 128, 512 * c, [(1, 512)])
            self.op("pool", lambda e, o=dst, a=self.T(t1), c_=self.T(t2):
                    e.tensor_tensor(out=o, in0=a, in1=c_, op=ALU.add), [("tmp", t1), ("tmp", t2)], [("mg", c)])
            if inter is not None and c >= 1:
                for _ in range(per_chunk):
                    next(inter, None)
        if inter is not None:
            for _ in inter:
                pass

    def tokmajor_out(self, xi, act_t, akey, chunk_groups, gbc, gname, stbase, inter=None, after_pair=None):
        xb = self.xb[xi]
        F = act_t.shape[1]
        ktot = sum(nk for _, nk in chunk_groups[0])
        resident = all(len(g) == 1 for g in chunk_groups)
        held = None
        for pair in range(2):
            blocks = (2 * pair, 2 * pair + 1)
            banks = {}
            for hf in range(2):
                for b in blocks:
                    banks[(hf, b)] = self.nbank()
            self.held |= set(banks.values())
            if resident:
                if pair == 0:
                    held = [self.next_chunk(chunk_groups[hf][0][0]) for hf in range(2)]
            for hf in range(2):
                k0 = 0
                for gi, (cid, nk) in enumerate(chunk_groups[hf]):
                    slot, rk = held[hf] if resident else self.next_chunk(cid)
                    for b in blocks:
                        for kk in range(nk):
                            kg = k0 + kk
                            lhsT = AP(act_t, F, 0, 128, 512 * kg + 128 * b, [(1, 128)])
                            rhs = AP(slot, CHE, 0, 128, 512 * kk, [(1, 512)])
                            self.mm(self.P(banks[(hf, b)]), lhsT, rhs, kg == 0, kg == ktot - 1,
                                    rk + [(akey, kg)], [("ps", banks[(hf, b)])])
                    k0 += nk
                    if not resident:
                        self.chunk_done()
                    if inter is not None:
                        next(inter, None)
            if resident and pair == 1:
                self.chunk_done()
            self.held -= set(banks.values())
            self.soft = set(banks.values())
            self.soft_n = 4
            tys = {}
            for b in blocks:
                for hf in range(2):
                    bk = banks[(hf, b)]
                    tj = self.ntmp()
                    acc = AP(self.st, 64, 0, 128, stbase + 2 * b + hf, [(1, 1)])
                    self.op("act", lambda e, o=self.T(tj), i=self.P(bk), a=acc:
                            e.activation(out=o, in_=i, func=AF.Square, accum_out=a),
                            [("ps", bk)], [("tmp", tj), ("st", stbase + 2 * b + hf)])
                    ty = self.ntmp()
                    tys[(hf, b)] = ty
                    g = AP(gbc, 1024, 0, 128, 512 * hf, [(1, 512)])
                    self.op("dve", lambda e, o=self.T(ty), i=self.P(bk), g=g:
                            e.tensor_tensor(out=o, in0=i, in1=g, op=ALU.mult),
                            [("ps", bk), ("c", gname)], [("tmp", ty)])
            b0 = blocks[0]
            sse = AP(self.st, 64, 0, 128, stbase + 2 * b0, [(2, 2)])
            sso = AP(self.st, 64, 0, 128, stbase + 2 * b0 + 1, [(2, 2)])
            ms = AP(self.st, 64, 0, 128, stbase + 8 + b0, [(1, 2)])
            mkey = ("st", stbase + 8 + pair)
            self.op("dve", lambda e, o=ms, a=sse, c=sso: e.tensor_tensor(out=o, in0=a, in1=c, op=ALU.add),
                    [("st", stbase + 2 * b + hf) for b in blocks for hf in range(2)], [mkey])
            self.op("dve", lambda e, o=ms: e.tensor_scalar(out=o, in0=o, scalar1=1.0 / D, scalar2=EPS,
                                                           op0=ALU.mult, op1=ALU.add), [mkey], [mkey])
            nh = AP(self.neghalf, 8, 0, 128, 0, [(1, 2)])
            self.op("pool", lambda e, o=ms, h=nh: e.tensor_tensor(out=o, in0=o, in1=h, op=ALU.pow),
                    [mkey, ("c", "neghalf")], [mkey])
            for b in blocks:
                for hf in range(2):
                    ty = tys[(hf, b)]
                    rs = AP(self.st, 64, 0, 128, stbase + 8 + b, [(1, 1)])
                    xh = AP(xb, 4096, 0, 128, D * b + 512 * hf, [(1, 512)])
                    self.op("dve", lambda e, o=xh, i=self.T(ty), s=rs, x_=xh:
                            e.scalar_tensor_tensor(out=o, in0=i, scalar=s, in1=x_, op0=ALU.mult, op1=ALU.add),
                            [("tmp", ty), mkey, ("xb", xi, b)], [("xb", xi, b)])
            if after_pair is not None:
                after_pair(pair)
        if inter is not None:
            for _ in inter:
                pass

    def stage_ffn_gu(self, t, mid=None, mid_at=2):
        for f in range(11):
            if mid is not None and f == mid_at:
                mid()
            slot, rk = self.next_chunk(C_G0 + f)
            for j in range(2):
                fc = 2 * f + j
                bg = self.nbank()
                bu = self.nbank()
                self.featT_mm(slot, rk, 128 * j, bg, self.hnT, "hnT", 512)
                self.featT_mm(slot, rk, 256 + 128 * j, bu, self.hnT, "hnT", 512)
                tg = self.ntmp()
                self.op("act", lambda e, o=self.T(tg), s=self.P(bg): e.activation(out=o, in_=s, func=AF.Silu),
                        [("ps", bg)], [("tmp", tg)])
                dst = AP(self.hT, NFC * 512, 0, 128, 512 * fc, [(1, 512)])
                self.op("dve", lambda e, o=dst, a=self.P(bu), c=self.T(tg):
                        e.tensor_tensor(out=o, in0=a, in1=c, op=ALU.mult),
                        [("ps", bu), ("tmp", tg)], [("h", fc)])
            self.chunk_done()

    def store_y(self, t, pair):
        xi = t % NXB
        xb = self.xb[xi]
        dst = bass.AP(self.y_d, (t * TT + 256 * pair) * D, [[D, 128], [128 * D, 2], [1, D]])
        src = AP(xb, 4096, 0, 128, 2 * D * pair, [(D, 2), (1, D)])
        self.dma(dst, src, [("xb", xi, 2 * pair), ("xb", xi, 2 * pair + 1)], [("y", t, pair)], ("Y", xi, pair))

    def record(self):
        n = self.n_tiles
        self.stream_init()
        self.load_x(0)
        self.load_consts(0)
        for _ in range(NSLOT - 1):
            self.record_load()
        self.rms_scale_all(0, 0)
        self.transposes(self.xnT, "xnT", self.g1, "g1")
        self.load_consts(1)
        gen0 = self.stage_attn(0)
        self.stage_front(0, mid=lambda: [next(gen0, None) for _ in range(4)])
        for t in range(1, min(n, NXB)):
            self.load_x(t)
        wo_groups = [[(C_O0, 8)], [(C_O0 + 1, 8)]]
        dn_groups = [[(C_D0, 8), (C_D0 + 1, 8), (C_D0 + 2, 6)], [(C_D0 + 3, 8), (C_D0 + 4, 8), (C_D0 + 5, 6)]]
        for _ in gen0:
            pass
        self.stage_spatial(0)
        for t in range(n):
            xi = t % NXB
            nxt = t + 1 < n
            self.stage_merge(t, inter=self.rms_scale((t + 1) % NXB, 0) if nxt else None)
            if t >= 1 and t - 1 + NXB < n:
                self.load_x(t - 1 + NXB)
            if nxt:
                self.transposes(self.xnT, "xnT", self.g1, "g1")
            self.tokmajor_out(xi, self.mergedT, "mg", wo_groups, self.g2bc, "g2bc", 16,
                              after_pair=lambda pair, xi=xi: self.rms_scale_all(xi, 1, (2 * pair, 2 * pair + 1)))
            hT = lambda: self.transposes(self.hnT, "hnT", self.g3, "g3")
            if nxt:
                self.stage_front(t + 1, mid=hT)
                self.stage_ffn_gu(t, mid=lambda t=t: self.stage_spatial(t + 1))
            else:
                hT()
                self.stage_ffn_gu(t)
            inter = self.stage_attn(t + 1) if nxt else None
            self.tokmajor_out(xi, self.hT, "h", dn_groups, self.g4bc, "g4bc", 16, inter=inter,
                              after_pair=lambda pair, t=t: self.store_y(t, pair))
        self.op("sp", None, [("y", t, p) for t in range(n) for p in range(2)], [])
        self.S.finalize()

    def emit(self):
        nc = self.nc
        S = self.S
        for key in S.dma_count:
            self.sem_dma[key] = self.es.enter_context(nc.semaphore("d_" + "_".join(str(k) for k in key)))

        def semof(key):
            return self.sem_eng[key[1]] if key[0] == "eng" else self.sem_dma[key[1]]

        def run(eng_name):
            def body(e):
                for op in S.ops[eng_name]:
                    for key, val in op.waits:
                        e.wait_ge(semof(key), val)
                    if op.fn is None:
                        continue
                    ins = op.fn(e)
                    if op.dma_key is not None:
                        ins.then_inc(self.sem_dma[op.dma_key], 16)
                    elif op.marked:
                        ins.then_inc(self.sem_eng[eng_name], 1)
            return body

        with nc.Block() as block:
            block.sync(run("sp"))
            block.tensor(run("pe"))
            block.scalar(run("act"))
            block.vector(run("dve"))
            block.gpsimd(run("pool"))


def build(n_tiles, tiles_per_seq):
    b = Builder(n_tiles, tiles_per_seq)
    es = ExitStack()
    with es:
        b.alloc(es)
        b.record()
        b.emit()
    return b.nc


def prepare_weights(inp):
    wsrc = build_wsrc(inp["w_in"][0], inp["w_attn_branch"][0], inp["w_gmlp_branch"][0], inp["w_out"][0],
                      inp["w_ffn_gate"][0], inp["w_ffn_up"][0], inp["w_ffn_down"][0])
    consts = build_consts(inp["norm_mix_pre"][0], inp["norm_mix_post"][0], inp["norm_ffn_pre"][0],
                          inp["norm_ffn_post"][0], inp["attn_sinks"][0], inp["gmlp_ln_g"][0],
                          inp["gmlp_ln_b"][0], inp["gmlp_w_s"][0], inp["gmlp_b_s"][0])
    m = {"wsrc": wsrc.reshape(NCH * 128, CHE)}
    for k, v in consts.items():
        m["c_" + k] = v
    return m


def kernel(**inputs):
    inp = {k: np.asarray(v, dtype=np.float32) for k, v in inputs.items()}
    x = inp["x"]
    B, S, _ = x.shape
    tps = S // TT
    seq_per_core = B // N_CORES
    n_tiles = seq_per_core * tps
    shared = prepare_weights(inp)
    xs = np.ascontiguousarray(x).reshape(N_CORES, seq_per_core * S, D)
    nc = build(n_tiles, tps)
    in_maps = []
    for c in range(N_CORES):
        m = dict(shared)
        m["x"] = xs[c]
        in_maps.append(m)
    res = run_bass_kernel_spmd(nc, in_maps, core_ids=list(range(N_CORES)))
    out = np.stack([np.asarray(r["y"], dtype=np.float32) for r in res.results], axis=0)
    return out.reshape(B, S, D)
```

```python
import numpy as np
from contextlib import ExitStack
import concourse.bass as bass
import concourse.mybir as mybir
from concourse.bass_utils import run_bass_kernel_spmd

F32 = mybir.dt.float32
BF16 = mybir.dt.bfloat16
AF = mybir.ActivationFunctionType
ALU = mybir.AluOpType

D = 1024
DFF = 2816
NFC = 22
TT = 512
NB = 4
EPS = 1e-6
LN_EPS = 1e-5
NCH = 31
CHE = 4096
NSLOT = 4
NTMP = 10
N_CORES = 8
CAST_DMA = True
NXB = 3

ENGS = ("pe", "act", "dve", "pool", "sp")


class Op:
    __slots__ = ("eng", "fn", "deps", "marked", "mark", "dma_key", "dma_val", "waits")

    def __init__(self, eng, fn):
        self.eng = eng
        self.fn = fn
        self.deps = []
        self.marked = False
        self.mark = 0
        self.dma_key = None
        self.dma_val = 0
        self.waits = []


class Sched:
    def __init__(self):
        self.ops = {e: [] for e in ENGS}
        self.state = {}
        self.dma_count = {}
        self.bulk = {}

    def add(self, eng, fn, reads=(), writes=(), dma=None, bulk=False):
        op = Op(eng, fn)
        deps = {}
        tmpkey = False

        def dep(o, kind):
            nonlocal tmpkey
            if o is op:
                return
            if o.dma_key is None and o.eng == eng and kind != "raw" and not (kind == "waw" and tmpkey):
                return
            k = id(o)
            if k not in deps:
                deps[k] = o

        for k in reads:
            st = self.state.get(k)
            if st is None:
                continue
            if st[0] is not None:
                dep(st[0], "raw")
            if isinstance(k, tuple) and k[0] == "ps":
                for r in st[1].values():
                    if r.eng != eng:
                        dep(r, "rar")
        for k in writes:
            st = self.state.get(k)
            if st is None:
                continue
            tmpkey = isinstance(k, tuple) and k[0] == "tmp" and eng != "pe"
            if st[0] is not None:
                dep(st[0], "waw")
            tmpkey = False
            for r in st[1].values():
                dep(r, "war")
        op.deps = list(deps.values())
        for o in op.deps:
            o.marked = True
        if dma is not None:
            op.dma_key = dma
            c = self.dma_count.get(dma, 0) + 1
            self.dma_count[dma] = c
            op.dma_val = 16 * c
            if bulk:
                self.bulk.setdefault(dma, []).append(op)
        for k in writes:
            self.state[k] = [op, {}]
        for k in reads:
            st = self.state.get(k)
            if st is None:
                st = [None, {}]
                self.state[k] = st
            st[1][eng] = op
        self.ops[eng].append(op)
        return op

    def finalize(self):
        for key, ops in self.bulk.items():
            tot = 16 * self.dma_count[key]
            for o in ops:
                o.dma_val = tot
        for e in ENGS:
            n = 0
            for op in self.ops[e]:
                if op.dma_key is None and op.marked:
                    n += 1
                    op.mark = n
        for e in ENGS:
            waited = {}
            for op in self.ops[e]:
                w = {}
                for d in op.deps:
                    if d.dma_key is not None:
                        key, val = ("dma", d.dma_key), d.dma_val
                    else:
                        key, val = ("eng", d.eng), d.mark
                    if val > w.get(key, 0):
                        w[key] = val
                op.waits = []
                for key, val in w.items():
                    if waited.get(key, 0) >= val:
                        continue
                    waited[key] = val
                    op.waits.append((key, val))


def AP(t, F, p0, npart, off, dims):
    return bass.AP(t, p0 * F + off, [[F, npart]] + [[s, c] for (s, c) in dims])


def _kc_layout(w):
    C = w.shape[1]
    return np.ascontiguousarray(w.reshape(8, 128, C).transpose(1, 0, 2)).reshape(128, 8 * C)


def _pad(a):
    out = np.zeros((128, CHE), np.float32)
    out[: a.shape[0], : a.shape[1]] = a
    return out


def _q_perm():
    cols = []
    for i in range(4):
        g, j = i // 2, i % 2
        lo, hi = 4 * g + j, 4 * g + 2 + j
        cols += list(range(64 * lo, 64 * lo + 64)) + list(range(64 * hi, 64 * hi + 64))
    return np.array(cols)


C_Q, C_KV, C_U, C_VG = 0, 1, 2, 3
C_M0 = 4
C_O0 = 12
C_G0 = 14
C_D0 = 25


def build_wsrc(w_in, w_a, w_g, w_out, w_gate, w_up, w_down):
    chunks = []
    wq = w_in[:, 0:512][:, _q_perm()]
    chunks.append(_pad(_kc_layout(wq)))
    wk = w_in[:, 512:640]
    wv = w_in[:, 640:768]
    kv = np.concatenate([wk[:, 0:64], wk[:, 0:64], wk[:, 64:128], wk[:, 64:128], wv], axis=1)
    chunks.append(_pad(_kc_layout(kv)))
    chunks.append(_pad(_kc_layout(w_in[:, 768:1280])))
    chunks.append(_pad(_kc_layout(w_in[:, 1280:1792])))
    for c in range(8):
        ga = w_in[:, 1792 + 128 * c: 1792 + 128 * (c + 1)]
        gb = w_in[:, 2816 + 128 * c: 2816 + 128 * (c + 1)]
        gab = _kc_layout(np.concatenate([ga, gb], axis=1))
        wa = w_a[:, 128 * c:128 * (c + 1)].reshape(4, 128, 128).transpose(1, 0, 2).reshape(128, 512)
        wg = w_g[:, 128 * c:128 * (c + 1)].reshape(4, 128, 128).transpose(1, 0, 2).reshape(128, 512)
        chunks.append(_pad(np.concatenate([gab, wa, wg], axis=1)))
    for hf in range(2):
        chunks.append(_pad(_kc_layout(w_out[:, 512 * hf:512 * (hf + 1)])))
    for f in range(11):
        gu = np.concatenate([w_gate[:, 256 * f:256 * (f + 1)], w_up[:, 256 * f:256 * (f + 1)]], axis=1)
        chunks.append(_pad(_kc_layout(gu)))
    for hf in range(2):
        for (f0, f1) in ((0, 8), (8, 16), (16, 22)):
            blk = w_down[128 * f0:128 * f1, 512 * hf:512 * (hf + 1)]
            n = f1 - f0
            chunks.append(_pad(blk.reshape(n, 128, 512).transpose(1, 0, 2).reshape(128, n * 512)))
    assert len(chunks) == NCH
    return np.stack(chunks, axis=0)


def build_consts(norm_mix_pre, norm_mix_post, norm_ffn_pre, norm_ffn_post, attn_sinks,
                 ln_g, ln_b, w_s, b_s):
    c = {}
    c["ident"] = np.eye(128, dtype=np.float32)
    tbl = np.zeros((128, 2, 2, 2, 2, 128), np.float32)
    j = np.arange(128)[:, None]
    a = np.arange(128)[None, :]
    for g in range(2):
        for half in range(2):
            for hh2 in range(2):
                h = 4 * g + 2 * half + hh2
                slope = 2.0 ** (-(h + 1))
                rel0 = 128 + a - j
                rel1 = a - j
                tbl[:, g, half, 0, hh2, :] = np.where(a < j, -slope * rel0 * 8.0, -30000.0)
                tbl[:, g, half, 1, hh2, :] = np.where(a >= j, -slope * rel1 * 8.0, -30000.0)
    c["tbl"] = tbl.reshape(128, 2048)
    c["g1"] = np.ascontiguousarray(norm_mix_pre.reshape(8, 128).T)
    c["g3"] = np.ascontiguousarray(norm_ffn_pre.reshape(8, 128).T)
    c["g2bc"] = np.ascontiguousarray(np.broadcast_to(norm_mix_post.reshape(1, D), (128, D)))
    c["g4bc"] = np.ascontiguousarray(np.broadcast_to(norm_ffn_post.reshape(1, D), (128, D)))
    c["lngpp"] = np.ascontiguousarray(ln_g.reshape(4, 128).T)
    c["lnbpp"] = np.ascontiguousarray(ln_b.reshape(4, 128).T)
    c["bsbc"] = np.ascontiguousarray(np.broadcast_to(b_s.reshape(1, 512), (128, 512)))
    c["sinkbc"] = np.ascontiguousarray(np.broadcast_to(attn_sinks.reshape(1, 8), (128, 8)))
    c["wsT"] = np.ascontiguousarray(w_s.transpose(2, 0, 1)).reshape(128, 512)
    s_ = np.arange(128)[:, None]
    t_ = np.arange(128)[None, :]
    cm = (t_ >= s_).astype(np.float32)
    c["cmask"] = np.ascontiguousarray(np.broadcast_to(cm[:, None, :], (128, 4, 128))).reshape(128, 512)
    return {k: np.ascontiguousarray(v, dtype=np.float32) for k, v in c.items()}


CONST_SHAPES = {"ident": 128, "tbl": 2048, "g1": 8, "g3": 8, "g2bc": 1024, "g4bc": 1024,
                "lngpp": 4, "lnbpp": 4, "bsbc": 512, "sinkbc": 8, "wsT": 512, "cmask": 512}


class Builder:
    def __init__(self, n_tiles, tiles_per_seq):
        self.n_tiles = n_tiles
        self.tps = tiles_per_seq
        self.S = Sched()
        self.nc = bass.Bass("TRN2", target_bir_lowering=False)
        self.ntok = n_tiles * TT

    def alloc(self, es):
        nc = self.nc
        self.x_d = nc.dram_tensor("x", [self.ntok, D], F32, kind="ExternalInput")
        self.y_d = nc.dram_tensor("y", [self.ntok, D], F32, kind="ExternalOutput")
        self.wsrc_d = nc.dram_tensor("wsrc", [NCH * 128, CHE], F32, kind="ExternalInput")
        self.wbf_d = nc.dram_tensor("wbf", [NCH * 128, CHE], BF16, kind="Internal")
        self.c_d = {k: nc.dram_tensor("c_" + k, [128, n], F32, kind="ExternalInput")
                    for k, n in CONST_SHAPES.items()}

        def sb(name, n, dt):
            return es.enter_context(nc.sbuf_tensor(name, [128, n], dt))

        self.xb = [sb("xb%d" % i, 4096, F32) for i in range(NXB)]
        self.xn = sb("xn", 4096, BF16)
        self.xnT = sb("xnT", 4096, BF16)
        self.hnT = sb("hnT", 4096, BF16)
        self.Q2 = sb("Q2", 2048, BF16)
        self.kdup = sb("kdup", 1280, BF16)
        self.vtok = sb("vtok", 650, BF16)
        self.uT = sb("uT", 2048, BF16)
        self.vln = sb("vln", 2048, BF16)
        self.PT = sb("PT", 2048, BF16)
        self.attnT = sb("attnT", 2048, BF16)
        self.atok = sb("atok", 2048, BF16)
        self.gmT = sb("gmT", 2048, BF16)
        self.mergedT = sb("mergedT", 4096, BF16)
        self.hT = sb("hT", NFC * 512, BF16)
        self.ring = [sb("ring%d" % i, CHE, BF16) for i in range(NSLOT)]
        self.stg = [sb("stg%d" % i, 1024, F32) for i in range(2)] if not CAST_DMA else []
        self.stg_i = 0
        self.tmp = [sb("tmp%d" % i, 512, F32) for i in range(NTMP)]
        self.tmp_i = 0
        self.tbl = sb("tbl", 2048, BF16)
        self.g2bc = sb("g2bc", 1024, F32)
        self.g4bc = sb("g4bc", 1024, F32)
        self.B2 = sb("B2", 512, F32)
        self.esink = sb("esink", 8, F32)
        self.wsT = sb("wsT", 512, BF16)
        self.ident = sb("ident", 128, BF16)
        self.ones = sb("ones", 128, BF16)
        self.g1 = sb("g1", 8, F32)
        self.g3 = sb("g3", 8, F32)
        self.lngpp = sb("lngpp", 4, F32)
        self.lnbpp = sb("lnbpp", 4, F32)
        self.neghalf = sb("neghalf", 8, F32)
        self.st = sb("stats", 64, F32)
        self.bnst = sb("bnst", 4 * 6, F32)
        self.ps = [es.enter_context(nc.psum_tensor("ps%d" % i, [128, 512], F32)) for i in range(8)]
        self.psb = [p.bitcast(BF16) for p in self.ps]
        self.ps_i = 0
        self.held = set()
        self.soft = set()
        self.soft_n = 0
        self.sem_eng = {e: es.enter_context(nc.semaphore("s_" + e)) for e in ("pe", "act", "dve", "pool")}
        self.sem_dma = {}
        self.es = es

    def nbank(self):
        pick = None
        for k in range(8):
            i = (self.ps_i + k) % 8
            if i in self.held:
                continue
            if self.soft_n > 0 and i in self.soft:
                continue
            pick = i
            break
        if pick is None:
            for k in range(8):
                i = (self.ps_i + k) % 8
                if i not in self.held:
                    pick = i
                    break
        self.ps_i = (pick + 1) % 8
        if self.soft_n > 0:
            self.soft_n -= 1
        return pick

    def ntmp(self):
        i = self.tmp_i
        self.tmp_i = (i + 1) % NTMP
        return i

    def T(self, i, n=512, p0=0, npart=128, off=0):
        return AP(self.tmp[i], 512, p0, npart, off, [(1, n)])

    def P(self, bk, n=512, off=0, p0=0, npart=128):
        return AP(self.ps[bk], 512, p0, npart, off, [(1, n)])

    def op(self, eng, fn, reads=(), writes=(), dma=None, bulk=False):
        return self.S.add(eng, fn, reads, writes, dma, bulk)

    def mm(self, out, lhsT, rhs, start, stop, reads, writes):
        self.op("pe", lambda e: e.matmul(out, lhsT, rhs, start=start, stop=stop), reads, writes)

    def dma(self, out, in_, reads, writes, key, bulk=False, q="sp", max_last=None):
        if max_last is None:
            fn = lambda e: e.dma_start(out=out, in_=in_)
        else:
            fn = lambda e: e.dma_start(out=out, in_=in_, max_dma_last_dim=max_last)
        self.op(q, fn, reads, writes, dma=key, bulk=bulk)

    def load_consts(self, part):
        ckey = ("C%d" % part,)
        def cdram(name, off=0, n=None):
            F = CONST_SHAPES[name]
            return AP(self.c_d[name], F, 0, 128, off, [(1, n or F)])

        def direct(name, t):
            n = CONST_SHAPES[name]
            self.dma(AP(t, n, 0, 128, 0, [(1, n)]), cdram(name), [], [("c", name)], ckey, bulk=True)

        def via_tmp(name, off, n):
            ti = self.ntmp()
            self.dma(self.T(ti, n), cdram(name, off, n), [], [("tmp", ti)], ckey, bulk=True)
            return ti

        if part == 0:
            for name, t in (("g1", self.g1), ("g3", self.g3), ("lngpp", self.lngpp), ("lnbpp", self.lnbpp)):
                direct(name, t)
        else:
            for name, t in (("g2bc", self.g2bc), ("g4bc", self.g4bc)):
                direct(name, t)
        if part == 0:
            self.op("dve", lambda e, o=AP(self.ones, 128, 0, 128, 0, [(1, 128)]): e.memset(o, 1.0), [], [("c", "ones")])
            self.op("dve", lambda e, o=AP(self.neghalf, 8, 0, 128, 0, [(1, 8)]): e.memset(o, -0.5), [],
                    [("c", "neghalf")])
            ti = via_tmp("ident", 0, 128)
            self.op("dve", lambda e, o=AP(self.ident, 128, 0, 128, 0, [(1, 128)]), s=self.T(ti, 128):
                    e.tensor_copy(out=o, in_=s), [("tmp", ti)], [("c", "ident")])
        else:
            for q4 in range(4):
                ti = via_tmp("tbl", 512 * q4, 512)
                self.op("dve", lambda e, o=AP(self.tbl, 2048, 0, 128, 512 * q4, [(1, 512)]), s=self.T(ti):
                        e.tensor_copy(out=o, in_=s), [("tmp", ti)], [("c", "tbl")])
            ti = via_tmp("sinkbc", 0, 8)
            self.op("act", lambda e, o=AP(self.esink, 8, 0, 128, 0, [(1, 8)]), s=self.T(ti, 8):
                    e.activation(out=o, in_=s, func=AF.Exp), [("tmp", ti)], [("c", "esink")])
            self.op("dve", lambda e, o=AP(self.vtok, 650, 0, 128, 0, [(1, 650)]): e.memset(o, 1.0), [],
                    [("v", i) for i in range(5)])
            t3 = via_tmp("wsT", 0, 512)
            t4 = via_tmp("cmask", 0, 512)
            self.op("dve", lambda e, o=AP(self.wsT, 512, 0, 128, 0, [(1, 512)]), a=self.T(t3), b=self.T(t4):
                    e.tensor_tensor(out=o, in0=a, in1=b, op=ALU.mult), [("tmp", t3), ("tmp", t4)], [("c", "wsT")])
            t5 = via_tmp("bsbc", 0, 512)
            bk = self.nbank()
            self.mm(self.P(bk), AP(self.ones, 128, 0, 128, 0, [(1, 128)]), AP(self.wsT, 512, 0, 128, 0, [(1, 512)]),
                    True, True, [("c", "ones"), ("c", "wsT")], [("ps", bk)])
            for g in range(4):
                self.op("dve", lambda e, o=AP(self.B2, 512, 0, 128, 128 * g, [(1, 128)]), i=self.P(bk, 128, 128 * g),
                        s=AP(self.lnbpp, 4, 0, 128, g, [(1, 1)]), b=self.T(t5, 128, off=128 * g):
                        e.scalar_tensor_tensor(out=o, in0=i, scalar=s, in1=b, op0=ALU.mult, op1=ALU.add),
                        [("ps", bk), ("c", "lnbpp"), ("tmp", t5)], [("c", "B2")])

    def stream_init(self):
        order = [C_Q, C_KV, C_U, C_VG]
        for t in range(self.n_tiles):
            order += [C_M0 + i for i in range(8)]
            order += [C_O0, C_O0 + 1]
            if t + 1 < self.n_tiles:
                order += [C_Q, C_KV, C_U, C_VG]
            order += [C_G0 + i for i in range(11)]
            order += [C_D0 + i for i in range(6)]
            order += [C_D0 + i for i in range(6)]
        self.order = order
        self.cast_done = set()
        self.n_loaded = 0
        self.n_used = 0

    def record_load(self):
        n = self.n_loaded
        if n >= len(self.order):
            return
        self.n_loaded += 1
        c = self.order[n]
        s = n % NSLOT
        slot = self.ring[s]
        rkeys = [("ring", s, q4) for q4 in range(4)]
        if c not in self.cast_done:
            self.cast_done.add(c)
            if CAST_DMA:
                self.dma(AP(slot, CHE, 0, 128, 0, [(1, CHE)]), AP(self.wsrc_d, CHE, c * 128, 128, 0, [(1, CHE)]),
                         [], rkeys, ("WC", s), q="pool", max_last=8192)
            for q4 in (range(4) if not CAST_DMA else ()):
                si = self.stg_i
                self.stg_i = 1 - si
                stg = AP(self.stg[si], 1024, 0, 128, 0, [(1, 1024)])
                src = AP(self.wsrc_d, CHE, c * 128, 128, 1024 * q4, [(1, 1024)])
                self.dma(stg, src, [], [("stg", si)], ("ST", si))
                dst = AP(slot, CHE, 0, 128, 1024 * q4, [(1, 1024)])
                eng = ("dve", "act", "pool", "dve")[q4]
                if eng == "act":
                    fn = lambda e, o=dst, i=stg: e.activation(out=o, in_=i, func=AF.Copy)
                else:
                    fn = lambda e, o=dst, i=stg: e.tensor_copy(out=o, in_=i)
                self.op(eng, fn, [("stg", si)], [("ring", s, q4)])
            self.dma(AP(self.wbf_d, CHE, c * 128, 128, 0, [(1, CHE)]), AP(slot, CHE, 0, 128, 0, [(1, CHE)]),
                     rkeys, [("wbf", c)], ("WS", s))
        else:
            self.dma(AP(slot, CHE, 0, 128, 0, [(1, CHE)]), AP(self.wbf_d, CHE, c * 128, 128, 0, [(1, CHE)]),
                     [("wbf", c)], rkeys, ("W", s))

    def next_chunk(self, expect):
        n = self.n_used
        assert self.order[n] == expect, (n, self.order[n], expect)
        self.n_used += 1
        while self.n_loaded <= n:
            self.record_load()
        s = n % NSLOT
        return self.ring[s], [("ring", s, q4) for q4 in range(4)]

    def chunk_done(self):
        while self.n_loaded < min(self.n_used + NSLOT - 1, len(self.order)):
            self.record_load()

    def load_x(self, t):
        xb = self.xb[t % NXB]
        src = bass.AP(self.x_d, t * TT * D, [[D, 128], [128 * D, NB], [1, D]])
        dst = AP(xb, 4096, 0, 128, 0, [(D, NB), (1, D)])
        self.dma(dst, src, [], [("xb", t % NXB, b) for b in range(NB)], ("X", t % NXB))

    def rms_scale(self, xi, which, blocks=(0, 1, 2, 3)):
        xb = self.xb[xi]
        base = 8 * which
        nb_ = len(blocks)
        b0 = blocks[0]
        mkey = ("stm", which, b0)
        for b in blocks:
            src = AP(xb, 4096, 0, 128, D * b, [(1, D)])
            junk = AP(self.xn, 4096, 0, 128, D * b, [(1, D)])
            acc = AP(self.st, 64, 0, 128, base + b, [(1, 1)])
            self.op("act", lambda e, o=junk, i=src, a=acc: e.activation(out=o, in_=i, func=AF.Square, accum_out=a),
                    [("xb", xi, b)], [("xn", b), ("st", base + b)])
            yield
        ss = AP(self.st, 64, 0, 128, base + b0, [(1, nb_)])
        ms = AP(self.st, 64, 0, 128, base + 4 + b0, [(1, nb_)])
        self.op("dve", lambda e, o=ms, i=ss: e.tensor_scalar(out=o, in0=i, scalar1=1.0 / D, scalar2=EPS,
                                                             op0=ALU.mult, op1=ALU.add),
                [("st", base + b) for b in blocks], [mkey])
        nh = AP(self.neghalf, 8, 0, 128, 0, [(1, nb_)])
        self.op("pool", lambda e, o=ms, i=ms, h=nh: e.tensor_tensor(out=o, in0=i, in1=h, op=ALU.pow),
                [mkey, ("c", "neghalf")], [mkey])
        for b in blocks:
            src = AP(xb, 4096, 0, 128, D * b, [(1, D)])
            dst = AP(self.xn, 4096, 0, 128, D * b, [(1, D)])
            sc = AP(self.st, 64, 0, 128, base + 4 + b, [(1, 1)])
            eng = "dve" if b % 2 == 0 else "pool"
            self.op(eng, lambda e, o=dst, i=src, s=sc: e.tensor_scalar(out=o, in0=i, scalar1=s, scalar2=0.0,
                                                                       op0=ALU.mult, op1=ALU.add),
                    [("xb", xi, b), mkey], [("xn", b)])
            yield

    def rms_scale_all(self, xi, which, blocks=(0, 1, 2, 3)):
        for _ in self.rms_scale(xi, which, blocks):
            pass

    def transposes(self, dst_t, dkey, gain, gname):
        for kc in range(8):
            bk = self.nbank()
            for b in range(NB):
                o = AP(self.psb[bk], 1024, 0, 128, 128 * b, [(1, 128)])
                i = AP(self.xn, 4096, 0, 128, D * b + 128 * kc, [(1, 128)])
                idn = AP(self.ident, 128, 0, 128, 0, [(1, 128)])
                self.op("pe", lambda e, o=o, i=i, idn=idn: e.transpose(out=o, in_=i, identity=idn),
                        [("xn", b), ("c", "ident")], [("ps", bk)])
            src = AP(self.psb[bk], 1024, 0, 128, 0, [(1, 512)])
            dst = AP(dst_t, 4096, 0, 128, 512 * kc, [(1, 512)])
            gsc = AP(gain, 8, 0, 128, kc, [(1, 1)])
            self.op("dve", lambda e, o=dst, i=src, s=gsc: e.tensor_scalar(out=o, in0=i, scalar1=s, scalar2=None,
                                                                          op0=ALU.mult),
                    [("ps", bk), ("c", gname)], [(dkey, kc)])

    def featT_mm(self, slot, rkeys, col0, bk, act_t, akey, wstride, nk=8, woff=0):
        F = act_t.shape[1]
        for kc in range(nk):
            lhsT = AP(slot, CHE, 0, 128, woff + kc * wstride + col0, [(1, 128)])
            rhs = AP(act_t, F, 0, 128, 512 * kc, [(1, 512)])
            self.mm(self.P(bk), lhsT, rhs, kc == 0, kc == nk - 1, rkeys + [(akey, kc)], [("ps", bk)])

    def stage_front(self, t, mid=None):
        ts = t % self.tps
        if ts > 0:
            o = AP(self.kdup, 1280, 0, 128, 0, [(640, 2), (1, 128)])
            i = AP(self.kdup, 1280, 0, 128, 512, [(640, 2), (1, 128)])
            self.op("pool", lambda e, o=o, i=i: e.tensor_copy(out=o, in_=i), [("k", 4)], [("k", 0)])
            o = AP(self.vtok, 650, 0, 128, 0, [(1, 130)])
            i = AP(self.vtok, 650, 0, 128, 520, [(1, 130)])
            self.op("pool", lambda e, o=o, i=i: e.tensor_copy(out=o, in_=i), [("v", 4)], [("v", 0)])
        slot, rk = self.next_chunk(C_Q)
        for i in range(4):
            bk = self.nbank()
            self.featT_mm(slot, rk, 128 * i, bk, self.xnT, "xnT", 512)
            dst = AP(self.Q2, 2048, 0, 128, 512 * i, [(1, 512)])
            if i % 2 == 0:
                self.op("act", lambda e, o=dst, s=self.P(bk): e.activation(out=o, in_=s, func=AF.Copy),
                        [("ps", bk)], [("q", i)])
            else:
                self.op("dve", lambda e, o=dst, s=self.P(bk): e.tensor_copy(out=o, in_=s), [("ps", bk)],
                        [("q", i)])
        self.chunk_done()
        slot, rk = self.next_chunk(C_KV)
        for g in range(2):
            bk = self.nbank()
            self.featT_mm(slot, rk, 128 * g, bk, self.xnT, "xnT", 384)
            dst = AP(self.kdup, 1280, 0, 128, 640 * g + 128, [(1, 512)])
            self.op("dve", lambda e, o=dst, s=self.P(bk): e.tensor_copy(out=o, in_=s), [("ps", bk)],
                    [("k", 1), ("k", 2), ("k", 3), ("k", 4)])
        bk = self.nbank()
        for b in range(NB):
            for kc in range(8):
                lhsT = AP(self.xnT, 4096, 0, 128, 512 * kc + 128 * b, [(1, 128)])
                rhs = AP(slot, CHE, 0, 128, kc * 384 + 256, [(1, 128)])
                self.mm(self.P(bk, 128, 128 * b), lhsT, rhs, kc == 0, kc == 7, rk + [("xnT", kc)], [("ps", bk)])
        dst = AP(self.vtok, 650, 0, 128, 130, [(130, 4), (65, 2), (1, 64)])
        src = AP(self.ps[bk], 512, 0, 128, 0, [(128, 4), (64, 2), (1, 64)])
        self.op("act", lambda e, o=dst, s=src: e.activation(out=o, in_=s, func=AF.Copy), [("ps", bk)],
                [("v", 1), ("v", 2), ("v", 3), ("v", 4)])
        self.chunk_done()
        slot, rk = self.next_chunk(C_U)
        for i in range(4):
            bk = self.nbank()
            self.featT_mm(slot, rk, 128 * i, bk, self.xnT, "xnT", 512)
            dst = AP(self.uT, 2048, 0, 128, 512 * i, [(1, 512)])
            self.op("act", lambda e, o=dst, s=self.P(bk): e.activation(out=o, in_=s, func=AF.Gelu), [("ps", bk)],
                    [("u", i)])
        self.chunk_done()
        if mid is not None:
            mid()
        slot, rk = self.next_chunk(C_VG)
        for b in range(NB):
            bk = self.nbank()
            for kc in range(8):
                lhsT = AP(self.xnT, 4096, 0, 128, 512 * kc + 128 * b, [(1, 128)])
                rhs = AP(slot, CHE, 0, 128, kc * 512, [(1, 512)])
                self.mm(self.P(bk), lhsT, rhs, kc == 0, kc == 7, rk + [("xnT", kc)], [("ps", bk)])
            tv = self.ntmp()
            vg = self.T(tv)
            self.op("act", lambda e, o=vg, s=self.P(bk): e.activation(out=o, in_=s, func=AF.Gelu), [("ps", bk)],
                    [("tmp", tv)])
            bst = AP(self.bnst, 24, 0, 128, 6 * b, [(1, 6)])
            self.op("dve", lambda e, o=bst, i=vg: e.bn_stats(out=o, in_=i), [("tmp", tv)], [("bn", b)])
            mv = AP(self.st, 64, 0, 128, 32 + 2 * b, [(1, 2)])
            self.op("dve", lambda e, o=mv, i=bst: e.bn_aggr(out=o, in_=i), [("bn", b)], [("mv", b)])
            var = AP(self.st, 64, 0, 128, 32 + 2 * b + 1, [(1, 1)])
            mean = AP(self.st, 64, 0, 128, 32 + 2 * b, [(1, 1)])
            rs = AP(self.st, 64, 0, 128, 48 + b, [(1, 1)])
            self.op("dve", lambda e, o=rs, i=var: e.tensor_scalar(out=o, in0=i, scalar1=LN_EPS, scalar2=None,
                                                                  op0=ALU.add),
                    [("mv", b)], [("lnr", b)])
            nh = AP(self.neghalf, 8, 0, 128, 0, [(1, 1)])
            self.op("pool", lambda e, o=rs, i=rs, h=nh: e.tensor_tensor(out=o, in0=i, in1=h, op=ALU.pow),
                    [("lnr", b), ("c", "neghalf")], [("lnr", b)])
            dst = AP(self.vln, 2048, 0, 128, 512 * b, [(1, 512)])
            self.op("dve", lambda e, o=dst, i=vg, m=mean, r=rs: e.tensor_scalar(out=o, in0=i, scalar1=m, scalar2=r,
                                                                                 op0=ALU.subtract, op1=ALU.mult),
                    [("tmp", tv), ("mv", b), ("lnr", b)], [("vln", b)])
        self.chunk_done()

    def stage_attn(self, t):
        ts = t % self.tps
        units = [(b, g) for b in range(NB) for g in range(2)]
        pend = None
        idn = AP(self.ident, 128, 0, 128, 0, [(1, 128)])
        ones = AP(self.ones, 128, 0, 128, 0, [(1, 128)])

        def pv(u):
            b, g, parts, pti, ui = u
            bo = self.nbank()
            for hh in range(4):
                for idx, p in enumerate(parts):
                    lhsT = AP(self.PT, 2048, 0, 128, 1024 * pti + 512 * (hh // 2) + 256 * p + 128 * (hh % 2),
                              [(1, 128)])
                    rhs = AP(self.vtok, 650, 0, 128, 130 * (b + p) + 65 * g, [(1, 65)])
                    self.mm(self.P(bo, 65, 65 * hh), lhsT, rhs, idx == 0, idx == len(parts) - 1,
                            [("v", b + p), ("PT", pti)], [("ps", bo)])
            ro = 56 + 4 * (ui % 2)
            r = AP(self.st, 64, 0, 128, ro, [(1, 4)])
            den = AP(self.ps[bo], 512, 0, 128, 64, [(65, 4)])
            es_ = AP(self.esink, 8, 0, 128, 4 * g, [(1, 4)])
            self.op("dve", lambda e, o=r, a=den, c=es_: e.tensor_tensor(out=o, in0=a, in1=c, op=ALU.add),
                    [("ps", bo), ("c", "esink")], [("st", ro)])
            self.op("dve", lambda e, o=r: e.reciprocal(out=o, in_=o), [("st", ro)], [("st", ro)])
            for hh in range(4):
                dst = AP(self.atok, 2048, 0, 128, 512 * b + 64 * (4 * g + hh), [(1, 64)])
                rs = AP(self.st, 64, 0, 128, ro + hh, [(1, 1)])
                if hh % 2 == 0:
                    self.op("act", lambda e, o=dst, i=self.P(bo, 64, 65 * hh), s_=rs:
                            e.activation(out=o, in_=i, func=AF.Copy, scale=s_),
                            [("ps", bo), ("st", ro)], [("atok", b, g)])
                else:
                    self.op("dve", lambda e, o=dst, i=self.P(bo, 64, 65 * hh), s_=rs:
                            e.tensor_scalar(out=o, in0=i, scalar1=s_, scalar2=None, op0=ALU.mult),
                            [("ps", bo), ("st", ro)], [("atok", b, g)])
            return b if g == 1 else None

        def tr_block(b):
            bt = self.nbank()
            for c4 in range(4):
                o = AP(self.psb[bt], 1024, 0, 128, 128 * c4, [(1, 128)])
                i = AP(self.atok, 2048, 0, 128, 512 * b + 128 * c4, [(1, 128)])
                self.op("pe", lambda e, o=o, i=i: e.transpose(out=o, in_=i, identity=idn),
                        [("atok", b, 0), ("atok", b, 1), ("c", "ident")], [("ps", bt)])
            src = AP(self.psb[bt], 1024, 0, 128, 0, [(128, 4), (1, 128)])
            dst = AP(self.attnT, 2048, 0, 128, 128 * b, [(512, 4), (1, 128)])
            self.op("dve", lambda e, o=dst, s_=src: e.tensor_copy(out=o, in_=s_), [("ps", bt)],
                    [("attn", c4) for c4 in range(4)])

        trp = None
        for ui, (b, g) in enumerate(units):
            parts = [1] if (ts == 0 and b == 0) else [0, 1]
            pti = ui % 2
            banks = (self.nbank(), self.nbank())
            c0 = 256 * parts[0]
            n = 256 * len(parts)
            for half in range(2):
                tb = AP(self.tbl, 2048, 0, 128, 1024 * g + 512 * half + c0, [(1, n)])
                self.mm(self.P(banks[half], n, c0), idn, tb, True, False, [("c", "ident"), ("c", "tbl")],
                        [("ps", banks[half])])
            for pi, p in enumerate(parts):
                kslot = b + p
                for half in range(2):
                    lhsT = AP(self.kdup, 1280, 64 * half, 64, 640 * g + 128 * kslot, [(1, 128)])
                    rhs = AP(self.Q2, 2048, 64 * half, 64, 512 * (2 * g) + 128 * b, [(512, 2), (1, 128)])
                    self.mm(self.P(banks[half], 256, 256 * p), lhsT, rhs, False, pi == len(parts) - 1,
                            [("k", kslot), ("q", 2 * g), ("q", 2 * g + 1)], [("ps", banks[half])])
            for half in range(2):
                dst = AP(self.PT, 2048, 0, 128, 1024 * pti + 512 * half + c0, [(1, n)])
                self.op("act", lambda e, o=dst, s=self.P(banks[half], n, c0):
                        e.activation(out=o, in_=s, func=AF.Exp, scale=0.125),
                        [("ps", banks[half])], [("PT", pti)])
            done = pv(pend) if pend is not None else None
            if trp is not None:
                tr_block(trp)
            trp = done
            pend = (b, g, parts, pti, ui)
            yield
        done = pv(pend)
        if trp is not None:
            tr_block(trp)
        yield
        tr_block(done)
        yield

    def stage_spatial(self, t):
        for b in range(NB):
            bk = self.nbank()
            for gq in range(4):
                lhsT = AP(self.vln, 2048, 0, 128, 512 * b + 128 * gq, [(1, 128)])
                rhs = AP(self.wsT, 512, 0, 128, 128 * gq, [(1, 128)])
                self.mm(self.P(bk, 128, 128 * gq), lhsT, rhs, True, True, [("vln", b), ("c", "wsT")], [("ps", bk)])
            tq = self.ntmp()
            for gq in range(4):
                self.op("dve", lambda e, o=self.T(tq, 128, off=128 * gq), i=self.P(bk, 128, 128 * gq),
                        s=AP(self.lngpp, 4, 0, 128, gq, [(1, 1)]), c=AP(self.B2, 512, 0, 128, 128 * gq, [(1, 128)]):
                        e.scalar_tensor_tensor(out=o, in0=i, scalar=s, in1=c, op0=ALU.mult, op1=ALU.add),
                        [("ps", bk), ("c", "lngpp"), ("c", "B2")], [("tmp", tq)])
            t4 = AP(self.tmp[tq], 512, 0, 128, 0, [(128, 4), (1, 128)])
            u4 = AP(self.uT, 2048, 0, 128, 128 * b, [(512, 4), (1, 128)])
            dst = AP(self.gmT, 2048, 0, 128, 128 * b, [(512, 4), (1, 128)])
            self.op("dve", lambda e, o=dst, a=t4, c=u4: e.tensor_tensor(out=o, in0=a, in1=c, op=ALU.mult),
                    [("tmp", tq)] + [("u", i) for i in range(4)], [("gm", i) for i in range(4)])

    def stage_merge(self, t, inter=None, per_chunk=2):
        for c in range(8):
            slot, rk = self.next_chunk(C_M0 + c)
            bga, bgb, bra, brb = self.nbank(), self.nbank(), self.nbank(), self.nbank()
            self.featT_mm(slot, rk, 0, bga, self.xnT, "xnT", 256)
            self.featT_mm(slot, rk, 128, bgb, self.xnT, "xnT", 256)
            self.featT_mm(slot, rk, 0, bra, self.attnT, "attn", 128, nk=4, woff=2048)
            self.featT_mm(slot, rk, 0, brb, self.gmT, "gm", 128, nk=4, woff=2560)
            self.chunk_done()
            ta, tb_, t1, t2 = self.ntmp(), self.ntmp(), self.ntmp(), self.ntmp()
            self.op("act", lambda e, o=self.T(ta), s=self.P(bga): e.activation(out=o, in_=s, func=AF.Sigmoid),
                    [("ps", bga)], [("tmp", ta)])
            self.op("act", lambda e, o=self.T(tb_), s=self.P(bgb): e.activation(out=o, in_=s, func=AF.Sigmoid),
                    [("ps", bgb)], [("tmp", tb_)])
            self.op("dve", lambda e, o=self.T(t1), a=self.P(bra), c=self.T(ta):
                    e.tensor_tensor(out=o, in0=a, in1=c, op=ALU.mult), [("ps", bra), ("tmp", ta)], [("tmp", t1)])
            self.op("dve", lambda e, o=self.T(t2), a=self.P(brb), c=self.T(tb_):
                    e.tensor_tensor(out=o, in0=a, in1=c, op=ALU.mult), [("ps", brb), ("tmp", tb_)], [("tmp", t2)])
            dst = AP(self.mergedT, 4096, 0, 128, 512 * c, [(1, 512)])
            self.op("dve", lambda e, o=dst, a=self.T(t1), c_=self.T(t2):
                    e.tensor_tensor(out=o, in0=a, in1=c_, op=ALU.add), [("tmp", t1), ("tmp", t2)], [("mg", c)])
            if inter is not None and c >= 1:
                for _ in range(per_chunk):
                    next(inter, None)
        if inter is not None:
            for _ in inter:
                pass

    def tokmajor_out(self, xi, act_t, akey, chunk_groups, gbc, gname, stbase, inter=None, after_pair=None):
        xb = self.xb[xi]
        F = act_t.shape[1]
        ktot = sum(nk for _, nk in chunk_groups[0])
        resident = all(len(g) == 1 for g in chunk_groups)
        held = None
        for pair in range(2):
            blocks = (2 * pair, 2 * pair + 1)
            banks = {}
            for hf in range(2):
                for b in blocks:
                    banks[(hf, b)] = self.nbank()
            self.held |= set(banks.values())
            if resident:
                if pair == 0:
                    held = [self.next_chunk(chunk_groups[hf][0][0]) for hf in range(2)]
            for hf in range(2):
                k0 = 0
                for gi, (cid, nk) in enumerate(chunk_groups[hf]):
                    slot, rk = held[hf] if resident else self.next_chunk(cid)
                    for b in blocks:
                        for kk in range(nk):
                            kg = k0 + kk
                            lhsT = AP(act_t, F, 0, 128, 512 * kg + 128 * b, [(1, 128)])
                            rhs = AP(slot, CHE, 0, 128, 512 * kk, [(1, 512)])
                            self.mm(self.P(banks[(hf, b)]), lhsT, rhs, kg == 0, kg == ktot - 1,
                                    rk + [(akey, kg)], [("ps", banks[(hf, b)])])
                    k0 += nk
                    if not resident:
                        self.chunk_done()
                    if inter is not None:
                        next(inter, None)
            if resident and pair == 1:
                self.chunk_done()
            self.held -= set(banks.values())
            self.soft = set(banks.values())
            self.soft_n = 4
            tys = {}
            for b in blocks:
                for hf in range(2):
                    bk = banks[(hf, b)]
                    tj = self.ntmp()
                    acc = AP(self.st, 64, 0, 128, stbase + 2 * b + hf, [(1, 1)])
                    self.op("act", lambda e, o=self.T(tj), i=self.P(bk), a=acc:
                            e.activation(out=o, in_=i, func=AF.Square, accum_out=a),
                            [("ps", bk)], [("tmp", tj), ("st", stbase + 2 * b + hf)])
                    ty = self.ntmp()
                    tys[(hf, b)] = ty
                    g = AP(gbc, 1024, 0, 128, 512 * hf, [(1, 512)])
                    self.op("dve", lambda e, o=self.T(ty), i=self.P(bk), g=g:
                            e.tensor_tensor(out=o, in0=i, in1=g, op=ALU.mult),
                            [("ps", bk), ("c", gname)], [("tmp", ty)])
            b0 = blocks[0]
            sse = AP(self.st, 64, 0, 128, stbase + 2 * b0, [(2, 2)])
            sso = AP(self.st, 64, 0, 128, stbase + 2 * b0 + 1, [(2, 2)])
            ms = AP(self.st, 64, 0, 128, stbase + 8 + b0, [(1, 2)])
            mkey = ("st", stbase + 8 + pair)
            self.op("dve", lambda e, o=ms, a=sse, c=sso: e.tensor_tensor(out=o, in0=a, in1=c, op=ALU.add),
                    [("st", stbase + 2 * b + hf) for b in blocks for hf in range(2)], [mkey])
            self.op("dve", lambda e, o=ms: e.tensor_scalar(out=o, in0=o, scalar1=1.0 / D, scalar2=EPS,
                                                           op0=ALU.mult, op1=ALU.add), [mkey], [mkey])
            nh = AP(self.neghalf, 8, 0, 128, 0, [(1, 2)])
            self.op("pool", lambda e, o=ms, h=nh: e.tensor_tensor(out=o, in0=o, in1=h, op=ALU.pow),
                    [mkey, ("c", "neghalf")], [mkey])
            for b in blocks:
                for hf in range(2):
                    ty = tys[(hf, b)]
                    rs = AP(self.st, 64, 0, 128, stbase + 8 + b, [(1, 1)])
                    xh = AP(xb, 4096, 0, 128, D * b + 512 * hf, [(1, 512)])
                    self.op("dve", lambda e, o=xh, i=self.T(ty), s=rs, x_=xh:
                            e.scalar_tensor_tensor(out=o, in0=i, scalar=s, in1=x_, op0=ALU.mult, op1=ALU.add),
                            [("tmp", ty), mkey, ("xb", xi, b)], [("xb", xi, b)])
            if after_pair is not None:
                after_pair(pair)
        if inter is not None:
            for _ in inter:
                pass

    def stage_ffn_gu(self, t, mid=None, mid_at=2):
        for f in range(11):
            if mid is not None and f == mid_at:
                mid()
            slot, rk = self.next_chunk(C_G0 + f)
            for j in range(2):
                fc = 2 * f + j
                bg = self.nbank()
                bu = self.nbank()
                self.featT_mm(slot, rk, 128 * j, bg, self.hnT, "hnT", 512)
                self.featT_mm(slot, rk, 256 + 128 * j, bu, self.hnT, "hnT", 512)
                tg = self.ntmp()
                self.op("act", lambda e, o=self.T(tg), s=self.P(bg): e.activation(out=o, in_=s, func=AF.Silu),
                        [("ps", bg)], [("tmp", tg)])
                dst = AP(self.hT, NFC * 512, 0, 128, 512 * fc, [(1, 512)])
                self.op("dve", lambda e, o=dst, a=self.P(bu), c=self.T(tg):
                        e.tensor_tensor(out=o, in0=a, in1=c, op=ALU.mult),
                        [("ps", bu), ("tmp", tg)], [("h", fc)])
            self.chunk_done()

    def store_y(self, t):
        xi = t % NXB
        xb = self.xb[xi]
        dst = bass.AP(self.y_d, t * TT * D, [[D, 128], [128 * D, NB], [1, D]])
        src = AP(xb, 4096, 0, 128, 0, [(D, NB), (1, D)])
        self.dma(dst, src, [("xb", xi, b) for b in range(NB)], [("y", t)], ("Y", xi))

    def record(self):
        n = self.n_tiles
        self.stream_init()
        self.load_x(0)
        self.load_consts(0)
        for _ in range(NSLOT - 1):
            self.record_load()
        self.rms_scale_all(0, 0)
        self.transposes(self.xnT, "xnT", self.g1, "g1")
        self.load_consts(1)
        self.stage_front(0)
        for t in range(1, min(n, NXB)):
            self.load_x(t)
        wo_groups = [[(C_O0, 8)], [(C_O0 + 1, 8)]]
        dn_groups = [[(C_D0, 8), (C_D0 + 1, 8), (C_D0 + 2, 6)], [(C_D0 + 3, 8), (C_D0 + 4, 8), (C_D0 + 5, 6)]]
        for _ in self.stage_attn(0):
            pass
        self.stage_spatial(0)
        for t in range(n):
            xi = t % NXB
            nxt = t + 1 < n
            self.stage_merge(t, inter=self.rms_scale((t + 1) % NXB, 0) if nxt else None)
            if t >= 1 and t - 1 + NXB < n:
                self.load_x(t - 1 + NXB)
            if nxt:
                self.transposes(self.xnT, "xnT", self.g1, "g1")
            self.tokmajor_out(xi, self.mergedT, "mg", wo_groups, self.g2bc, "g2bc", 16,
                              after_pair=lambda pair, xi=xi: self.rms_scale_all(xi, 1, (2 * pair, 2 * pair + 1)))
            hT = lambda: self.transposes(self.hnT, "hnT", self.g3, "g3")
            if nxt:
                self.stage_front(t + 1, mid=hT)
                self.stage_ffn_gu(t, mid=lambda t=t: self.stage_spatial(t + 1))
            else:
                hT()
                self.stage_ffn_gu(t)
            inter = self.stage_attn(t + 1) if nxt else None
            self.tokmajor_out(xi, self.hT, "h", dn_groups, self.g4bc, "g4bc", 16, inter=inter)
            self.store_y(t)
        self.op("sp", None, [("y", t) for t in range(n)], [])
        self.S.finalize()

    def emit(self):
        nc = self.nc
        S = self.S
        for key in S.dma_count:
            self.sem_dma[key] = self.es.enter_context(nc.semaphore("d_" + "_".join(str(k) for k in key)))

        def semof(key):
            return self.sem_eng[key[1]] if key[0] == "eng" else self.sem_dma[key[1]]

        def run(eng_name):
            def body(e):
                for op in S.ops[eng_name]:
                    for key, val in op.waits:
                        e.wait_ge(semof(key), val)
                    if op.fn is None:
                        continue
                    ins = op.fn(e)
                    if op.dma_key is not None:
                        ins.then_inc(self.sem_dma[op.dma_key], 16)
                    elif op.marked:
                        ins.then_inc(self.sem_eng[eng_name], 1)
            return body

        with nc.Block() as block:
            block.sync(run("sp"))
            block.tensor(run("pe"))
            block.scalar(run("act"))
            block.vector(run("dve"))
            block.gpsimd(run("pool"))


def build(n_tiles, tiles_per_seq):
    b = Builder(n_tiles, tiles_per_seq)
    es = ExitStack()
    with es:
        b.alloc(es)
        b.record()
        b.emit()
    return b.nc


def prepare_weights(inp):
    wsrc = build_wsrc(inp["w_in"][0], inp["w_attn_branch"][0], inp["w_gmlp_branch"][0], inp["w_out"][0],
                      inp["w_ffn_gate"][0], inp["w_ffn_up"][0], inp["w_ffn_down"][0])
    consts = build_consts(inp["norm_mix_pre"][0], inp["norm_mix_post"][0], inp["norm_ffn_pre"][0],
                          inp["norm_ffn_post"][0], inp["attn_sinks"][0], inp["gmlp_ln_g"][0],
                          inp["gmlp_ln_b"][0], inp["gmlp_w_s"][0], inp["gmlp_b_s"][0])
    m = {"wsrc": wsrc.reshape(NCH * 128, CHE)}
    for k, v in consts.items():
        m["c_" + k] = v
    return m


def kernel(**inputs):
    inp = {k: np.asarray(v, dtype=np.float32) for k, v in inputs.items()}
    x = inp["x"]
    B, S, _ = x.shape
    tps = S // TT
    seq_per_core = B // N_CORES
    n_tiles = seq_per_core * tps
    shared = prepare_weights(inp)
    xs = np.ascontiguousarray(x).reshape(N_CORES, seq_per_core * S, D)
    nc = build(n_tiles, tps)
    in_maps = []
    for c in range(N_CORES):
        m = dict(shared)
        m["x"] = xs[c]
        in_maps.append(m)
    res = run_bass_kernel_spmd(nc, in_maps, core_ids=list(range(N_CORES)))
    out = np.stack([np.asarray(r["y"], dtype=np.float32) for r in res.results], axis=0)
    return out.reshape(B, S, D)
```

```python
import numpy as np
from contextlib import ExitStack
import concourse.bass as bass
import concourse.mybir as mybir
from concourse.bass_utils import run_bass_kernel_spmd

F32 = mybir.dt.float32
BF16 = mybir.dt.bfloat16
AF = mybir.ActivationFunctionType
ALU = mybir.AluOpType

D = 1024
DFF = 2816
NFC = 22
TT = 512
NB = 4
EPS = 1e-6
LN_EPS = 1e-5
NCH = 31
CHE = 4096
NSLOT = 4
NTMP = 10
N_CORES = 8
CAST_DMA = True
NXB = 3

ENGS = ("pe", "act", "dve", "pool", "sp")


class Op:
    __slots__ = ("eng", "fn", "deps", "marked", "mark", "dma_key", "dma_val", "waits")

    def __init__(self, eng, fn):
        self.eng = eng
        self.fn = fn
        self.deps = []
        self.marked = False
        self.mark = 0
        self.dma_key = None
        self.dma_val = 0
        self.waits = []


class Sched:
    def __init__(self):
        self.ops = {e: [] for e in ENGS}
        self.state = {}
        self.dma_count = {}
        self.bulk = {}

    def add(self, eng, fn, reads=(), writes=(), dma=None, bulk=False):
        op = Op(eng, fn)
        deps = {}
        tmpkey = False

        def dep(o, kind):
            nonlocal tmpkey
            if o is op:
                return
            if o.dma_key is None and o.eng == eng and kind != "raw" and not (kind == "waw" and tmpkey):
                return
            k = id(o)
            if k not in deps:
                deps[k] = o

        for k in reads:
            st = self.state.get(k)
            if st is None:
                continue
            if st[0] is not None:
                dep(st[0], "raw")
            if isinstance(k, tuple) and k[0] == "ps":
                for r in st[1].values():
                    if r.eng != eng:
                        dep(r, "rar")
        for k in writes:
            st = self.state.get(k)
            if st is None:
                continue
            tmpkey = isinstance(k, tuple) and k[0] == "tmp" and eng != "pe"
            if st[0] is not None:
                dep(st[0], "waw")
            tmpkey = False
            for r in st[1].values():
                dep(r, "war")
        op.deps = list(deps.values())
        for o in op.deps:
            o.marked = True
        if dma is not None:
            op.dma_key = dma
            c = self.dma_count.get(dma, 0) + 1
            self.dma_count[dma] = c
            op.dma_val = 16 * c
            if bulk:
                self.bulk.setdefault(dma, []).append(op)
        for k in writes:
            self.state[k] = [op, {}]
        for k in reads:
            st = self.state.get(k)
            if st is None:
                st = [None, {}]
                self.state[k] = st
            st[1][eng] = op
        self.ops[eng].append(op)
        return op

    def finalize(self):
        for key, ops in self.bulk.items():
            tot = 16 * self.dma_count[key]
            for o in ops:
                o.dma_val = tot
        for e in ENGS:
            n = 0
            for op in self.ops[e]:
                if op.dma_key is None and op.marked:
                    n += 1
                    op.mark = n
        for e in ENGS:
            waited = {}
            for op in self.ops[e]:
                w = {}
                for d in op.deps:
                    if d.dma_key is not None:
                        key, val = ("dma", d.dma_key), d.dma_val
                    else:
                        key, val = ("eng", d.eng), d.mark
                    if val > w.get(key, 0):
                        w[key] = val
                op.waits = []
                for key, val in w.items():
                    if waited.get(key, 0) >= val:
                        continue
                    waited[key] = val
                    op.waits.append((key, val))


def AP(t, F, p0, npart, off, dims):
    return bass.AP(t, p0 * F + off, [[F, npart]] + [[s, c] for (s, c) in dims])


def _kc_layout(w):
    C = w.shape[1]
    return np.ascontiguousarray(w.reshape(8, 128, C).transpose(1, 0, 2)).reshape(128, 8 * C)


def _pad(a):
    out = np.zeros((128, CHE), np.float32)
    out[: a.shape[0], : a.shape[1]] = a
    return out


def _q_perm():
    cols = []
    for i in range(4):
        g, j = i // 2, i % 2
        lo, hi = 4 * g + j, 4 * g + 2 + j
        cols += list(range(64 * lo, 64 * lo + 64)) + list(range(64 * hi, 64 * hi + 64))
    return np.array(cols)


C_Q, C_KV, C_U, C_VG = 0, 1, 2, 3
C_M0 = 4
C_O0 = 12
C_G0 = 14
C_D0 = 25


def build_wsrc(w_in, w_a, w_g, w_out, w_gate, w_up, w_down):
    chunks = []
    wq = w_in[:, 0:512][:, _q_perm()]
    chunks.append(_pad(_kc_layout(wq)))
    wk = w_in[:, 512:640]
    wv = w_in[:, 640:768]
    kv = np.concatenate([wk[:, 0:64], wk[:, 0:64], wk[:, 64:128], wk[:, 64:128], wv], axis=1)
    chunks.append(_pad(_kc_layout(kv)))
    chunks.append(_pad(_kc_layout(w_in[:, 768:1280])))
    chunks.append(_pad(_kc_layout(w_in[:, 1280:1792])))
    for c in range(8):
        ga = w_in[:, 1792 + 128 * c: 1792 + 128 * (c + 1)]
        gb = w_in[:, 2816 + 128 * c: 2816 + 128 * (c + 1)]
        gab = _kc_layout(np.concatenate([ga, gb], axis=1))
        wa = w_a[:, 128 * c:128 * (c + 1)].reshape(4, 128, 128).transpose(1, 0, 2).reshape(128, 512)
        wg = w_g[:, 128 * c:128 * (c + 1)].reshape(4, 128, 128).transpose(1, 0, 2).reshape(128, 512)
        chunks.append(_pad(np.concatenate([gab, wa, wg], axis=1)))
    for hf in range(2):
        chunks.append(_pad(_kc_layout(w_out[:, 512 * hf:512 * (hf + 1)])))
    for f in range(11):
        gu = np.concatenate([w_gate[:, 256 * f:256 * (f + 1)], w_up[:, 256 * f:256 * (f + 1)]], axis=1)
        chunks.append(_pad(_kc_layout(gu)))
    for hf in range(2):
        for (f0, f1) in ((0, 8), (8, 16), (16, 22)):
            blk = w_down[128 * f0:128 * f1, 512 * hf:512 * (hf + 1)]
            n = f1 - f0
            chunks.append(_pad(blk.reshape(n, 128, 512).transpose(1, 0, 2).reshape(128, n * 512)))
    assert len(chunks) == NCH
    return np.stack(chunks, axis=0)


def build_consts(norm_mix_pre, norm_mix_post, norm_ffn_pre, norm_ffn_post, attn_sinks,
                 ln_g, ln_b, w_s, b_s):
    c = {}
    c["ident"] = np.eye(128, dtype=np.float32)
    tbl = np.zeros((128, 2, 2, 2, 2, 128), np.float32)
    j = np.arange(128)[:, None]
    a = np.arange(128)[None, :]
    for g in range(2):
        for half in range(2):
            for hh2 in range(2):
                h = 4 * g + 2 * half + hh2
                slope = 2.0 ** (-(h + 1))
                rel0 = 128 + a - j
                rel1 = a - j
                tbl[:, g, half, 0, hh2, :] = np.where(a < j, -slope * rel0 * 8.0, -30000.0)
                tbl[:, g, half, 1, hh2, :] = np.where(a >= j, -slope * rel1 * 8.0, -30000.0)
    c["tbl"] = tbl.reshape(128, 2048)
    c["g1"] = np.ascontiguousarray(norm_mix_pre.reshape(8, 128).T)
    c["g3"] = np.ascontiguousarray(norm_ffn_pre.reshape(8, 128).T)
    c["g2bc"] = np.ascontiguousarray(np.broadcast_to(norm_mix_post.reshape(1, D), (128, D)))
    c["g4bc"] = np.ascontiguousarray(np.broadcast_to(norm_ffn_post.reshape(1, D), (128, D)))
    c["lngpp"] = np.ascontiguousarray(ln_g.reshape(4, 128).T)
    c["lnbpp"] = np.ascontiguousarray(ln_b.reshape(4, 128).T)
    c["bsbc"] = np.ascontiguousarray(np.broadcast_to(b_s.reshape(1, 512), (128, 512)))
    c["sinkbc"] = np.ascontiguousarray(np.broadcast_to(attn_sinks.reshape(1, 8), (128, 8)))
    c["wsT"] = np.ascontiguousarray(w_s.transpose(2, 0, 1)).reshape(128, 512)
    s_ = np.arange(128)[:, None]
    t_ = np.arange(128)[None, :]
    cm = (t_ >= s_).astype(np.float32)
    c["cmask"] = np.ascontiguousarray(np.broadcast_to(cm[:, None, :], (128, 4, 128))).reshape(128, 512)
    return {k: np.ascontiguousarray(v, dtype=np.float32) for k, v in c.items()}


CONST_SHAPES = {"ident": 128, "tbl": 2048, "g1": 8, "g3": 8, "g2bc": 1024, "g4bc": 1024,
                "lngpp": 4, "lnbpp": 4, "bsbc": 512, "sinkbc": 8, "wsT": 512, "cmask": 512}


class Builder:
    def __init__(self, n_tiles, tiles_per_seq):
        self.n_tiles = n_tiles
        self.tps = tiles_per_seq
        self.S = Sched()
        self.nc = bass.Bass("TRN2", target_bir_lowering=False)
        self.ntok = n_tiles * TT

    def alloc(self, es):
        nc = self.nc
        self.x_d = nc.dram_tensor("x", [self.ntok, D], F32, kind="ExternalInput")
        self.y_d = nc.dram_tensor("y", [self.ntok, D], F32, kind="ExternalOutput")
        self.wsrc_d = nc.dram_tensor("wsrc", [NCH * 128, CHE], F32, kind="ExternalInput")
        self.wbf_d = nc.dram_tensor("wbf", [NCH * 128, CHE], BF16, kind="Internal")
        self.c_d = {k: nc.dram_tensor("c_" + k, [128, n], F32, kind="ExternalInput")
                    for k, n in CONST_SHAPES.items()}

        def sb(name, n, dt):
            return es.enter_context(nc.sbuf_tensor(name, [128, n], dt))

        self.xb = [sb("xb%d" % i, 4096, F32) for i in range(NXB)]
        self.xn = sb("xn", 4096, BF16)
        self.xnT = sb("xnT", 4096, BF16)
        self.hnT = sb("hnT", 4096, BF16)
        self.Q2 = sb("Q2", 2048, BF16)
        self.kdup = sb("kdup", 1280, BF16)
        self.vtok = sb("vtok", 650, BF16)
        self.uT = sb("uT", 2048, BF16)
        self.vln = sb("vln", 2048, BF16)
        self.PT = sb("PT", 2048, BF16)
        self.attnT = sb("attnT", 2048, BF16)
        self.atok = sb("atok", 2048, BF16)
        self.gmT = sb("gmT", 2048, BF16)
        self.mergedT = sb("mergedT", 4096, BF16)
        self.hT = sb("hT", NFC * 512, BF16)
        self.ring = [sb("ring%d" % i, CHE, BF16) for i in range(NSLOT)]
        self.stg = [sb("stg%d" % i, 1024, F32) for i in range(2)] if not CAST_DMA else []
        self.stg_i = 0
        self.tmp = [sb("tmp%d" % i, 512, F32) for i in range(NTMP)]
        self.tmp_i = 0
        self.tbl = sb("tbl", 2048, BF16)
        self.g2bc = sb("g2bc", 1024, F32)
        self.g4bc = sb("g4bc", 1024, F32)
        self.B2 = sb("B2", 512, F32)
        self.esink = sb("esink", 8, F32)
        self.wsT = sb("wsT", 512, BF16)
        self.ident = sb("ident", 128, BF16)
        self.ones = sb("ones", 128, BF16)
        self.g1 = sb("g1", 8, F32)
        self.g3 = sb("g3", 8, F32)
        self.lngpp = sb("lngpp", 4, F32)
        self.lnbpp = sb("lnbpp", 4, F32)
        self.neghalf = sb("neghalf", 8, F32)
        self.st = sb("stats", 64, F32)
        self.bnst = sb("bnst", 4 * 6, F32)
        self.ps = [es.enter_context(nc.psum_tensor("ps%d" % i, [128, 512], F32)) for i in range(8)]
        self.psb = [p.bitcast(BF16) for p in self.ps]
        self.ps_i = 0
        self.held = set()
        self.soft = set()
        self.soft_n = 0
        self.sem_eng = {e: es.enter_context(nc.semaphore("s_" + e)) for e in ("pe", "act", "dve", "pool")}
        self.sem_dma = {}
        self.es = es

    def nbank(self):
        pick = None
        for k in range(8):
            i = (self.ps_i + k) % 8
            if i in self.held:
                continue
            if self.soft_n > 0 and i in self.soft:
                continue
            pick = i
            break
        if pick is None:
            for k in range(8):
                i = (self.ps_i + k) % 8
                if i not in self.held:
                    pick = i
                    break
        self.ps_i = (pick + 1) % 8
        if self.soft_n > 0:
            self.soft_n -= 1
        return pick

    def ntmp(self):
        i = self.tmp_i
        self.tmp_i = (i + 1) % NTMP
        return i

    def T(self, i, n=512, p0=0, npart=128, off=0):
        return AP(self.tmp[i], 512, p0, npart, off, [(1, n)])

    def P(self, bk, n=512, off=0, p0=0, npart=128):
        return AP(self.ps[bk], 512, p0, npart, off, [(1, n)])

    def op(self, eng, fn, reads=(), writes=(), dma=None, bulk=False):
        return self.S.add(eng, fn, reads, writes, dma, bulk)

    def mm(self, out, lhsT, rhs, start, stop, reads, writes):
        self.op("pe", lambda e: e.matmul(out, lhsT, rhs, start=start, stop=stop), reads, writes)

    def dma(self, out, in_, reads, writes, key, bulk=False, q="sp", max_last=None):
        if max_last is None:
            fn = lambda e: e.dma_start(out=out, in_=in_)
        else:
            fn = lambda e: e.dma_start(out=out, in_=in_, max_dma_last_dim=max_last)
        self.op(q, fn, reads, writes, dma=key, bulk=bulk)

    def load_consts(self, part):
        ckey = ("C%d" % part,)
        def cdram(name, off=0, n=None):
            F = CONST_SHAPES[name]
            return AP(self.c_d[name], F, 0, 128, off, [(1, n or F)])

        def direct(name, t):
            n = CONST_SHAPES[name]
            self.dma(AP(t, n, 0, 128, 0, [(1, n)]), cdram(name), [], [("c", name)], ckey, bulk=True)

        def via_tmp(name, off, n):
            ti = self.ntmp()
            self.dma(self.T(ti, n), cdram(name, off, n), [], [("tmp", ti)], ckey, bulk=True)
            return ti

        if part == 0:
            for name, t in (("g1", self.g1), ("g3", self.g3), ("lngpp", self.lngpp), ("lnbpp", self.lnbpp)):
                direct(name, t)
        else:
            for name, t in (("g2bc", self.g2bc), ("g4bc", self.g4bc)):
                direct(name, t)
        if part == 0:
            self.op("dve", lambda e, o=AP(self.ones, 128, 0, 128, 0, [(1, 128)]): e.memset(o, 1.0), [], [("c", "ones")])
            self.op("dve", lambda e, o=AP(self.neghalf, 8, 0, 128, 0, [(1, 8)]): e.memset(o, -0.5), [],
                    [("c", "neghalf")])
            ti = via_tmp("ident", 0, 128)
            self.op("dve", lambda e, o=AP(self.ident, 128, 0, 128, 0, [(1, 128)]), s=self.T(ti, 128):
                    e.tensor_copy(out=o, in_=s), [("tmp", ti)], [("c", "ident")])
        else:
            for q4 in range(4):
                ti = via_tmp("tbl", 512 * q4, 512)
                self.op("dve", lambda e, o=AP(self.tbl, 2048, 0, 128, 512 * q4, [(1, 512)]), s=self.T(ti):
                        e.tensor_copy(out=o, in_=s), [("tmp", ti)], [("c", "tbl")])
            ti = via_tmp("sinkbc", 0, 8)
            self.op("act", lambda e, o=AP(self.esink, 8, 0, 128, 0, [(1, 8)]), s=self.T(ti, 8):
                    e.activation(out=o, in_=s, func=AF.Exp), [("tmp", ti)], [("c", "esink")])
            self.op("dve", lambda e, o=AP(self.vtok, 650, 0, 128, 0, [(1, 650)]): e.memset(o, 1.0), [],
                    [("v", i) for i in range(5)])
            t3 = via_tmp("wsT", 0, 512)
            t4 = via_tmp("cmask", 0, 512)
            self.op("dve", lambda e, o=AP(self.wsT, 512, 0, 128, 0, [(1, 512)]), a=self.T(t3), b=self.T(t4):
                    e.tensor_tensor(out=o, in0=a, in1=b, op=ALU.mult), [("tmp", t3), ("tmp", t4)], [("c", "wsT")])
            t5 = via_tmp("bsbc", 0, 512)
            bk = self.nbank()
            self.mm(self.P(bk), AP(self.ones, 128, 0, 128, 0, [(1, 128)]), AP(self.wsT, 512, 0, 128, 0, [(1, 512)]),
                    True, True, [("c", "ones"), ("c", "wsT")], [("ps", bk)])
            for g in range(4):
                self.op("dve", lambda e, o=AP(self.B2, 512, 0, 128, 128 * g, [(1, 128)]), i=self.P(bk, 128, 128 * g),
                        s=AP(self.lnbpp, 4, 0, 128, g, [(1, 1)]), b=self.T(t5, 128, off=128 * g):
                        e.scalar_tensor_tensor(out=o, in0=i, scalar=s, in1=b, op0=ALU.mult, op1=ALU.add),
                        [("ps", bk), ("c", "lnbpp"), ("tmp", t5)], [("c", "B2")])

    def stream_init(self):
        order = [C_Q, C_KV, C_U, C_VG]
        for t in range(self.n_tiles):
            order += [C_M0 + i for i in range(8)]
            order += [C_O0, C_O0 + 1]
            if t + 1 < self.n_tiles:
                order += [C_Q, C_KV, C_U, C_VG]
            order += [C_G0 + i for i in range(11)]
            order += [C_D0 + i for i in range(6)]
            order += [C_D0 + i for i in range(6)]
        self.order = order
        self.cast_done = set()
        self.n_loaded = 0
        self.n_used = 0

    def record_load(self):
        n = self.n_loaded
        if n >= len(self.order):
            return
        self.n_loaded += 1
        c = self.order[n]
        s = n % NSLOT
        slot = self.ring[s]
        rkeys = [("ring", s, q4) for q4 in range(4)]
        if c not in self.cast_done:
            self.cast_done.add(c)
            if CAST_DMA:
                self.dma(AP(slot, CHE, 0, 128, 0, [(1, CHE)]), AP(self.wsrc_d, CHE, c * 128, 128, 0, [(1, CHE)]),
                         [], rkeys, ("WC", s), q="pool", max_last=8192)
            for q4 in (range(4) if not CAST_DMA else ()):
                si = self.stg_i
                self.stg_i = 1 - si
                stg = AP(self.stg[si], 1024, 0, 128, 0, [(1, 1024)])
                src = AP(self.wsrc_d, CHE, c * 128, 128, 1024 * q4, [(1, 1024)])
                self.dma(stg, src, [], [("stg", si)], ("ST", si))
                dst = AP(slot, CHE, 0, 128, 1024 * q4, [(1, 1024)])
                eng = ("dve", "act", "pool", "dve")[q4]
                if eng == "act":
                    fn = lambda e, o=dst, i=stg: e.activation(out=o, in_=i, func=AF.Copy)
                else:
                    fn = lambda e, o=dst, i=stg: e.tensor_copy(out=o, in_=i)
                self.op(eng, fn, [("stg", si)], [("ring", s, q4)])
            self.dma(AP(self.wbf_d, CHE, c * 128, 128, 0, [(1, CHE)]), AP(slot, CHE, 0, 128, 0, [(1, CHE)]),
                     rkeys, [("wbf", c)], ("WS", s))
        else:
            self.dma(AP(slot, CHE, 0, 128, 0, [(1, CHE)]), AP(self.wbf_d, CHE, c * 128, 128, 0, [(1, CHE)]),
                     [("wbf", c)], rkeys, ("W", s))

    def next_chunk(self, expect):
        n = self.n_used
        assert self.order[n] == expect, (n, self.order[n], expect)
        self.n_used += 1
        while self.n_loaded <= n:
            self.record_load()
        s = n % NSLOT
        return self.ring[s], [("ring", s, q4) for q4 in range(4)]

    def chunk_done(self):
        while self.n_loaded < min(self.n_used + NSLOT, len(self.order)):
            self.record_load()

    def load_x(self, t):
        xb = self.xb[t % NXB]
        src = bass.AP(self.x_d, t * TT * D, [[D, 128], [128 * D, NB], [1, D]])
        dst = AP(xb, 4096, 0, 128, 0, [(D, NB), (1, D)])
        self.dma(dst, src, [], [("xb", t % NXB, b) for b in range(NB)], ("X", t % NXB))

    def rms_scale(self, xi, which, blocks=(0, 1, 2, 3)):
        xb = self.xb[xi]
        base = 8 * which
        nb_ = len(blocks)
        b0 = blocks[0]
        mkey = ("stm", which, b0)
        for b in blocks:
            src = AP(xb, 4096, 0, 128, D * b, [(1, D)])
            junk = AP(self.xn, 4096, 0, 128, D * b, [(1, D)])
            acc = AP(self.st, 64, 0, 128, base + b, [(1, 1)])
            self.op("act", lambda e, o=junk, i=src, a=acc: e.activation(out=o, in_=i, func=AF.Square, accum_out=a),
                    [("xb", xi, b)], [("xn", b), ("st", base + b)])
            yield
        ss = AP(self.st, 64, 0, 128, base + b0, [(1, nb_)])
        ms = AP(self.st, 64, 0, 128, base + 4 + b0, [(1, nb_)])
        self.op("dve", lambda e, o=ms, i=ss: e.tensor_scalar(out=o, in0=i, scalar1=1.0 / D, scalar2=EPS,
                                                             op0=ALU.mult, op1=ALU.add),
                [("st", base + b) for b in blocks], [mkey])
        nh = AP(self.neghalf, 8, 0, 128, 0, [(1, nb_)])
        self.op("pool", lambda e, o=ms, i=ms, h=nh: e.tensor_tensor(out=o, in0=i, in1=h, op=ALU.pow),
                [mkey, ("c", "neghalf")], [mkey])
        for b in blocks:
            src = AP(xb, 4096, 0, 128, D * b, [(1, D)])
            dst = AP(self.xn, 4096, 0, 128, D * b, [(1, D)])
            sc = AP(self.st, 64, 0, 128, base + 4 + b, [(1, 1)])
            eng = "dve" if b % 2 == 0 else "pool"
            self.op(eng, lambda e, o=dst, i=src, s=sc: e.tensor_scalar(out=o, in0=i, scalar1=s, scalar2=0.0,
                                                                       op0=ALU.mult, op1=ALU.add),
                    [("xb", xi, b), mkey], [("xn", b)])
            yield

    def rms_scale_all(self, xi, which, blocks=(0, 1, 2, 3)):
        for _ in self.rms_scale(xi, which, blocks):
            pass

    def transposes(self, dst_t, dkey, gain, gname):
        for kc in range(8):
            bk = self.nbank()
            for b in range(NB):
                o = AP(self.psb[bk], 1024, 0, 128, 128 * b, [(1, 128)])
                i = AP(self.xn, 4096, 0, 128, D * b + 128 * kc, [(1, 128)])
                idn = AP(self.ident, 128, 0, 128, 0, [(1, 128)])
                self.op("pe", lambda e, o=o, i=i, idn=idn: e.transpose(out=o, in_=i, identity=idn),
                        [("xn", b), ("c", "ident")], [("ps", bk)])
            src = AP(self.psb[bk], 1024, 0, 128, 0, [(1, 512)])
            dst = AP(dst_t, 4096, 0, 128, 512 * kc, [(1, 512)])
            gsc = AP(gain, 8, 0, 128, kc, [(1, 1)])
            self.op("dve", lambda e, o=dst, i=src, s=gsc: e.tensor_scalar(out=o, in0=i, scalar1=s, scalar2=None,
                                                                          op0=ALU.mult),
                    [("ps", bk), ("c", gname)], [(dkey, kc)])

    def featT_mm(self, slot, rkeys, col0, bk, act_t, akey, wstride, nk=8, woff=0):
        F = act_t.shape[1]
        for kc in range(nk):
            lhsT = AP(slot, CHE, 0, 128, woff + kc * wstride + col0, [(1, 128)])
            rhs = AP(act_t, F, 0, 128, 512 * kc, [(1, 512)])
            self.mm(self.P(bk), lhsT, rhs, kc == 0, kc == nk - 1, rkeys + [(akey, kc)], [("ps", bk)])

    def stage_front(self, t, mid=None):
        ts = t % self.tps
        if ts > 0:
            o = AP(self.kdup, 1280, 0, 128, 0, [(640, 2), (1, 128)])
            i = AP(self.kdup, 1280, 0, 128, 512, [(640, 2), (1, 128)])
            self.op("pool", lambda e, o=o, i=i: e.tensor_copy(out=o, in_=i), [("k", 4)], [("k", 0)])
            o = AP(self.vtok, 650, 0, 128, 0, [(1, 130)])
            i = AP(self.vtok, 650, 0, 128, 520, [(1, 130)])
            self.op("pool", lambda e, o=o, i=i: e.tensor_copy(out=o, in_=i), [("v", 4)], [("v", 0)])
        slot, rk = self.next_chunk(C_Q)
        for i in range(4):
            bk = self.nbank()
            self.featT_mm(slot, rk, 128 * i, bk, self.xnT, "xnT", 512)
            dst = AP(self.Q2, 2048, 0, 128, 512 * i, [(1, 512)])
            if i % 2 == 0:
                self.op("act", lambda e, o=dst, s=self.P(bk): e.activation(out=o, in_=s, func=AF.Copy),
                        [("ps", bk)], [("q", i)])
            else:
                self.op("dve", lambda e, o=dst, s=self.P(bk): e.tensor_copy(out=o, in_=s), [("ps", bk)],
                        [("q", i)])
        self.chunk_done()
        slot, rk = self.next_chunk(C_KV)
        for g in range(2):
            bk = self.nbank()
            self.featT_mm(slot, rk, 128 * g, bk, self.xnT, "xnT", 384)
            dst = AP(self.kdup, 1280, 0, 128, 640 * g + 128, [(1, 512)])
            self.op("dve", lambda e, o=dst, s=self.P(bk): e.tensor_copy(out=o, in_=s), [("ps", bk)],
                    [("k", 1), ("k", 2), ("k", 3), ("k", 4)])
        bk = self.nbank()
        for b in range(NB):
            for kc in range(8):
                lhsT = AP(self.xnT, 4096, 0, 128, 512 * kc + 128 * b, [(1, 128)])
                rhs = AP(slot, CHE, 0, 128, kc * 384 + 256, [(1, 128)])
                self.mm(self.P(bk, 128, 128 * b), lhsT, rhs, kc == 0, kc == 7, rk + [("xnT", kc)], [("ps", bk)])
        dst = AP(self.vtok, 650, 0, 128, 130, [(130, 4), (65, 2), (1, 64)])
        src = AP(self.ps[bk], 512, 0, 128, 0, [(128, 4), (64, 2), (1, 64)])
        self.op("act", lambda e, o=dst, s=src: e.activation(out=o, in_=s, func=AF.Copy), [("ps", bk)],
                [("v", 1), ("v", 2), ("v", 3), ("v", 4)])
        self.chunk_done()
        slot, rk = self.next_chunk(C_U)
        for i in range(4):
            bk = self.nbank()
            self.featT_mm(slot, rk, 128 * i, bk, self.xnT, "xnT", 512)
            dst = AP(self.uT, 2048, 0, 128, 512 * i, [(1, 512)])
            self.op("act", lambda e, o=dst, s=self.P(bk): e.activation(out=o, in_=s, func=AF.Gelu), [("ps", bk)],
                    [("u", i)])
        self.chunk_done()
        if mid is not None:
            mid()
        slot, rk = self.next_chunk(C_VG)
        for b in range(NB):
            bk = self.nbank()
            for kc in range(8):
                lhsT = AP(self.xnT, 4096, 0, 128, 512 * kc + 128 * b, [(1, 128)])
                rhs = AP(slot, CHE, 0, 128, kc * 512, [(1, 512)])
                self.mm(self.P(bk), lhsT, rhs, kc == 0, kc == 7, rk + [("xnT", kc)], [("ps", bk)])
            tv = self.ntmp()
            vg = self.T(tv)
            self.op("act", lambda e, o=vg, s=self.P(bk): e.activation(out=o, in_=s, func=AF.Gelu), [("ps", bk)],
                    [("tmp", tv)])
            bst = AP(self.bnst, 24, 0, 128, 6 * b, [(1, 6)])
            self.op("dve", lambda e, o=bst, i=vg: e.bn_stats(out=o, in_=i), [("tmp", tv)], [("bn", b)])
            mv = AP(self.st, 64, 0, 128, 32 + 2 * b, [(1, 2)])
            self.op("dve", lambda e, o=mv, i=bst: e.bn_aggr(out=o, in_=i), [("bn", b)], [("mv", b)])
            var = AP(self.st, 64, 0, 128, 32 + 2 * b + 1, [(1, 1)])
            mean = AP(self.st, 64, 0, 128, 32 + 2 * b, [(1, 1)])
            rs = AP(self.st, 64, 0, 128, 48 + b, [(1, 1)])
            self.op("dve", lambda e, o=rs, i=var: e.tensor_scalar(out=o, in0=i, scalar1=LN_EPS, scalar2=None,
                                                                  op0=ALU.add),
                    [("mv", b)], [("lnr", b)])
            nh = AP(self.neghalf, 8, 0, 128, 0, [(1, 1)])
            self.op("pool", lambda e, o=rs, i=rs, h=nh: e.tensor_tensor(out=o, in0=i, in1=h, op=ALU.pow),
                    [("lnr", b), ("c", "neghalf")], [("lnr", b)])
            dst = AP(self.vln, 2048, 0, 128, 512 * b, [(1, 512)])
            self.op("dve", lambda e, o=dst, i=vg, m=mean, r=rs: e.tensor_scalar(out=o, in0=i, scalar1=m, scalar2=r,
                                                                                 op0=ALU.subtract, op1=ALU.mult),
                    [("tmp", tv), ("mv", b), ("lnr", b)], [("vln", b)])
        self.chunk_done()

    def stage_attn(self, t):
        ts = t % self.tps
        units = [(b, g) for b in range(NB) for g in range(2)]
        pend = None
        idn = AP(self.ident, 128, 0, 128, 0, [(1, 128)])
        ones = AP(self.ones, 128, 0, 128, 0, [(1, 128)])

        def pv(u):
            b, g, parts, pti, ui = u
            bo = self.nbank()
            for hh in range(4):
                for idx, p in enumerate(parts):
                    lhsT = AP(self.PT, 2048, 0, 128, 1024 * pti + 512 * (hh // 2) + 256 * p + 128 * (hh % 2),
                              [(1, 128)])
                    rhs = AP(self.vtok, 650, 0, 128, 130 * (b + p) + 65 * g, [(1, 65)])
                    self.mm(self.P(bo, 65, 65 * hh), lhsT, rhs, idx == 0, idx == len(parts) - 1,
                            [("v", b + p), ("PT", pti)], [("ps", bo)])
            ro = 56 + 4 * (ui % 2)
            r = AP(self.st, 64, 0, 128, ro, [(1, 4)])
            den = AP(self.ps[bo], 512, 0, 128, 64, [(65, 4)])
            es_ = AP(self.esink, 8, 0, 128, 4 * g, [(1, 4)])
            self.op("dve", lambda e, o=r, a=den, c=es_: e.tensor_tensor(out=o, in0=a, in1=c, op=ALU.add),
                    [("ps", bo), ("c", "esink")], [("st", ro)])
            self.op("dve", lambda e, o=r: e.reciprocal(out=o, in_=o), [("st", ro)], [("st", ro)])
            for hh in range(4):
                dst = AP(self.atok, 2048, 0, 128, 512 * b + 64 * (4 * g + hh), [(1, 64)])
                rs = AP(self.st, 64, 0, 128, ro + hh, [(1, 1)])
                if hh % 2 == 0:
                    self.op("act", lambda e, o=dst, i=self.P(bo, 64, 65 * hh), s_=rs:
                            e.activation(out=o, in_=i, func=AF.Copy, scale=s_),
                            [("ps", bo), ("st", ro)], [("atok", b, g)])
                else:
                    self.op("dve", lambda e, o=dst, i=self.P(bo, 64, 65 * hh), s_=rs:
                            e.tensor_scalar(out=o, in0=i, scalar1=s_, scalar2=None, op0=ALU.mult),
                            [("ps", bo), ("st", ro)], [("atok", b, g)])
            return b if g == 1 else None

        def tr_block(b):
            bt = self.nbank()
            for c4 in range(4):
                o = AP(self.psb[bt], 1024, 0, 128, 128 * c4, [(1, 128)])
                i = AP(self.atok, 2048, 0, 128, 512 * b + 128 * c4, [(1, 128)])
                self.op("pe", lambda e, o=o, i=i: e.transpose(out=o, in_=i, identity=idn),
                        [("atok", b, 0), ("atok", b, 1), ("c", "ident")], [("ps", bt)])
            src = AP(self.psb[bt], 1024, 0, 128, 0, [(128, 4), (1, 128)])
            dst = AP(self.attnT, 2048, 0, 128, 128 * b, [(512, 4), (1, 128)])
            self.op("dve", lambda e, o=dst, s_=src: e.tensor_copy(out=o, in_=s_), [("ps", bt)],
                    [("attn", c4) for c4 in range(4)])

        trp = None
        for ui, (b, g) in enumerate(units):
            parts = [1] if (ts == 0 and b == 0) else [0, 1]
            pti = ui % 2
            banks = (self.nbank(), self.nbank())
            c0 = 256 * parts[0]
            n = 256 * len(parts)
            for half in range(2):
                tb = AP(self.tbl, 2048, 0, 128, 1024 * g + 512 * half + c0, [(1, n)])
                self.mm(self.P(banks[half], n, c0), idn, tb, True, False, [("c", "ident"), ("c", "tbl")],
                        [("ps", banks[half])])
            for pi, p in enumerate(parts):
                kslot = b + p
                for half in range(2):
                    lhsT = AP(self.kdup, 1280, 64 * half, 64, 640 * g + 128 * kslot, [(1, 128)])
                    rhs = AP(self.Q2, 2048, 64 * half, 64, 512 * (2 * g) + 128 * b, [(512, 2), (1, 128)])
                    self.mm(self.P(banks[half], 256, 256 * p), lhsT, rhs, False, pi == len(parts) - 1,
                            [("k", kslot), ("q", 2 * g), ("q", 2 * g + 1)], [("ps", banks[half])])
            for half in range(2):
                dst = AP(self.PT, 2048, 0, 128, 1024 * pti + 512 * half + c0, [(1, n)])
                self.op("act", lambda e, o=dst, s=self.P(banks[half], n, c0):
                        e.activation(out=o, in_=s, func=AF.Exp, scale=0.125),
                        [("ps", banks[half])], [("PT", pti)])
            done = pv(pend) if pend is not None else None
            if trp is not None:
                tr_block(trp)
            trp = done
            pend = (b, g, parts, pti, ui)
            yield
        done = pv(pend)
        if trp is not None:
            tr_block(trp)
        yield
        tr_block(done)
        yield

    def stage_spatial(self, t):
        for b in range(NB):
            bk = self.nbank()
            for gq in range(4):
                lhsT = AP(self.vln, 2048, 0, 128, 512 * b + 128 * gq, [(1, 128)])
                rhs = AP(self.wsT, 512, 0, 128, 128 * gq, [(1, 128)])
                self.mm(self.P(bk, 128, 128 * gq), lhsT, rhs, True, True, [("vln", b), ("c", "wsT")], [("ps", bk)])
            tq = self.ntmp()
            for gq in range(4):
                self.op("dve", lambda e, o=self.T(tq, 128, off=128 * gq), i=self.P(bk, 128, 128 * gq),
                        s=AP(self.lngpp, 4, 0, 128, gq, [(1, 1)]), c=AP(self.B2, 512, 0, 128, 128 * gq, [(1, 128)]):
                        e.scalar_tensor_tensor(out=o, in0=i, scalar=s, in1=c, op0=ALU.mult, op1=ALU.add),
                        [("ps", bk), ("c", "lngpp"), ("c", "B2")], [("tmp", tq)])
            t4 = AP(self.tmp[tq], 512, 0, 128, 0, [(128, 4), (1, 128)])
            u4 = AP(self.uT, 2048, 0, 128, 128 * b, [(512, 4), (1, 128)])
            dst = AP(self.gmT, 2048, 0, 128, 128 * b, [(512, 4), (1, 128)])
            self.op("dve", lambda e, o=dst, a=t4, c=u4: e.tensor_tensor(out=o, in0=a, in1=c, op=ALU.mult),
                    [("tmp", tq)] + [("u", i) for i in range(4)], [("gm", i) for i in range(4)])

    def stage_merge(self, t, inter=None, per_chunk=2):
        for c in range(8):
            slot, rk = self.next_chunk(C_M0 + c)
            bga, bgb, bra, brb = self.nbank(), self.nbank(), self.nbank(), self.nbank()
            self.featT_mm(slot, rk, 0, bga, self.xnT, "xnT", 256)
            self.featT_mm(slot, rk, 128, bgb, self.xnT, "xnT", 256)
            self.featT_mm(slot, rk, 0, bra, self.attnT, "attn", 128, nk=4, woff=2048)
            self.featT_mm(slot, rk, 0, brb, self.gmT, "gm", 128, nk=4, woff=2560)
            self.chunk_done()
            ta, tb_, t1, t2 = self.ntmp(), self.ntmp(), self.ntmp(), self.ntmp()
            self.op("act", lambda e, o=self.T(ta), s=self.P(bga): e.activation(out=o, in_=s, func=AF.Sigmoid),
                    [("ps", bga)], [("tmp", ta)])
            self.op("act", lambda e, o=self.T(tb_), s=self.P(bgb): e.activation(out=o, in_=s, func=AF.Sigmoid),
                    [("ps", bgb)], [("tmp", tb_)])
            self.op("dve", lambda e, o=self.T(t1), a=self.P(bra), c=self.T(ta):
                    e.tensor_tensor(out=o, in0=a, in1=c, op=ALU.mult), [("ps", bra), ("tmp", ta)], [("tmp", t1)])
            self.op("dve", lambda e, o=self.T(t2), a=self.P(brb), c=self.T(tb_):
                    e.tensor_tensor(out=o, in0=a, in1=c, op=ALU.mult), [("ps", brb), ("tmp", tb_)], [("tmp", t2)])
            dst = AP(self.mergedT, 4096, 0, 128, 512 * c, [(1, 512)])
            self.op("pool", lambda e, o=dst, a=self.T(t1), c_=self.T(t2):
                    e.tensor_tensor(out=o, in0=a, in1=c_, op=ALU.add), [("tmp", t1), ("tmp", t2)], [("mg", c)])
            if inter is not None and c >= 1:
                for _ in range(per_chunk):
                    next(inter, None)
        if inter is not None:
            for _ in inter:
                pass

    def tokmajor_out(self, xi, act_t, akey, chunk_groups, gbc, gname, stbase, inter=None, after_pair=None):
        xb = self.xb[xi]
        F = act_t.shape[1]
        ktot = sum(nk for _, nk in chunk_groups[0])
        resident = all(len(g) == 1 for g in chunk_groups)
        held = None
        for pair in range(2):
            blocks = (2 * pair, 2 * pair + 1)
            banks = {}
            for hf in range(2):
                for b in blocks:
                    banks[(hf, b)] = self.nbank()
            self.held |= set(banks.values())
            if resident:
                if pair == 0:
                    held = [self.next_chunk(chunk_groups[hf][0][0]) for hf in range(2)]
            for hf in range(2):
                k0 = 0
                for gi, (cid, nk) in enumerate(chunk_groups[hf]):
                    slot, rk = held[hf] if resident else self.next_chunk(cid)
                    for b in blocks:
                        for kk in range(nk):
                            kg = k0 + kk
                            lhsT = AP(act_t, F, 0, 128, 512 * kg + 128 * b, [(1, 128)])
                            rhs = AP(slot, CHE, 0, 128, 512 * kk, [(1, 512)])
                            self.mm(self.P(banks[(hf, b)]), lhsT, rhs, kg == 0, kg == ktot - 1,
                                    rk + [(akey, kg)], [("ps", banks[(hf, b)])])
                    k0 += nk
                    if not resident:
                        self.chunk_done()
                    if inter is not None:
                        next(inter, None)
            if resident and pair == 1:
                self.chunk_done()
            self.held -= set(banks.values())
            self.soft = set(banks.values())
            self.soft_n = 4
            tys = {}
            for b in blocks:
                for hf in range(2):
                    bk = banks[(hf, b)]
                    tj = self.ntmp()
                    acc = AP(self.st, 64, 0, 128, stbase + 2 * b + hf, [(1, 1)])
                    self.op("act", lambda e, o=self.T(tj), i=self.P(bk), a=acc:
                            e.activation(out=o, in_=i, func=AF.Square, accum_out=a),
                            [("ps", bk)], [("tmp", tj), ("st", stbase + 2 * b + hf)])
                    ty = self.ntmp()
                    tys[(hf, b)] = ty
                    g = AP(gbc, 1024, 0, 128, 512 * hf, [(1, 512)])
                    self.op("dve", lambda e, o=self.T(ty), i=self.P(bk), g=g:
                            e.tensor_tensor(out=o, in0=i, in1=g, op=ALU.mult),
                            [("ps", bk), ("c", gname)], [("tmp", ty)])
            b0 = blocks[0]
            sse = AP(self.st, 64, 0, 128, stbase + 2 * b0, [(2, 2)])
            sso = AP(self.st, 64, 0, 128, stbase + 2 * b0 + 1, [(2, 2)])
            ms = AP(self.st, 64, 0, 128, stbase + 8 + b0, [(1, 2)])
            mkey = ("st", stbase + 8 + pair)
            self.op("dve", lambda e, o=ms, a=sse, c=sso: e.tensor_tensor(out=o, in0=a, in1=c, op=ALU.add),
                    [("st", stbase + 2 * b + hf) for b in blocks for hf in range(2)], [mkey])
            self.op("dve", lambda e, o=ms: e.tensor_scalar(out=o, in0=o, scalar1=1.0 / D, scalar2=EPS,
                                                           op0=ALU.mult, op1=ALU.add), [mkey], [mkey])
            nh = AP(self.neghalf, 8, 0, 128, 0, [(1, 2)])
            self.op("pool", lambda e, o=ms, h=nh: e.tensor_tensor(out=o, in0=o, in1=h, op=ALU.pow),
                    [mkey, ("c", "neghalf")], [mkey])
            for b in blocks:
                for hf in range(2):
                    ty = tys[(hf, b)]
                    rs = AP(self.st, 64, 0, 128, stbase + 8 + b, [(1, 1)])
                    xh = AP(xb, 4096, 0, 128, D * b + 512 * hf, [(1, 512)])
                    self.op("dve", lambda e, o=xh, i=self.T(ty), s=rs, x_=xh:
                            e.scalar_tensor_tensor(out=o, in0=i, scalar=s, in1=x_, op0=ALU.mult, op1=ALU.add),
                            [("tmp", ty), mkey, ("xb", xi, b)], [("xb", xi, b)])
            if after_pair is not None:
                after_pair(pair)
        if inter is not None:
            for _ in inter:
                pass

    def stage_ffn_gu(self, t, mid=None, mid_at=2):
        for f in range(11):
            if mid is not None and f == mid_at:
                mid()
            slot, rk = self.next_chunk(C_G0 + f)
            for j in range(2):
                fc = 2 * f + j
                bg = self.nbank()
                bu = self.nbank()
                self.featT_mm(slot, rk, 128 * j, bg, self.hnT, "hnT", 512)
                self.featT_mm(slot, rk, 256 + 128 * j, bu, self.hnT, "hnT", 512)
                tg = self.ntmp()
                self.op("act", lambda e, o=self.T(tg), s=self.P(bg): e.activation(out=o, in_=s, func=AF.Silu),
                        [("ps", bg)], [("tmp", tg)])
                dst = AP(self.hT, NFC * 512, 0, 128, 512 * fc, [(1, 512)])
                self.op("dve", lambda e, o=dst, a=self.P(bu), c=self.T(tg):
                        e.tensor_tensor(out=o, in0=a, in1=c, op=ALU.mult),
                        [("ps", bu), ("tmp", tg)], [("h", fc)])
            self.chunk_done()

    def store_y(self, t):
        xi = t % NXB
        xb = self.xb[xi]
        dst = bass.AP(self.y_d, t * TT * D, [[D, 128], [128 * D, NB], [1, D]])
        src = AP(xb, 4096, 0, 128, 0, [(D, NB), (1, D)])
        self.dma(dst, src, [("xb", xi, b) for b in range(NB)], [("y", t)], ("Y", xi))

    def record(self):
        n = self.n_tiles
        self.stream_init()
        self.load_x(0)
        self.load_consts(0)
        for _ in range(NSLOT - 1):
            self.record_load()
        self.rms_scale_all(0, 0)
        self.transposes(self.xnT, "xnT", self.g1, "g1")
        self.load_consts(1)
        self.stage_front(0)
        for t in range(1, min(n, NXB)):
            self.load_x(t)
        wo_groups = [[(C_O0, 8)], [(C_O0 + 1, 8)]]
        dn_groups = [[(C_D0, 8), (C_D0 + 1, 8), (C_D0 + 2, 6)], [(C_D0 + 3, 8), (C_D0 + 4, 8), (C_D0 + 5, 6)]]
        for _ in self.stage_attn(0):
            pass
        self.stage_spatial(0)
        for t in range(n):
            xi = t % NXB
            nxt = t + 1 < n
            self.stage_merge(t, inter=self.rms_scale((t + 1) % NXB, 0) if nxt else None)
            if t >= 1 and t - 1 + NXB < n:
                self.load_x(t - 1 + NXB)
            if nxt:
                self.transposes(self.xnT, "xnT", self.g1, "g1")
            self.tokmajor_out(xi, self.mergedT, "mg", wo_groups, self.g2bc, "g2bc", 16,
                              after_pair=lambda pair, xi=xi: self.rms_scale_all(xi, 1, (2 * pair, 2 * pair + 1)))
            hT = lambda: self.transposes(self.hnT, "hnT", self.g3, "g3")
            if nxt:
                self.stage_front(t + 1, mid=hT)
                self.stage_ffn_gu(t, mid=lambda t=t: self.stage_spatial(t + 1))
            else:
                hT()
                self.stage_ffn_gu(t)
            inter = self.stage_attn(t + 1) if nxt else None
            self.tokmajor_out(xi, self.hT, "h", dn_groups, self.g4bc, "g4bc", 16, inter=inter)
            self.store_y(t)
        self.op("sp", None, [("y", t) for t in range(n)], [])
        self.S.finalize()

    def emit(self):
        nc = self.nc
        S = self.S
        for key in S.dma_count:
            self.sem_dma[key] = self.es.enter_context(nc.semaphore("d_" + "_".join(str(k) for k in key)))

        def semof(key):
            return self.sem_eng[key[1]] if key[0] == "eng" else self.sem_dma[key[1]]

        def run(eng_name):
            def body(e):
                for op in S.ops[eng_name]:
                    for key, val in op.waits:
                        e.wait_ge(semof(key), val)
                    if op.fn is None:
                        continue
                    ins = op.fn(e)
                    if op.dma_key is not None:
                        ins.then_inc(self.sem_dma[op.dma_key], 16)
                    elif op.marked:
                        ins.then_inc(self.sem_eng[eng_name], 1)
            return body

        with nc.Block() as block:
            block.sync(run("sp"))
            block.tensor(run("pe"))
            block.scalar(run("act"))
            block.vector(run("dve"))
            block.gpsimd(run("pool"))


def build(n_tiles, tiles_per_seq):
    b = Builder(n_tiles, tiles_per_seq)
    es = ExitStack()
    with es:
        b.alloc(es)
        b.record()
        b.emit()
    return b.nc


def prepare_weights(inp):
    wsrc = build_wsrc(inp["w_in"][0], inp["w_attn_branch"][0], inp["w_gmlp_branch"][0], inp["w_out"][0],
                      inp["w_ffn_gate"][0], inp["w_ffn_up"][0], inp["w_ffn_down"][0])
    consts = build_consts(inp["norm_mix_pre"][0], inp["norm_mix_post"][0], inp["norm_ffn_pre"][0],
                          inp["norm_ffn_post"][0], inp["attn_sinks"][0], inp["gmlp_ln_g"][0],
                          inp["gmlp_ln_b"][0], inp["gmlp_w_s"][0], inp["gmlp_b_s"][0])
    m = {"wsrc": wsrc.reshape(NCH * 128, CHE)}
    for k, v in consts.items():
        m["c_" + k] = v
    return m


def kernel(**inputs):
    inp = {k: np.asarray(v, dtype=np.float32) for k, v in inputs.items()}
    x = inp["x"]
    B, S, _ = x.shape
    tps = S // TT
    seq_per_core = B // N_CORES
    n_tiles = seq_per_core * tps
    shared = prepare_weights(inp)
    xs = np.ascontiguousarray(x).reshape(N_CORES, seq_per_core * S, D)
    nc = build(n_tiles, tps)
    in_maps = []
    for c in range(N_CORES):
        m = dict(shared)
        m["x"] = xs[c]
        in_maps.append(m)
    res = run_bass_kernel_spmd(nc, in_maps, core_ids=list(range(N_CORES)))
    out = np.stack([np.asarray(r["y"], dtype=np.float32) for r in res.results], axis=0)
    return out.reshape(B, S, D)
```
